# Optimizing a Trainium2 kernel written in Bass

```python
import math
import jax, jax.numpy as jnp
from jax import lax
import numpy as np

D_MODEL = 1024
BATCH = 4
SEQ = 8192
DEPTH = 2

N_MIXERS = 2
D_FF = 2816
RMS_EPS = 1e-6
LN_EPS = 1e-5
DN_HEAD_DIM = 128
DN_HEADS = D_MODEL // DN_HEAD_DIM
DN_WIDTH = DN_HEADS * DN_HEAD_DIM
DN_CONV = 4
DN_CHUNK = 64
SG_WIDTH = 2 * D_MODEL
SG_GROUPS = 8
SG_CHUNK = 128
N_A = (DEPTH + 1) // 2
N_B = DEPTH // 2

kernel_name = 'hybrid_deltanet_spatialgate_macaron'


def rmsnorm(x, g, eps=RMS_EPS):
    xf = x.astype(jnp.float32)
    y = xf * lax.rsqrt(jnp.mean(xf * xf, axis=-1, keepdims=True) + eps)
    return (y * g.astype(jnp.float32)).astype(x.dtype)


def layernorm(x, g, b, eps=LN_EPS):
    xf = x.astype(jnp.float32)
    mu = jnp.mean(xf, axis=-1, keepdims=True)
    xc = xf - mu
    y = xc * lax.rsqrt(jnp.mean(xc * xc, axis=-1, keepdims=True) + eps)
    return (y * g.astype(jnp.float32) + b.astype(jnp.float32)).astype(x.dtype)


def l2norm(x, eps=1e-6):
    return x * lax.rsqrt(jnp.sum(x * x, axis=-1, keepdims=True) + eps)


def swiglu(h, w_gate, w_up, w_down):
    return (jax.nn.silu(h @ w_gate) * (h @ w_up)) @ w_down


def causal_short_conv(x, w):
    K = w.shape[0]
    S = x.shape[1]
    xp = jnp.pad(x, ((0, 0), (K - 1, 0), (0, 0)))
    return sum(xp[:, j:j + S, :] * w[j] for j in range(K))


def gated_delta_rule(q, k, v, g, beta):
    B, H, S, Dk = q.shape
    Dv = v.shape[-1]
    C = DN_CHUNK
    N = S // C
    q = q * (Dk ** -0.5)
    q = q.reshape(B, H, N, C, Dk)
    k = k.reshape(B, H, N, C, Dk)
    v = v.reshape(B, H, N, C, Dv)
    g = g.reshape(B, H, N, C)
    beta = beta.reshape(B, H, N, C)
    gc = jnp.cumsum(g, axis=-1)
    causal = jnp.tril(jnp.ones((C, C), dtype=bool))
    strict = jnp.tril(jnp.ones((C, C), dtype=bool), -1)
    diff = gc[..., :, None] - gc[..., None, :]
    decay = jnp.where(causal, jnp.exp(jnp.where(causal, diff, 0.0)), 0.0)
    k_beta = k * beta[..., None]
    v_beta = v * beta[..., None]
    L = jnp.where(strict, jnp.einsum('bhnid,bhnjd->bhnij', k_beta, k) * decay, 0.0)
    A = L + jnp.eye(C, dtype=jnp.float32)
    rhs = jnp.concatenate([v_beta, k_beta * jnp.exp(gc)[..., None]], axis=-1)
    sol = lax.linalg.triangular_solve(A, rhs, left_side=True, lower=True, unit_diagonal=True)
    u = sol[..., :Dv]
    w = sol[..., Dv:]
    attn = jnp.where(causal, jnp.einsum('bhnid,bhnjd->bhnij', q, k) * decay, 0.0)
    q_dec = q * jnp.exp(gc)[..., None]
    k_dec = k * jnp.exp(gc[..., -1:] - gc)[..., None]
    g_last = jnp.exp(gc[..., -1])

    def step(state, xs):
        u_i, w_i, attn_i, qd_i, kd_i, gl_i = xs
        v_new = u_i - jnp.einsum('bhck,bhkv->bhcv', w_i, state)
        o = jnp.einsum('bhck,bhkv->bhcv', qd_i, state) + jnp.einsum('bhij,bhjv->bhiv', attn_i, v_new)
        state = state * gl_i[..., None, None] + jnp.einsum('bhck,bhcv->bhkv', kd_i, v_new)
        return state, o

    xs = tuple(jnp.moveaxis(t, 2, 0) for t in (u, w, attn, q_dec, k_dec, g_last))
    s0 = jnp.zeros((B, H, Dk, Dv), jnp.float32)
    _, o = lax.scan(step, s0, xs)
    return jnp.moveaxis(o, 0, 2).reshape(B, H, S, Dv)


def gated_deltanet(h, w_in, conv_w, a_log, dt_bias, norm_g, w_out):
    B, S, _ = h.shape
    H, Dh, W = DN_HEADS, DN_HEAD_DIM, DN_WIDTH
    f32 = jnp.float32
    proj = h @ w_in
    qkv = proj[..., :3 * W]
    z = proj[..., 3 * W:4 * W]
    b_raw = proj[..., 4 * W:4 * W + H]
    a_raw = proj[..., 4 * W + H:]
    qkv = jax.nn.silu(causal_short_conv(qkv, conv_w))
    q, k, v = jnp.split(qkv, 3, axis=-1)
    to_heads = lambda t: t.reshape(B, S, H, Dh).transpose(0, 2, 1, 3).astype(f32)
    q = l2norm(to_heads(q))
    k = l2norm(to_heads(k))
    v = to_heads(v)
    beta = jax.nn.sigmoid(b_raw.astype(f32)).transpose(0, 2, 1)
    g = (-jnp.exp(a_log.astype(f32)) *
         jax.nn.softplus(a_raw.astype(f32) + dt_bias.astype(f32))).transpose(0, 2, 1)
    o = gated_delta_rule(q, k, v, g, beta).transpose(0, 2, 1, 3)
    o = rmsnorm(o, norm_g) * jax.nn.silu(z.reshape(B, S, H, Dh).astype(f32))
    return o.reshape(B, S, W).astype(h.dtype) @ w_out


def spatial_gating(h, w_in, b_in, ln_g, ln_b, w_s, b_s, w_out):
    B, S, _ = h.shape
    E, G, C = SG_WIDTH, SG_GROUPS, SG_CHUNK
    N = S // C
    zz = jax.nn.gelu(h @ w_in + b_in, approximate=False)
    u = zz[..., :E]
    v = layernorm(zz[..., E:], ln_g, ln_b)
    mask = jnp.tril(jnp.ones((C, C), dtype=bool))
    w_c = jnp.where(mask, w_s, 0.0)
    vg = v.reshape(B, N, C, G, E // G)
    mixed = jnp.einsum('gts,bnsgc->bntgc', w_c, vg) + b_s.T[None, None, :, :, None]
    return (u * mixed.reshape(B, S, E)) @ w_out


def setup_inputs(seed: int = 0) -> dict:
    key = jax.random.key(seed)
    ks = jax.random.split(key, 20)
    D, F, H, Dh, W = D_MODEL, D_FF, DN_HEADS, DN_HEAD_DIM, DN_WIDTH
    E, G, C = SG_WIDTH, SG_GROUPS, SG_CHUNK
    nrm = jax.random.normal
    x = nrm(ks[0], (BATCH, SEQ, D), jnp.float32)
    norm_g = 1.0 + 0.02 * nrm(ks[1], (DEPTH, 6, D), jnp.float32)
    ffn_w_gate = nrm(ks[2], (DEPTH, 2, D, F), jnp.float32) * D ** -0.5
    ffn_w_up = nrm(ks[3], (DEPTH, 2, D, F), jnp.float32) * D ** -0.5
    ffn_w_down = nrm(ks[4], (DEPTH, 2, F, D), jnp.float32) * F ** -0.5
    dn_w_in = nrm(ks[5], (N_A, D, 4 * W + 2 * H), jnp.float32) * D ** -0.5
    dn_conv_w = nrm(ks[6], (N_A, DN_CONV, 3 * W), jnp.float32) * DN_CONV ** -0.5
    dn_a_log = jnp.log(jax.random.uniform(ks[7], (N_A, H), jnp.float32, minval=1.0, maxval=16.0))
    dt = jnp.exp(jax.random.uniform(ks[8], (N_A, H), jnp.float32,
                                    minval=math.log(1e-3), maxval=math.log(1e-1)))
    dn_dt_bias = dt + jnp.log(-jnp.expm1(-dt))
    dn_norm_g = 1.0 + 0.02 * nrm(ks[9], (N_A, Dh), jnp.float32)
    dn_w_out = nrm(ks[10], (N_A, W, D), jnp.float32) * W ** -0.5
    sg_w_in = nrm(ks[11], (N_B, D, 2 * E), jnp.float32) * D ** -0.5
    sg_b_in = 0.02 * nrm(ks[12], (N_B, 2 * E), jnp.float32)
    sg_ln_g = 1.0 + 0.02 * nrm(ks[13], (N_B, E), jnp.float32)
    sg_ln_b = 0.02 * nrm(ks[14], (N_B, E), jnp.float32)
    sg_w_s = nrm(ks[15], (N_B, G, C, C), jnp.float32) * C ** -0.5
    sg_b_s = 1.0 + 0.02 * nrm(ks[16], (N_B, G, C), jnp.float32)
    sg_w_out = nrm(ks[17], (N_B, E, D), jnp.float32) * E ** -0.5
    return {'x': x, 'norm_g': norm_g, 'ffn_w_gate': ffn_w_gate, 'ffn_w_up': ffn_w_up,
            'ffn_w_down': ffn_w_down, 'dn_w_in': dn_w_in, 'dn_conv_w': dn_conv_w,
            'dn_a_log': dn_a_log, 'dn_dt_bias': dn_dt_bias, 'dn_norm_g': dn_norm_g,
            'dn_w_out': dn_w_out, 'sg_w_in': sg_w_in, 'sg_b_in': sg_b_in, 'sg_ln_g': sg_ln_g,
            'sg_ln_b': sg_ln_b, 'sg_w_s': sg_w_s, 'sg_b_s': sg_b_s, 'sg_w_out': sg_w_out}


def reference(x, norm_g, ffn_w_gate, ffn_w_up, ffn_w_down, dn_w_in, dn_conv_w, dn_a_log,
              dn_dt_bias, dn_norm_g, dn_w_out, sg_w_in, sg_b_in, sg_ln_g, sg_ln_b, sg_w_s,
              sg_b_s, sg_w_out):
    for i in range(DEPTH):
        ng = norm_g[i]
        h = rmsnorm(x, ng[0])
        x = x + 0.5 * rmsnorm(swiglu(h, ffn_w_gate[i, 0], ffn_w_up[i, 0], ffn_w_down[i, 0]), ng[1])
        h = rmsnorm(x, ng[2])
        j = i // N_MIXERS
        if i % N_MIXERS == 0:
            m = gated_deltanet(h, dn_w_in[j], dn_conv_w[j], dn_a_log[j], dn_dt_bias[j],
                               dn_norm_g[j], dn_w_out[j])
        else:
            m = spatial_gating(h, sg_w_in[j], sg_b_in[j], sg_ln_g[j], sg_ln_b[j], sg_w_s[j],
                               sg_b_s[j], sg_w_out[j])
        x = x + rmsnorm(m, ng[3])
        h = rmsnorm(x, ng[4])
        x = x + 0.5 * rmsnorm(swiglu(h, ffn_w_gate[i, 1], ffn_w_up[i, 1], ffn_w_down[i, 1]), ng[5])
    return x
```

```python
import contextlib
import os
DN_STOP = int(os.environ.get('DN_STOP', '99'))
import numpy as np
import concourse.bass as bass
import concourse.mybir as mybir
from concourse.bass_utils import run_bass_kernel_spmd

F32 = mybir.dt.float32
BF16 = mybir.dt.bfloat16
AF = mybir.ActivationFunctionType
ALU = mybir.AluOpType
AX = mybir.AxisListType

D = 1024
FF = 2816
NFC = FF // 128
T = 512
RMS_EPS = 1e-6
LN_EPS = 1e-5
ENGS = ("pe", "act", "dve", "pool", "sp")


class Op:
    __slots__ = ("eng", "fn", "reads", "writes", "dma", "deps", "sig", "sem", "val", "idx", "bar")

    def __init__(self, eng, fn, reads, writes, dma):
        self.eng = eng
        self.fn = fn
        self.reads = reads
        self.writes = writes
        self.dma = dma
        self.deps = []
        self.sig = False
        self.sem = None
        self.val = 0
        self.bar = 0


class Sched:
    def __init__(self, nc):
        self.nc = nc
        self.ops = []
        self.phase = 0
        self.nbar = 0

    def barrier(self):
        self.nbar += 1
        self.phase += 1

    def op(self, eng, fn, reads=(), writes=(), dma=None):
        if dma == "cst":
            writes = tuple(writes) + ("cstall",)
        o = Op(eng, fn, tuple(reads), tuple(writes), dma)
        o.idx = len(self.ops)
        o.bar = self.nbar
        o.sem = ("dma", dma) if dma is not None else ("eng", eng)
        self.ops.append(o)
        return o

    def emit(self):
        nc = self.nc
        ops = self.ops
        last_w = {}
        readers = {}
        last_eng = {}
        last_dma = {}
        seen_bar = {e: 0 for e in ENGS}
        bar_snap = None
        cur_bar = 0
        for o in ops:
            if o.bar != cur_bar:
                cur_bar = o.bar
                bar_snap = (dict(last_eng), dict(last_dma))
            deps = set()
            for r in o.reads:
                w = last_w.get(r)
                if w is not None:
                    deps.add(w)
            for wk in o.writes:
                w = last_w.get(wk)
                if w is not None:
                    deps.add(w)
                for rd in readers.get(wk, ()):
                    deps.add(rd)
            for r in o.reads:
                readers.setdefault(r, []).append(o.idx)
            for wk in o.writes:
                last_w[wk] = o.idx
                readers[wk] = []
            deps.discard(o.idx)
            real = []
            for d in deps:
                p = ops[d]
                if p.dma is None and o.dma is None and p.eng == o.eng:
                    if o.eng == "pe":
                        continue
                    if not any(r in p.writes for r in o.reads):
                        continue
                real.append(d)
            if seen_bar[o.eng] != o.bar:
                seen_bar[o.eng] = o.bar
                for e, i in bar_snap[0].items():
                    if e != o.eng:
                        real.append(i)
                for k, i in bar_snap[1].items():
                    real.append(i)
            for d in real:
                ops[d].sig = True
            o.deps = real
            if o.dma is None:
                last_eng[o.eng] = o.idx
            else:
                last_dma[o.dma] = o.idx
        counts = {}
        for o in ops:
            if o.dma is not None:
                counts[o.sem] = counts.get(o.sem, 0) + 16
                o.val = counts[o.sem]
            elif o.sig:
                counts[o.sem] = counts.get(o.sem, 0) + 1
                o.val = counts[o.sem]
        semkeys = sorted(counts.keys(), key=str)
        self.n_sems = len(semkeys)
        self.max_val = max(counts.values()) if counts else 0
        per_eng = {e: [] for e in ENGS}
        for o in ops:
            per_eng[o.eng].append(o)
        with contextlib.ExitStack() as st:
            sems = {k: st.enter_context(nc.semaphore("s%d" % i)) for i, k in enumerate(semkeys)}
            block = st.enter_context(nc.Block())

            def run(engname, eng):
                known = {}
                for o in per_eng[engname]:
                    need = {}
                    for d in o.deps:
                        p = ops[d]
                        if known.get(p.sem, 0) >= p.val:
                            continue
                        if need.get(p.sem, 0) < p.val:
                            need[p.sem] = p.val
                    for s, v in need.items():
                        eng.wait_ge(sems[s], v)
                        known[s] = v
                    ins = o.fn(eng)
                    if o.dma is not None:
                        ins.then_inc(sems[o.sem], 16)
                    elif o.sig:
                        ins.then_inc(sems[o.sem], 1)
                fin = {}
                for o in per_eng[engname]:
                    if o.dma is not None:
                        fin[o.sem] = max(fin.get(o.sem, 0), o.val)
                for s, v in fin.items():
                    if known.get(s, 0) < v:
                        eng.wait_ge(sems[s], v)

            if per_eng["sp"]:
                @block.sync
                def _(e):
                    run("sp", e)
            if per_eng["pe"]:
                @block.tensor
                def _(e):
                    run("pe", e)
            if per_eng["act"]:
                @block.scalar
                def _(e):
                    run("act", e)
            if per_eng["dve"]:
                @block.vector
                def _(e):
                    run("dve", e)
            if per_eng["pool"]:
                @block.gpsimd
                def _(e):
                    run("pool", e)


class Builder:
    def __init__(self, npre, nown, layers):
        self.npre = npre
        self.nown = nown
        self.ntot = npre + nown
        self.layers = layers
        self.nc = bass.Bass("TRN2", target_bir_lowering=False)
        self.S = Sched(self.nc)
        self.st = contextlib.ExitStack()
        self.uid = 0

    def sb(self, name, shape, dt):
        return self.st.enter_context(self.nc.sbuf_tensor(name, shape, dt))

    def dram_in(self, name, shape):
        return self.nc.dram_tensor(name, list(shape), F32, kind="ExternalInput").ap()

    def build(self):
        nc, S = self.nc, self.S
        with self.st:
            self.declare_io()
            self.alloc()
            self.consts()
            li = 0
            for (kind, arg) in self.layers:
                S.barrier()
                if kind == "ffn":
                    self.ffn_phase(*arg)
                elif kind == "sg":
                    self.sg_phase(*arg)
                elif kind == "dn":
                    self.dn_phase(*arg)
                li += 1
            S.emit()
        return nc

    def declare_io(self):
        nc = self.nc
        self.x_in = self.dram_in("x", (self.ntot, D))
        self.norm_g = self.dram_in("norm_g", (2, 6, D))
        self.w_gate = self.dram_in("ffn_w_gate", (2, 2, D, FF))
        self.w_up = self.dram_in("ffn_w_up", (2, 2, D, FF))
        self.w_down = self.dram_in("ffn_w_down", (2, 2, FF, D))
        self.dn_w_in = self.dram_in("dn_w_in", (1, D, 4112))
        self.dn_conv_w = self.dram_in("dn_conv_w", (1, 4, 3072))
        self.dn_a_log = self.dram_in("dn_a_log", (1, 8))
        self.dn_dt_bias = self.dram_in("dn_dt_bias", (1, 8))
        self.dn_norm_g = self.dram_in("dn_norm_g", (1, 128))
        self.dn_w_out = self.dram_in("dn_w_out", (1, D, D))
        self.sg_w_in = self.dram_in("sg_w_in", (1, D, 4096))
        self.sg_b_in = self.dram_in("sg_b_in", (1, 4096))
        self.sg_ln_g = self.dram_in("sg_ln_g", (1, 2048))
        self.sg_ln_b = self.dram_in("sg_ln_b", (1, 2048))
        self.sg_w_s = self.dram_in("sg_w_s", (1, 8, 128, 128))
        self.sg_b_s = self.dram_in("sg_b_s", (1, 8, 128))
        self.sg_w_out = self.dram_in("sg_w_out", (1, 2048, D))
        self.y_out = nc.dram_tensor("y", [self.nown, D], F32, kind="ExternalOutput").ap()
        self.xs = nc.dram_tensor("xs_scratch", [self.ntot, D], F32, kind="Internal").ap()

    def alloc(self):
        nc = self.nc
        self.ARENA_N = 79872
        self.ARENA = self.sb("ARENA", [128, self.ARENA_N], BF16)
        self.xring = [self.sb("xr%d" % i, [128, D], F32) for i in range(4)]
        self.xres = [self.sb("xq%d" % i, [128, D], F32) for i in range(1)]
        self.xnew = [self.sb("xw%d" % i, [128, D], F32) for i in range(1)]
        self.xn = [self.sb("xn%d" % i, [128, D], BF16) for i in range(2)]
        self.junk = self.sb("junk", [128, D], BF16)
        self.hT = self.sb("hT", [128, 8, T], BF16)
        self.gpre = self.sb("gpre", [128, D], F32)
        self.gpost = self.sb("gpost", [128, D], F32)
        self.sgt = [self.sb("sgt%d" % i, [128, T], F32) for i in range(2)]
        self.ident = self.sb("ident", [128, 128], BF16)
        self.identf = self.sb("identf", [128, 128], F32)
        self.stat = self.sb("stat", [128, 64], F32)
        self.onesb = self.sb("onesb", [128, 128], BF16)
        self.onesf = self.sb("onesf", [128, 128], F32)
        self.PS = self.st.enter_context(nc.psum_tensor("PS", [128, 8, 512], F32))
        self.cnt = {"xr": 0, "xq": 0, "xw": 0, "xn": 0, "st": 0}

    def consts(self):
        S = self.S
        ident, identf = self.ident, self.identf
        S.op("pool", lambda e: e.memset(identf[:], 0.0), writes=["identf"])
        S.op("pool", lambda e: e.affine_select(out=identf[:], in_=identf[:], pattern=[[-1, 128]],
                                               compare_op=ALU.not_equal, fill=1.0, base=0,
                                               channel_multiplier=1),
             reads=["identf"], writes=["identf"])
        S.op("pool", lambda e: e.tensor_copy(out=ident[:], in_=identf[:]), reads=["identf"], writes=["ident"])
        S.op("pool", lambda e: e.memset(self.onesb[:], 1.0), writes=["onesb"])
        S.op("pool", lambda e: e.memset(self.onesf[:], 1.0), writes=["onesf"])

    def carve(self, off, n, dt=BF16):
        if dt == BF16:
            assert off + n <= self.ARENA_N
            return self.ARENA[:, off:off + n], off + n
        assert off % 2 == 0 and off + 2 * n <= self.ARENA_N
        return self.ARENA[:, off:off + 2 * n].bitcast(F32), off + 2 * n

    def bank(self, b):
        return self.PS[:, b, :]

    def load_gvec(self, dst, src_row, key):
        self.S.op("sp", lambda e: e.dma_start(out=dst[:], in_=src_row.partition_broadcast(128)),
                  writes=[key], dma=key)

    def prenorm_tile(self, src, row0, rstd_mode, nsub=4):
        S = self.S
        stat, junk, hT, gpre = self.stat, self.junk, self.hT, self.gpre
        xs_ = []
        for s in range(nsub):
            i = self.cnt["xr"] % 4
            self.cnt["xr"] += 1
            xt = self.xring[i]
            r0 = row0 + s * 128
            S.op("sp", lambda e, xt=xt, r0=r0: e.dma_start(out=xt[:], in_=src[r0:r0 + 128, :]),
                 writes=[("xr", i)], dma=("xr", i))
            S.op("act", lambda e, xt=xt, s=s: e.activation(out=junk[:], in_=xt[:], func=AF.Square,
                                                           accum_out=stat[:, s:s + 1]),
                 reads=[("xr", i)], writes=["junk", ("stat", s)])
            xs_.append((xt, i))
        self.rstd(stat[:, 0:nsub], stat[:, 4:4 + nsub], stat[:, 8:8 + nsub], 1.0 / D, RMS_EPS, rstd_mode,
                  [("stat", s) for s in range(nsub)], "rs_pre")
        for s in range(nsub):
            xt, i = xs_[s]
            j = self.cnt["xn"] % 2
            self.cnt["xn"] += 1
            xn = self.xn[j]
            S.op("dve", lambda e, xt=xt, xn=xn, s=s: e.scalar_tensor_tensor(
                out=xn[:], in0=xt[:], scalar=stat[:, 8 + s:9 + s], in1=gpre[:], op0=ALU.mult, op1=ALU.mult),
                reads=[("xr", i), "rs_pre", "gpre"], writes=[("xn", j)])
            pb = s % 4
            pT = self.PS[:, pb, :].bitcast(BF16)

            def tr(e, xn=xn, pT=pT):
                ins = None
                for c in range(8):
                    ins = e.transpose(out=pT[:, c * 128:(c + 1) * 128], in_=xn[:, c * 128:(c + 1) * 128],
                                      identity=self.ident[:])
                return ins
            S.op("pe", tr, reads=[("xn", j), "ident"], writes=[("ps", pb)])
            S.op("act", lambda e, pT=pT, s=s: e.activation(
                out=hT[:, :, s * 128:(s + 1) * 128], in_=pT.rearrange("p (c t) -> p c t", c=8), func=AF.Copy),
                writes=[("ps", pb), ("hT", s)])

    def rstd(self, ss, tmp, out, scale, eps, mode, rkeys, wkey):
        S = self.S
        if mode == "sqrt":
            S.op("dve", lambda e: e.tensor_scalar(out=tmp, in0=ss, scalar1=scale, scalar2=eps,
                                                  op0=ALU.mult, op1=ALU.add),
                 reads=rkeys, writes=[(wkey, "t")])
            S.op("act", lambda e: e.activation(out=tmp, in_=tmp, func=AF.Sqrt),
                 reads=[(wkey, "t")], writes=[(wkey, "t")])
            S.op("dve", lambda e: e.reciprocal(out=out, in_=tmp), reads=[(wkey, "t")], writes=[wkey])
        else:
            S.op("dve", lambda e: e.tensor_scalar(out=tmp, in0=ss, scalar1=scale, scalar2=eps,
                                                  op0=ALU.mult, op1=ALU.add),
                 reads=rkeys, writes=[(wkey, "t")])
            S.op("act", lambda e: e.activation(out=tmp, in_=tmp, func=AF.Ln),
                 reads=[(wkey, "t")], writes=[(wkey, "t")])
            S.op("act", lambda e: e.activation(out=out, in_=tmp, func=AF.Exp, scale=-0.5),
                 reads=[(wkey, "t")], writes=[wkey])

    def post_residual(self, src, dst, row0, ybanks, coef, rstd_mode, dst_off=0):
        S = self.S
        stat, junk, gpost = self.stat, self.junk, self.gpost
        b0, b1 = ybanks
        q = self.cnt["xq"] % 1
        self.cnt["xq"] += 1
        xq = self.xres[q]
        S.op("sp", lambda e: e.dma_start(out=xq[:], in_=src[row0:row0 + 128, :]),
             writes=[("xq", q)], dma=("xq", q))
        k = self.cnt["st"] % 4
        self.cnt["st"] += 1
        c0 = 16 + k * 8
        S.op("act", lambda e: e.activation(out=junk[:, 0:512], in_=self.PS[:, b0, :], func=AF.Square,
                                           accum_out=stat[:, c0:c0 + 1]),
             writes=[("ps", b0), "junk", ("pst", k, 0)])
        S.op("act", lambda e: e.activation(out=junk[:, 512:1024], in_=self.PS[:, b1, :], func=AF.Square,
                                           accum_out=stat[:, c0 + 1:c0 + 2]),
             writes=[("ps", b1), "junk", ("pst", k, 1)])
        S.op("dve", lambda e: e.tensor_tensor(out=stat[:, c0 + 2:c0 + 3], in0=stat[:, c0:c0 + 1],
                                              in1=stat[:, c0 + 1:c0 + 2], op=ALU.add),
             reads=[("pst", k, 0), ("pst", k, 1)], writes=[("pst", k, 2)])
        self.rstd(stat[:, c0 + 2:c0 + 3], stat[:, c0 + 3:c0 + 4], stat[:, c0 + 4:c0 + 5], 1.0 / D, RMS_EPS,
                  rstd_mode, [("pst", k, 2)], ("pst", k, 4))
        S.op("dve", lambda e: e.tensor_scalar(out=stat[:, c0 + 5:c0 + 6], in0=stat[:, c0 + 4:c0 + 5],
                                              scalar1=float(coef), scalar2=None, op0=ALU.mult),
             reads=[("pst", k, 4)], writes=[("pst", k, 5)])
        w = self.cnt["xw"] % 1
        self.cnt["xw"] += 1
        xw = self.xnew[w]
        for hh, b in ((0, b0), (1, b1)):
            S.op("dve", lambda e, hh=hh, b=b: e.scalar_tensor_tensor(
                out=xw[:, hh * 512:(hh + 1) * 512], in0=self.PS[:, b, :], scalar=stat[:, c0 + 5:c0 + 6],
                in1=gpost[:, hh * 512:(hh + 1) * 512], op0=ALU.mult, op1=ALU.mult),
                reads=[("pst", k, 5), "gpost"], writes=[("ps", b), ("xw", w, hh)])
        S.op("pool", lambda e: e.tensor_tensor(out=xw[:], in0=xw[:], in1=xq[:], op=ALU.add),
             reads=[("xw", w, 0), ("xw", w, 1), ("xq", q)], writes=[("xw", w, 0), ("xw", w, 1)])
        S.op("sp", lambda e: e.dma_start(out=dst[row0 + dst_off:row0 + dst_off + 128, :], in_=xw[:]),
             reads=[("xw", w, 0), ("xw", w, 1)], dma=("xwst", w))

    def ffn_phase(self, layer, which, src, dst, row_lo, row_hi, dst_off=0):
        S = self.S
        srcap, dstap = self.dram(src), self.dram(dst)
        ph = S.phase
        a, off = self.carve(0, 8 * FF)
        Wg = a.rearrange("p (c f) -> p c f", c=8)
        a, off = self.carve(off, 8 * FF)
        Wu = a.rearrange("p (c f) -> p c f", c=8)
        a, off = self.carve(off, NFC * D)
        Wd = a.rearrange("p (c d) -> p c d", c=NFC)
        a, off = self.carve(off, NFC * T)
        actT = a.rearrange("p (c t) -> p c t", c=NFC)
        sg = self.sgt
        wg_d = self.w_gate[layer, which].rearrange("(c p) f -> p c f", p=128)
        wu_d = self.w_up[layer, which].rearrange("(c p) f -> p c f", p=128)
        wd_d = self.w_down[layer, which].rearrange("(c p) d -> p c d", p=128)
        ipre, ipost = (0, 1) if which == 0 else (4, 5)
        self.load_gvec(self.gpre, self.norm_g[layer, ipre:ipre + 1, :], "gpre")
        self.load_gvec(self.gpost, self.norm_g[layer, ipost:ipost + 1, :], "gpost")
        for c in range(8):
            S.op("pool", lambda e, c=c: e.dma_start(out=Wg[:, c, :], in_=wg_d[:, c, :]),
                 writes=[("Wg", c)], dma="wgu")
            S.op("pool", lambda e, c=c: e.dma_start(out=Wu[:, c, :], in_=wu_d[:, c, :]),
                 writes=[("Wu", c)], dma="wgu")
        for c in range(0, NFC, 2):
            S.op("pool", lambda e, c=c: e.dma_start(out=Wd[:, c:c + 2, :], in_=wd_d[:, c:c + 2, :]),
                 writes=[("Wd", c), ("Wd", c + 1)], dma="wd")
        hT = self.hT
        for row0 in range(row_lo, row_hi, T):
            self.prenorm_tile(srcap, row0, "sqrt")
            for fc in range(NFC):
                par = fc % 2
                gb, ub = 4 + 2 * par, 5 + 2 * par

                def mm_gu(e, fc=fc, gb=gb, ub=ub):
                    ins = None
                    for c in range(8):
                        ins = e.matmul(self.PS[:, gb, :], lhsT=Wg[:, c, fc * 128:(fc + 1) * 128], rhs=hT[:, c, :],
                                       start=(c == 0), stop=(c == 7))
                    for c in range(8):
                        ins = e.matmul(self.PS[:, ub, :], lhsT=Wu[:, c, fc * 128:(fc + 1) * 128], rhs=hT[:, c, :],
                                       start=(c == 0), stop=(c == 7))
                    return ins
                S.op("pe", mm_gu, reads=[("hT", s) for s in range(4)] + [("Wg", c) for c in range(8)] +
                     [("Wu", c) for c in range(8)], writes=[("ps", gb), ("ps", ub)])
                sgt = sg[par]
                S.op("act", lambda e, gb=gb, sgt=sgt: e.activation(out=sgt[:], in_=self.PS[:, gb, :], func=AF.Silu),
                     writes=[("ps", gb), ("sg", par)])
                S.op("dve", lambda e, ub=ub, sgt=sgt, fc=fc: e.tensor_tensor(
                    out=actT[:, fc, :], in0=self.PS[:, ub, :], in1=sgt[:], op=ALU.mult),
                    reads=[("sg", par)], writes=[("ps", ub), ("actT", fc)])
            for s in range(4):
                yb = (0, 1) if s % 2 == 0 else (2, 3)
                for hh in range(2):
                    def mm_d(e, s=s, hh=hh, b=yb[hh]):
                        ins = None
                        for fc in range(NFC):
                            ins = e.matmul(self.PS[:, b, :], lhsT=actT[:, fc, s * 128:(s + 1) * 128],
                                           rhs=Wd[:, fc, hh * 512:(hh + 1) * 512],
                                           start=(fc == 0), stop=(fc == NFC - 1))
                        return ins
                    S.op("pe", mm_d, reads=[("actT", fc) for fc in range(NFC)] + [("Wd", c) for c in range(NFC)],
                         writes=[("ps", yb[hh])])
                self.post_residual(srcap, dstap, row0 + s * 128, yb, 0.5, "sqrt", dst_off)

    def sg_phase(self, layer, src, dst, row_lo, row_hi, dst_off=0):
        S = self.S
        srcap, dstap = self.dram(src), self.dram(dst)
        E = 2048
        a, off = self.carve(0, 8 * 4096)
        Win = a.rearrange("p (c f) -> p c f", c=8)
        a, off = self.carve(off, 16 * D)
        Wout = a.rearrange("p (c d) -> p c d", c=16)
        zz, off = self.carve(off, 4096, F32)
        lng, off = self.carve(off, E, F32)
        lnb, off = self.carve(off, E, F32)
        vn, off = self.carve(off, E)
        gated, off = self.carve(off, E)
        a, off = self.carve(off, E)
        gT = a.rearrange("p (c t) -> p c t", c=16)
        a, off = self.carve(off, 1024)
        wcT = a.rearrange("p (g t) -> p g t", g=8)
        a, off = self.carve(off, 1024)
        wcb = a.rearrange("p (g t) -> p g t", g=8)
        bhl, off = self.carve(off, 4096)
        bf32 = zz
        stat = self.stat
        hT = self.hT
        win_d = self.sg_w_in[0].rearrange("(c p) f -> p c f", p=128)
        wout_d = self.sg_w_out[0].rearrange("(c p) d -> p c d", p=128)
        self.load_gvec(self.gpre, self.norm_g[layer, 2:3, :], "gpre")
        self.load_gvec(self.gpost, self.norm_g[layer, 3:4, :], "gpost")
        for c in range(8):
            S.op("pool", lambda e, c=c: e.dma_start(out=Win[:, c, :], in_=win_d[:, c, :]),
                 writes=[("Wsi", c)], dma="wgu")
        for c in range(0, 16, 2):
            S.op("pool", lambda e, c=c: e.dma_start(out=Wout[:, c:c + 2, :], in_=wout_d[:, c:c + 2, :]),
                 writes=[("Wso", c), ("Wso", c + 1)], dma="wd")
        S.op("sp", lambda e: e.dma_start(out=lng[:], in_=self.sg_ln_g[0:1, :].partition_broadcast(128)),
             writes=["lng"], dma="cst")
        S.op("sp", lambda e: e.dma_start(out=lnb[:], in_=self.sg_ln_b[0:1, :].partition_broadcast(128)),
             writes=["lnb"], dma="cst")
        S.op("sp", lambda e: e.dma_start(out=bf32[0:1, :], in_=self.sg_b_in[0:1, :]), writes=[("zz", b_) for b_ in range(8)], dma="cst")
        S.op("sp", lambda e: e.dma_start(out=stat[:, 48:56], in_=self.sg_b_s[0].rearrange("g t -> t g"),
                                         allow_slow_non_contiguous=True), writes=["bstok"], dma="cst")
        S.op("dve", lambda e: e.tensor_copy(out=bhl[0:1, :], in_=bf32[0:1, :]), reads=[("zz", b_) for b_ in range(8)], writes=["bhl0"])
        S.op("dve", lambda e: e.tensor_tensor(out=bf32[0:1, :], in0=bf32[0:1, :], in1=bhl[0:1, :], op=ALU.subtract),
             reads=[("zz", b_) for b_ in range(8)] + ["bhl0"], writes=[("zz", b_) for b_ in range(8)])
        S.op("dve", lambda e: e.tensor_copy(out=vn[0:1, :], in_=bf32[0:1, 0:2048]), reads=[("zz", b_) for b_ in range(8)], writes=["vn"])
        S.op("dve", lambda e: e.tensor_copy(out=gated[0:1, :], in_=bf32[0:1, 2048:4096]), reads=[("zz", b_) for b_ in range(8)],
             writes=[("gated", g) for g in range(8)])
        S.op("sp", lambda e: e.dma_start(out=bhl[1:2, 0:2048], in_=vn[0:1, :]), reads=["vn"], writes=["bhl1"], dma="cst")
        S.op("sp", lambda e: e.dma_start(out=bhl[1:2, 2048:4096], in_=gated[0:1, :]),
             reads=[("gated", g) for g in range(8)], writes=["bhl1b"], dma="cst")
        ones2 = self.onesb[0:2, 0:128]
        wtmp = self.xnew[0]
        wtv = wtmp[:].rearrange("p (g s) -> p g s", g=8)
        S.op("sp", lambda e: e.dma_start(out=wtv, in_=self.sg_w_s[0].rearrange("g t s -> t g s")),
             writes=[("xw", 0, 0), ("xw", 0, 1)], dma="cst")
        S.op("pool", lambda e: e.affine_select(out=wtv, in_=wtv, pattern=[[0, 8], [-1, 128]], compare_op=ALU.is_ge,
                                               fill=0.0, base=0, channel_multiplier=1),
             reads=[("xw", 0, 0), ("xw", 0, 1)], writes=[("xw", 0, 0), ("xw", 0, 1)])
        S.op("pool", lambda e: e.tensor_copy(out=wcb[:], in_=wtv), reads=[("xw", 0, 0), ("xw", 0, 1)], writes=["wcb"])
        pTw = self.PS[:, 7, :].bitcast(BF16)

        def trw(e):
            ins = None
            for g in range(8):
                ins = e.transpose(out=pTw[:, g * 128:(g + 1) * 128], in_=wcb[:, g, :], identity=self.ident[:])
            return ins
        S.op("pe", trw, reads=["wcb", "ident"], writes=[("ps", 7)])
        S.op("act", lambda e: e.activation(out=wcT[:], in_=pTw.rearrange("p (g t) -> p g t", g=8), func=AF.Copy),
             writes=[("ps", 7), "wcT"])
        for row0 in range(row_lo, row_hi, T):
            self.prenorm_tile(srcap, row0, "sqrt")
            for s in range(4):
                tok = slice(s * 128, (s + 1) * 128)
                for blk in range(8):
                    b = 4 + blk % 2

                    def mm_in(e, blk=blk, b=b, tok=tok):
                        ins = None
                        for c in range(8):
                            ins = e.matmul(self.PS[:, b, :], lhsT=hT[:, c, tok], rhs=Win[:, c, blk * 512:(blk + 1) * 512],
                                           start=(c == 0), stop=False)
                        ins = e.matmul(self.PS[:, b, :], lhsT=ones2, rhs=bhl[0:2, blk * 512:(blk + 1) * 512],
                                       start=False, stop=True)
                        return ins
                    S.op("pe", mm_in, reads=[("hT", s), "bhl0", "bhl1", "bhl1b", "onesb"] + [("Wsi", c) for c in range(8)],
                         writes=[("ps", b)])
                    if blk < 4:
                        S.op("act", lambda e, blk=blk, b=b: e.activation(
                            out=zz[:, blk * 512:(blk + 1) * 512], in_=self.PS[:, b, :], func=AF.Gelu),
                            writes=[("ps", b), ("zz", blk)])
                    else:
                        S.op("act", lambda e, blk=blk, b=b: e.activation(
                            out=zz[:, blk * 512:(blk + 1) * 512], in_=self.PS[:, b, :], func=AF.Gelu,
                            accum_out=stat[:, 56 + blk - 4:57 + blk - 4]),
                            writes=[("ps", b), ("zz", blk), ("vs", blk)])
                zv = zz[:, E:2 * E]
                S.op("dve", lambda e: e.tensor_reduce(out=stat[:, 60:61], in_=stat[:, 56:60], axis=AX.X, op=ALU.add),
                     reads=[("vs", b_) for b_ in range(4, 8)], writes=["vsum"])
                S.op("dve", lambda e: e.tensor_scalar(out=stat[:, 61:62], in0=stat[:, 60:61], scalar1=-1.0 / E,
                                                      scalar2=None, op0=ALU.mult), reads=["vsum"], writes=["vnm"])
                S.op("act", lambda e: e.activation(out=self.junk[:, :], in_=zv[:, 0:1024], func=AF.Square,
                                                   bias=stat[:, 61:62], accum_out=stat[:, 62:63]),
                     reads=["vnm", ("zz", 4), ("zz", 5)], writes=["junk", "vq0"])
                S.op("act", lambda e: e.activation(out=self.junk[:, :], in_=zv[:, 1024:2048], func=AF.Square,
                                                   bias=stat[:, 61:62], accum_out=stat[:, 63:64]),
                     reads=["vnm", ("zz", 6), ("zz", 7)], writes=["junk", "vq1"])
                S.op("dve", lambda e: e.tensor_tensor(out=stat[:, 12:13], in0=stat[:, 62:63], in1=stat[:, 63:64], op=ALU.add),
                     reads=["vq0", "vq1"], writes=["vq"])
                self.rstd(stat[:, 12:13], stat[:, 13:14], stat[:, 14:15], 1.0 / E, LN_EPS, "sqrt", ["vq"], "vrs")
                S.op("dve", lambda e: e.tensor_scalar(out=zv, in0=zv, scalar1=stat[:, 61:62], scalar2=stat[:, 14:15],
                                                      op0=ALU.add, op1=ALU.mult),
                     reads=["vnm", "vrs", "vq0", "vq1"] + [("zz", b_) for b_ in range(4, 8)],
                     writes=[("zz", b_) for b_ in range(4, 8)])
                S.op("pool", lambda e: e.tensor_tensor(out=zv, in0=zv, in1=lng[:], op=ALU.mult),
                     reads=["lng"] + [("zz", b_) for b_ in range(4, 8)], writes=[("zz", b_) for b_ in range(4, 8)])
                S.op("pool", lambda e: e.tensor_tensor(out=vn[:], in0=zv, in1=lnb[:], op=ALU.add),
                     reads=["lnb"] + [("zz", b_) for b_ in range(4, 8)], writes=["vn"])
                for g in range(8):
                    b = g // 2
                    S.op("pe", lambda e, g=g, b=b: e.matmul(
                        self.PS[:, b, (g % 2) * 256:(g % 2) * 256 + 256], lhsT=wcT[:, g, :],
                        rhs=vn[:, g * 256:(g + 1) * 256], start=True, stop=True),
                        reads=["vn", "wcT"], writes=[("ps", b)])
                    S.op("dve", lambda e, g=g, b=b: e.scalar_tensor_tensor(
                        out=gated[:, g * 256:(g + 1) * 256], in0=self.PS[:, b, (g % 2) * 256:(g % 2) * 256 + 256],
                        scalar=stat[:, 48 + g:49 + g], in1=zz[:, g * 256:(g + 1) * 256], op0=ALU.add, op1=ALU.mult),
                        reads=["bstok", ("zz", g // 2)], writes=[("ps", b), ("gated", g)])
                for half in range(2):
                    pb = 6 + half
                    pT = self.PS[:, pb, :].bitcast(BF16)

                    def trg(e, half=half, pT=pT):
                        ins = None
                        for c in range(8):
                            ec = half * 8 + c
                            ins = e.transpose(out=pT[:, c * 128:(c + 1) * 128], in_=gated[:, ec * 128:(ec + 1) * 128],
                                              identity=self.ident[:])
                        return ins
                    S.op("pe", trg, reads=[("gated", g) for g in range(8)] + ["ident"], writes=[("ps", pb)])
                    S.op("act", lambda e, half=half, pT=pT: e.activation(
                        out=gT[:, half * 8:(half + 1) * 8, :], in_=pT.rearrange("p (c t) -> p c t", c=8), func=AF.Copy),
                        writes=[("ps", pb), ("gT", half)])
                yb = (0, 1) if s % 2 == 0 else (2, 3)
                for hh in range(2):
                    def mm_o(e, hh=hh, b=yb[hh]):
                        ins = None
                        for ec in range(16):
                            ins = e.matmul(self.PS[:, b, :], lhsT=gT[:, ec, :], rhs=Wout[:, ec, hh * 512:(hh + 1) * 512],
                                           start=(ec == 0), stop=(ec == 15))
                        return ins
                    S.op("pe", mm_o, reads=[("gT", 0), ("gT", 1)] + [("Wso", c) for c in range(16)], writes=[("ps", yb[hh])])
                self.post_residual(srcap, dstap, row0 + s * 128, yb, 1.0, "sqrt", dst_off)

    def dn_phase(self, layer, src, dst, row_lo, row_hi, full_from, dst_off=0):
        import itertools
        S = self.S
        srcap, dstap = self.dram(src), self.dram(dst)
        TD = 256
        PS = self.PS
        hT = self.hT
        a, off = self.carve(0, 8 * 4112)
        Win = a.rearrange("p (c f) -> p c f", c=8)
        a, off = self.carve(off, 8 * D)
        Wout = a.rearrange("p (c d) -> p c d", c=8)
        a, off = self.carve(off, 8 * TD); qT = a.rearrange("p (h t) -> p h t", h=8)
        a, off = self.carve(off, 8 * TD); kT = a.rearrange("p (h t) -> p h t", h=8)
        a, off = self.carve(off, 8 * TD); vT = a.rearrange("p (h t) -> p h t", h=8)
        pc = []
        cv = []
        for i in range(2):
            a, off = self.carve(off, 260, F32); pc.append(a)
        for i in range(2):
            a, off = self.carve(off, TD, F32); cv.append(a)
        sq, off = self.carve(off, TD)
        rinv, off = self.carve(off, TD, F32)
        a, off = self.carve(off, 72, F32); carry = a.rearrange("p (c j) -> p c j", c=24)
        a, off = self.carve(off, 96, F32); cw = a.rearrange("p (c j) -> p c j", c=24)
        blk_off = off
        zs2 = []
        for i in range(2):
            a, off = self.carve(off, 1024, F32); zs2.append(a)
        a, off = self.carve(off, 1024); vb = a.rearrange("p (h d) -> p h d", h=8)
        a, off = self.carve(off, 1024); kbg = a.rearrange("p (h d) -> p h d", h=8)
        a, off = self.carve(off, 1024); kd = a.rearrange("p (h d) -> p h d", h=8)
        dec2, off = self.carve(off, 1024, F32); dec = dec2.rearrange("p (h j) -> p h j", h=8)
        attn2, off = self.carve(off, 1024); attn = attn2.rearrange("p (h j) -> p h j", h=8)
        attnT2, off = self.carve(off, 1024); attnT = attnT2.rearrange("p (h j) -> p h j", h=8)
        A2 = []; B2 = []
        for i in range(2):
            a, off = self.carve(off, 1024); A2.append(a)
        for i in range(2):
            a, off = self.carve(off, 1024); B2.append(a)
        A = [x.rearrange("p (h j) -> p h j", h=8) for x in A2]
        B = [x.rearrange("p (h j) -> p h j", h=8) for x in B2]
        TT2, off = self.carve(off, 1024); TT = TT2.rearrange("p (h j) -> p h j", h=8)
        ubuf2, off = self.carve(off, 1024, F32); ubuf = ubuf2.rearrange("p (h d) -> p h d", h=8)
        wT2, off = self.carve(off, 1024); wT = wT2.rearrange("p (h t) -> p h t", h=8)
        qdT2, off = self.carve(off, 1024); qdT = qdT2.rearrange("p (h t) -> p h t", h=8)
        osb2 = dec2; osb = dec
        St2, off = self.carve(off, 1024, F32); St = St2.rearrange("p (h d) -> p h d", h=8)
        Sb2, off = self.carve(off, 1024); Sb = Sb2.rearrange("p (h d) -> p h d", h=8)
        vnew2, off = self.carve(off, 1024); vnew = vnew2.rearrange("p (h d) -> p h d", h=8)
        gatedb = vnew2
        a, off = self.carve(off, 1024); gatedT = a.rearrange("p (c t) -> p c t", c=8)
        dng, off = self.carve(off, 1024, F32)
        cwj, _ = self.carve(blk_off, 3072, F32)
        dsc = self.sgt[0]
        cst = self.sgt[1]
        Tri = cst[:, 0:128]
        Mc = cst[:, 128:256]
        Bones = cst[:, 256:384]
        Bsel0 = cst[:, 384:512]
        Bsel1 = dsc[:, 128:256]
        onesf, onesb, ident = self.onesf, self.onesb, self.ident
        C_T1, C_XA, C_BETA, C_NB, C_SP, C_G, C_GC, C_EGC, C_EDK, C_GL, C_BEGC, C_DTB, C_NEA, C_OSS = \
            0, 8, 16, 24, 32, 40, 48, 56, 64, 72, 88, 96, 104, 112

        def col(c, n=8):
            return dsc[:, c:c + n]
        win_d = self.dn_w_in[0].rearrange("(c p) f -> p c f", p=128)
        wout_d = self.dn_w_out[0].rearrange("(c p) d -> p c d", p=128)
        self.load_gvec(self.gpre, self.norm_g[layer, 2:3, :], "gpre")
        self.load_gvec(self.gpost, self.norm_g[layer, 3:4, :], "gpost")
        for c in range(8):
            S.op("pool", lambda e, c=c: e.dma_start(out=Win[:, c, :], in_=win_d[:, c, :]),
                 writes=[("Wdi", c)], dma="wgu")
        for c in range(0, 8, 2):
            S.op("pool", lambda e, c=c: e.dma_start(out=Wout[:, c:c + 2, :], in_=wout_d[:, c:c + 2, :]),
                 writes=[("Wdo", c), ("Wdo", c + 1)], dma="wd")
        for h in range(8):
            S.op("sp", lambda e, h=h: e.dma_start(out=dng[:, h * 128:(h + 1) * 128],
                                                  in_=self.dn_norm_g[0:1, :].partition_broadcast(128)),
                 writes=[("dng", h)], dma="cst")
        S.op("sp", lambda e: e.dma_start(out=col(C_DTB), in_=self.dn_dt_bias[0:1, :].partition_broadcast(128)),
             writes=["dtb"], dma="cst")
        S.op("sp", lambda e: e.dma_start(out=col(C_NEA), in_=self.dn_a_log[0:1, :].partition_broadcast(128)),
             writes=["nea"], dma="cst")
        S.op("act", lambda e: e.activation(out=col(C_NEA), in_=col(C_NEA), func=AF.Exp), reads=["nea"], writes=["nea"])
        S.op("dve", lambda e: e.tensor_scalar(out=col(C_NEA), in0=col(C_NEA), scalar1=-1.0, scalar2=None, op0=ALU.mult),
             reads=["nea"], writes=["nea"])
        S.op("sp", lambda e: e.dma_start(out=cwj[0:4, :], in_=self.dn_conv_w[0]), writes=["cwj"], dma="cst")
        for cc in range(24):
            S.op("pe", lambda e, cc=cc: e.transpose(out=PS[:, 7, cc * 4:(cc + 1) * 4], in_=cwj[0:4, cc * 128:(cc + 1) * 128],
                                                    identity=self.identf[0:4, 0:4]),
                 reads=["cwj", "identf"], writes=[("ps", 7)])
        S.op("act", lambda e: e.activation(out=cw[:], in_=PS[:, 7, 0:96].rearrange("p (c j) -> p c j", c=24), func=AF.Copy),
             writes=[("ps", 7), "cw"])
        S.op("pool", lambda e: e.memset(Tri, 1.0), writes=["Tri"])
        S.op("pool", lambda e: e.affine_select(out=Tri, in_=Tri, pattern=[[1, 128]], compare_op=ALU.is_ge, fill=0.0,
                                               base=0, channel_multiplier=-1), reads=["Tri"], writes=["Tri"])
        S.op("pool", lambda e: e.memset(cst[0:64, 64:128], 0.0), reads=["Tri"], writes=["Tri"])
        S.op("pool", lambda e: e.memset(Bones, 0.0), writes=["Bones"])
        S.op("pool", lambda e: e.memset(cst[0:64, 256:320], 1.0), reads=["Bones"], writes=["Bones"])
        S.op("pool", lambda e: e.memset(cst[64:128, 320:384], 1.0), reads=["Bones"], writes=["Bones"])
        S.op("pool", lambda e: e.memset(Mc, 3.0e4), writes=["Mc"])
        S.op("pool", lambda e: e.affine_select(out=Mc, in_=Mc, pattern=[[1, 128]], compare_op=ALU.is_ge, fill=0.0,
                                               base=-1, channel_multiplier=-1), reads=["Mc"], writes=["Mc"])
        S.op("pool", lambda e: e.memset(cst[64:128, 128:192], 3.0e4), reads=["Mc"], writes=["Mc"])
        S.op("pool", lambda e: e.memset(Bsel0, 0.0), writes=["Bsel"])
        S.op("pool", lambda e: e.memset(cst[0:64, 384:512], 1.0), reads=["Bsel"], writes=["Bsel"])
        S.op("pool", lambda e: e.memset(Bsel1, 0.0), reads=["Bsel"], writes=["Bsel"])
        S.op("pool", lambda e: e.memset(dsc[64:128, 128:256], 1.0), reads=["Bsel"], writes=["Bsel"])
        S.op("pool", lambda e: e.memset(carry[:], 0.0), writes=["carry"])
        S.op("pool", lambda e: e.memset(St2, 0.0), writes=[("St", 0), ("St", 1)])
        S.op("pool", lambda e: e.memset(Sb2, 0.0), writes=[("Sb", 0), ("Sb", 1)])
        S.barrier()
        QSCALE = 128.0 ** -0.5

        def pb_of(h, base):
            return PS[:, base + h // 4, (h % 4) * 128:(h % 4) * 128 + 128]

        for row0 in range(row_lo, row_hi, TD):
            full = row0 >= full_from
            need_q = full or (row0 + TD) >= full_from
            self.prenorm_tile(srcap, row0, "explog", nsub=2)
            hkeys = [("hT", 0), ("hT", 1)]
            for cc in range(24):
                if cc < 8 and not need_q:
                    continue
                par = cc % 2
                pb = 2 + par

                def mm_qkv(e, cc=cc, pb=pb):
                    ins = None
                    for c in range(8):
                        ins = e.matmul(PS[:, pb, 0:TD], lhsT=Win[:, c, cc * 128:(cc + 1) * 128], rhs=hT[:, c, 0:TD],
                                       start=(c == 0), stop=(c == 7))
                    return ins
                S.op("pe", mm_qkv, reads=hkeys + [("Wdi", c) for c in range(8)], writes=[("ps", pb)])
                pcb, cvb = pc[par], cv[par]
                S.op("pool", lambda e, cc=cc, pcb=pcb: e.tensor_copy(out=pcb[:, 0:3], in_=carry[:, cc, :]),
                     reads=["carry"], writes=[("pc", par, 0)])
                S.op("act", lambda e, pb=pb, pcb=pcb: e.activation(out=pcb[:, 3:3 + TD], in_=PS[:, pb, 0:TD], func=AF.Copy),
                     writes=[("ps", pb), ("pc", par, 1)])
                S.op("pool", lambda e, cc=cc, pcb=pcb: e.tensor_copy(out=carry[:, cc, :], in_=pcb[:, TD:TD + 3]),
                     reads=[("pc", par, 1), ("pc", par, 0)], writes=["carry"])
                pk = [("pc", par, 0), ("pc", par, 1), "cw"]
                S.op("dve", lambda e, cc=cc, pcb=pcb, cvb=cvb: e.tensor_scalar(
                    out=cvb, in0=pcb[:, 0:TD], scalar1=cw[:, cc, 0:1], scalar2=None, op0=ALU.mult),
                    reads=pk, writes=[("cv", par)])
                for j in range(1, 4):
                    S.op("dve", lambda e, cc=cc, pcb=pcb, cvb=cvb, j=j: e.scalar_tensor_tensor(
                        out=cvb, in0=pcb[:, j:j + TD], scalar=cw[:, cc, j:j + 1], in1=cvb, op0=ALU.mult, op1=ALU.add),
                        reads=pk + [("cv", par)], writes=[("cv", par)])
                dstT = qT if cc < 8 else (kT if cc < 16 else vT)
                nm = "qT" if cc < 8 else ("kT" if cc < 16 else "vT")
                S.op("act", lambda e, cvb=cvb, dstT=dstT, cc=cc: e.activation(out=dstT[:, cc % 8, :], in_=cvb, func=AF.Silu),
                     reads=[("cv", par)], writes=[(nm, cc % 8)])
            if full:
                for blk in range(2):
                    for hb in range(2):
                        def mm_z(e, hb=hb, blk=blk):
                            ins = None
                            for c in range(8):
                                ins = e.matmul(PS[:, 4 + hb, :], lhsT=hT[:, c, blk * 128:(blk + 1) * 128],
                                               rhs=Win[:, c, 3072 + hb * 512:3072 + (hb + 1) * 512], start=(c == 0), stop=(c == 7))
                            return ins
                        S.op("pe", mm_z, reads=[("hT", blk)] + [("Wdi", c) for c in range(8)], writes=[("ps", 4 + hb)])
                        zsl = zs2[blk][:, hb * 512:(hb + 1) * 512]
                        S.op("act", lambda e, hb=hb, zsl=zsl: e.activation(out=zsl, in_=PS[:, 4 + hb, :], func=AF.Silu),
                             writes=[("ps", 4 + hb), ("zs", blk, hb)])
                        S.op("pool", lambda e, hb=hb, zsl=zsl: e.tensor_tensor(out=zsl, in0=zsl, in1=dng[:, hb * 512:(hb + 1) * 512],
                                                                               op=ALU.mult),
                             reads=[("zs", blk, hb)] + [("dng", h) for h in range(8)], writes=[("zs", blk, hb)])
            for cc in range(16):
                if cc < 8 and not need_q:
                    continue
                dstT = qT if cc < 8 else kT
                nm = "qT" if cc < 8 else "kT"
                hh_ = cc % 8
                lb = 2 + cc % 2
                S.op("act", lambda e, dstT=dstT, hh_=hh_: e.activation(out=sq, in_=dstT[:, hh_, :], func=AF.Square),
                     reads=[(nm, hh_)], writes=["sq"])
                S.op("pe", lambda e, lb=lb: e.matmul(PS[:, lb, 0:TD], lhsT=onesb[:], rhs=sq, start=True, stop=True),
                     reads=["sq", "onesb"], writes=[("ps", lb)])
                S.op("dve", lambda e, lb=lb: e.tensor_scalar(out=rinv, in0=PS[:, lb, 0:TD], scalar1=1e-6, scalar2=None, op0=ALU.add),
                     reads=["rinv"], writes=[("ps", lb), "rinv"])
                S.op("act", lambda e: e.activation(out=rinv, in_=rinv, func=AF.Ln), reads=["rinv"], writes=["rinv"])
                S.op("act", lambda e: e.activation(out=rinv, in_=rinv, func=AF.Exp, scale=-0.5), reads=["rinv"], writes=["rinv"])
                sc = QSCALE if cc < 8 else 1.0
                S.op("dve", lambda e, dstT=dstT, hh_=hh_, sc=sc: e.scalar_tensor_tensor(
                    out=dstT[:, hh_, :], in0=dstT[:, hh_, :], scalar=sc, in1=rinv, op0=ALU.mult, op1=ALU.mult),
                    reads=[(nm, hh_), "rinv"], writes=[(nm, hh_)])
            for blk in range(2):
                tb = slice(blk * 128, (blk + 1) * 128)
                brow = row0 + blk * 128
                hk = [("hT", blk)]
                zs = zs2[blk]

                def mm_ba(e, tb=tb):
                    ins = None
                    for c in range(8):
                        ins = e.matmul(PS[:, 6, 0:16], lhsT=hT[:, c, tb], rhs=Win[:, c, 4096:4112], start=(c == 0), stop=(c == 7))
                    return ins
                S.op("pe", mm_ba, reads=hk + [("Wdi", c) for c in range(8)], writes=[("ps", 6)])
                S.op("act", lambda e: e.activation(out=col(C_T1), in_=PS[:, 6, 0:8], func=AF.Exp, scale=-1.0),
                     writes=[("ps", 6), "t1"])
                S.op("dve", lambda e: e.tensor_tensor(out=col(C_XA), in0=PS[:, 6, 8:16], in1=col(C_DTB), op=ALU.add),
                     reads=["dtb"], writes=[("ps", 6), "xa"])
                S.op("dve", lambda e: e.tensor_scalar(out=col(C_T1), in0=col(C_T1), scalar1=1.0, scalar2=None, op0=ALU.add),
                     reads=["t1"], writes=["t1"])
                S.op("dve", lambda e: e.reciprocal(out=col(C_BETA), in_=col(C_T1)), reads=["t1"], writes=["beta"])
                S.op("dve", lambda e: e.tensor_scalar(out=col(C_NB), in0=col(C_BETA), scalar1=-1.0, scalar2=None, op0=ALU.mult),
                     reads=["beta"], writes=["nb"])
                S.op("act", lambda e: e.activation(out=col(C_XA), in_=col(C_XA), func=AF.Exp), reads=["xa"], writes=["xa"])
                S.op("dve", lambda e: e.tensor_scalar(out=col(C_XA), in0=col(C_XA), scalar1=1.0, scalar2=None, op0=ALU.add),
                     reads=["xa"], writes=["xa"])
                S.op("act", lambda e: e.activation(out=col(C_SP), in_=col(C_XA), func=AF.Ln), reads=["xa"], writes=["sp"])
                S.op("dve", lambda e: e.tensor_tensor(out=col(C_G), in0=col(C_SP), in1=col(C_NEA), op=ALU.mult),
                     reads=["sp", "nea"], writes=["g"])

                def mm_gc(e):
                    e.matmul(PS[:, 6, 16:24], lhsT=Tri, rhs=col(C_G), start=True, stop=True)
                    e.matmul(PS[:, 6, 24:32], lhsT=Bones, rhs=col(C_G), start=True, stop=True)
                    e.matmul(PS[:, 6, 32:40], lhsT=Bsel0, rhs=col(C_G), start=True, stop=True)
                    return e.matmul(PS[:, 6, 40:48], lhsT=Bsel1, rhs=col(C_G), start=True, stop=True)
                S.op("pe", mm_gc, reads=["g", "Tri", "Bones", "Bsel"], writes=[("ps", 6)])
                S.op("act", lambda e: e.activation(out=col(C_GC), in_=PS[:, 6, 16:24], func=AF.Copy), writes=[("ps", 6), "gc"])
                S.op("act", lambda e: e.activation(out=col(C_EGC), in_=PS[:, 6, 16:24], func=AF.Exp), writes=[("ps", 6), "egc"])
                S.op("dve", lambda e: e.tensor_tensor(out=col(C_EDK), in0=PS[:, 6, 24:32], in1=col(C_GC), op=ALU.subtract),
                     reads=["gc"], writes=[("ps", 6), "edk"])
                S.op("act", lambda e: e.activation(out=col(C_EDK), in_=col(C_EDK), func=AF.Exp), reads=["edk"], writes=["edk"])
                S.op("act", lambda e: e.activation(out=col(C_GL, 16), in_=PS[:, 6, 32:48], func=AF.Exp), writes=[("ps", 6), "gl"])
                S.op("dve", lambda e: e.tensor_tensor(out=col(C_BEGC), in0=col(C_BETA), in1=col(C_EGC), op=ALU.mult),
                     reads=["beta", "egc"], writes=["begc"])

                def grp(hg, tb=tb, full=full, zs=zs):
                    H = range(hg * 4, hg * 4 + 4)
                    cs_ = slice(hg * 512, (hg + 1) * 512)
                    b0, b1, b2, bt = hg, 2 + hg, 4 + hg, 6 + hg
                    pT = PS[:, bt, :].bitcast(BF16)[:, 0:512]
                    KH = lambda nm: [(nm, h) for h in H]
                    for h in H:
                        S.op("pool", lambda e, h=h: e.tensor_scalar(out=ubuf[:, h, :], in0=Tri, scalar1=dsc[:, C_G + h:C_G + h + 1],
                                                                    scalar2=None, op0=ALU.mult),
                             reads=["g", "Tri"], writes=[("ubuf", h)])
                    S.op("pe", lambda e: e.matmul(PS[:, b1, :], lhsT=onesf[:], rhs=ubuf2[:, cs_], start=True, stop=True),
                         reads=KH("ubuf") + ["onesf"], writes=[("ps", b1)])
                    yield
                    if full:
                        S.op("act", lambda e: e.activation(out=dec2[:, cs_], in_=PS[:, b1, :], func=AF.Exp),
                             writes=[("ps", b1), ("dec", hg)])
                        S.op("dve", lambda e: e.tensor_tensor(out=qdT[:, hg * 4:hg * 4 + 4, :], in0=qT[:, hg * 4:hg * 4 + 4, tb],
                                                              in1=dec[:, hg * 4:hg * 4 + 4, :], op=ALU.mult),
                             reads=KH("qT") + [("dec", hg)], writes=[("qdT", hg)])
                        yield
                    for h in H:
                        S.op("dve", lambda e, h=h: e.scalar_tensor_tensor(
                            out=dec[:, h, :], in0=pb_of(h, 2), scalar=dsc[:, C_GC + h:C_GC + h + 1], in1=Mc,
                            op0=ALU.subtract, op1=ALU.max),
                            reads=["gc", "Mc"], writes=[("ps", b1), ("dec", hg)])
                    S.op("act", lambda e: e.activation(out=dec2[:, cs_], in_=dec2[:, cs_], func=AF.Exp, scale=-1.0),
                         reads=[("dec", hg)], writes=[("dec", hg)])
                    yield
                    def tr_k(e):
                        ins = None
                        for i, h in enumerate(H):
                            ins = e.transpose(out=pT[:, i * 128:(i + 1) * 128], in_=kT[:, h, tb], identity=ident[:])
                        return ins
                    S.op("pe", tr_k, reads=KH("kT") + ["ident"], writes=[("ps", bt)])
                    for i, h in enumerate(H):
                        S.op("act", lambda e, h=h, i=i: e.activation(out=kbg[:, h, :], in_=pT[:, i * 128:(i + 1) * 128], func=AF.Copy,
                                                                     scale=dsc[:, C_BEGC + h:C_BEGC + h + 1]),
                             reads=["begc"], writes=[("ps", bt), ("kbg", h)])
                        S.op("act", lambda e, h=h, i=i: e.activation(out=kd[:, h, :], in_=pT[:, i * 128:(i + 1) * 128], func=AF.Copy,
                                                                     scale=dsc[:, C_EDK + h:C_EDK + h + 1]),
                             reads=["edk"], writes=[("ps", bt), ("kd", h)])
                    yield

                    def tr_v(e):
                        ins = None
                        for i, h in enumerate(H):
                            ins = e.transpose(out=pT[:, i * 128:(i + 1) * 128], in_=vT[:, h, tb], identity=ident[:])
                        return ins
                    S.op("pe", tr_v, reads=KH("vT") + ["ident"], writes=[("ps", bt)])
                    for i, h in enumerate(H):
                        S.op("act", lambda e, h=h, i=i: e.activation(out=vb[:, h, :], in_=pT[:, i * 128:(i + 1) * 128], func=AF.Copy,
                                                                     scale=dsc[:, C_BETA + h:C_BETA + h + 1]),
                             reads=["beta"], writes=[("ps", bt), ("vb", h)])
                    yield
                    def mm_kk(e):
                        ins = None
                        for h in H:
                            ins = e.matmul(pb_of(h, 4), lhsT=kT[:, h, tb], rhs=kT[:, h, tb], start=True, stop=True)
                        return ins
                    S.op("pe", mm_kk, reads=KH("kT"), writes=[("ps", b2)])
                    for h in H:
                        S.op("dve", lambda e, h=h: e.scalar_tensor_tensor(
                            out=A[0][:, h, :], in0=pb_of(h, 4), scalar=dsc[:, C_NB + h:C_NB + h + 1], in1=dec[:, h, :],
                            op0=ALU.mult, op1=ALU.mult),
                            reads=["nb", ("dec", hg)], writes=[("ps", b2), ("A", 0, hg)])
                    S.op("pool", lambda e: e.affine_select(out=A[0][:, hg * 4:hg * 4 + 4, :], in_=A[0][:, hg * 4:hg * 4 + 4, :],
                                                           pattern=[[0, 4], [-1, 128]], compare_op=ALU.not_equal, fill=0.0,
                                                           base=0, channel_multiplier=1),
                         reads=[("A", 0, hg)], writes=[("A", 0, hg)])
                    yield
                    if full:
                        def mm_qk(e):
                            ins = None
                            for h in H:
                                ins = e.matmul(pb_of(h, 0), lhsT=qT[:, h, tb], rhs=kT[:, h, tb], start=True, stop=True)
                            return ins
                        S.op("pe", mm_qk, reads=KH("kT") + KH("qT"), writes=[("ps", b0)])
                        S.op("dve", lambda e: e.tensor_tensor(out=attn2[:, cs_], in0=PS[:, b0, :], in1=dec2[:, cs_], op=ALU.mult),
                             reads=[("dec", hg)], writes=[("ps", b0), ("attn", hg)])
                        yield

                        def tr_at(e):
                            ins = None
                            for i, h in enumerate(H):
                                ins = e.transpose(out=pT[:, i * 128:(i + 1) * 128], in_=attn[:, h, :], identity=ident[:])
                            return ins
                        S.op("pe", tr_at, reads=[("attn", hg), "ident"], writes=[("ps", bt)])
                        S.op("act", lambda e: e.activation(out=attnT2[:, cs_], in_=pT, func=AF.Copy), writes=[("ps", bt), ("attnT", hg)])
                        yield

                    def tr_a0(e):
                        ins = None
                        for i, h in enumerate(H):
                            ins = e.transpose(out=pT[:, i * 128:(i + 1) * 128], in_=A[0][:, h, :], identity=ident[:])
                        return ins
                    S.op("pe", tr_a0, reads=[("A", 0, hg), "ident"], writes=[("ps", bt)])
                    S.op("act", lambda e: e.activation(out=B2[0][:, cs_], in_=pT, func=AF.Copy), writes=[("ps", bt), ("B", 0, hg)])
                    yield
                    for h in H:
                        S.op("pool", lambda e, h=h: e.tensor_tensor(out=TT[:, h, :], in0=B[0][:, h, :], in1=ident[:], op=ALU.add),
                             reads=[("B", 0, hg), "ident"], writes=[("TT", h)])
                    for k in range(5):
                        cur, nxt = k % 2, 1 - k % 2

                        def mm_a(e, cur=cur):
                            ins = None
                            for h in H:
                                ins = e.matmul(pb_of(h, 0), lhsT=B[cur][:, h, :], rhs=A[cur][:, h, :], start=True, stop=True)
                            return ins
                        S.op("pe", mm_a, reads=[("A", cur, hg), ("B", cur, hg)], writes=[("ps", b0)])
                        S.op("act", lambda e, nxt=nxt: e.activation(out=A2[nxt][:, cs_], in_=PS[:, b0, :], func=AF.Copy),
                             writes=[("ps", b0), ("A", nxt, hg)])
                        if k <= 3:
                            def mm_b(e, cur=cur):
                                ins = None
                                for h in H:
                                    ins = e.matmul(pb_of(h, 2), lhsT=A[cur][:, h, :], rhs=B[cur][:, h, :], start=True, stop=True)
                                return ins
                            S.op("pe", mm_b, reads=[("A", cur, hg), ("B", cur, hg)], writes=[("ps", b1)])
                            S.op("dve", lambda e, nxt=nxt: e.tensor_copy(out=B2[nxt][:, cs_], in_=PS[:, b1, :]),
                                 writes=[("ps", b1), ("B", nxt, hg)])
                        yield

                        def mm_t(e, nxt=nxt):
                            ins = None
                            for h in H:
                                ins = e.matmul(pb_of(h, 4), lhsT=A[nxt][:, h, :], rhs=TT[:, h, :], start=True, stop=True)
                            return ins
                        S.op("pe", mm_t, reads=[("A", nxt, hg)] + KH("TT"), writes=[("ps", b2)])
                        S.op("dve", lambda e: e.tensor_tensor(out=TT2[:, cs_], in0=TT2[:, cs_], in1=PS[:, b2, :], op=ALU.add),
                             reads=KH("TT"), writes=[("ps", b2)] + KH("TT"))
                        yield
                    def mm_u(e):
                        ins = None
                        for h in H:
                            ins = e.matmul(pb_of(h, 0), lhsT=TT[:, h, :], rhs=vb[:, h, :], start=True, stop=True)
                        return ins
                    S.op("pe", mm_u, reads=KH("TT") + KH("vb"), writes=[("ps", b0)])
                    S.op("act", lambda e: e.activation(out=ubuf2[:, cs_], in_=PS[:, b0, :], func=AF.Copy),
                         writes=[("ps", b0)] + KH("ubuf"))

                    def mm_w(e):
                        ins = None
                        for h in H:
                            ins = e.matmul(pb_of(h, 2), lhsT=kbg[:, h, :], rhs=TT[:, h, :], start=True, stop=True)
                        return ins
                    S.op("pe", mm_w, reads=KH("TT") + KH("kbg"), writes=[("ps", b1)])
                    S.op("dve", lambda e: e.tensor_copy(out=wT2[:, cs_], in_=PS[:, b1, :]), writes=[("ps", b1), ("wT", hg)])
                    yield
                    for c in range(2):
                        rs = slice(c * 64, (c + 1) * 64)

                        def mm_ws(e):
                            ins = None
                            for h in H:
                                ins = e.matmul(pb_of(h, 4), lhsT=wT[:, h, :], rhs=Sb[:, h, :], start=True, stop=True)
                            return ins
                        S.op("pe", mm_ws, reads=[("wT", hg), ("Sb", hg)], writes=[("ps", b2)])
                        S.op("dve", lambda e, rs=rs: e.tensor_tensor(out=vnew2[rs, cs_], in0=ubuf2[rs, cs_], in1=PS[rs, b2, :],
                                                                     op=ALU.subtract),
                             reads=KH("ubuf"), writes=[("ps", b2), ("vnew", hg)])
                        yield
                        if full:
                            def mm_o(e, rs=rs):
                                ins = None
                                for h in H:
                                    e.matmul(pb_of(h, 0), lhsT=qdT[:, h, :], rhs=Sb[:, h, :], start=True, stop=False)
                                    ins = e.matmul(pb_of(h, 0), lhsT=attnT[rs, h, :], rhs=vnew[rs, h, :], start=False, stop=True)
                                return ins
                            S.op("pe", mm_o, reads=[("qdT", hg), ("Sb", hg), ("attnT", hg), ("vnew", hg)], writes=[("ps", b0)])
                            S.op("act", lambda e, rs=rs: e.activation(out=osb2[rs, cs_], in_=PS[rs, b0, :], func=AF.Copy),
                                 writes=[("ps", b0), ("osb", c, hg), ("dec", hg)])

                        def mm_s(e, rs=rs):
                            ins = None
                            for h in H:
                                ins = e.matmul(pb_of(h, 2), lhsT=kd[rs, h, :], rhs=vnew[rs, h, :], start=True, stop=True)
                            return ins
                        S.op("pe", mm_s, reads=KH("kd") + [("vnew", hg)], writes=[("ps", b1)])
                        for h in H:
                            S.op("dve", lambda e, h=h, c=c: e.scalar_tensor_tensor(
                                out=St[:, h, :], in0=St[:, h, :], scalar=dsc[:, C_GL + c * 8 + h:C_GL + c * 8 + h + 1], in1=pb_of(h, 2),
                                op0=ALU.mult, op1=ALU.add),
                                reads=["gl", ("St", hg)], writes=[("ps", b1), ("St", hg)])
                        S.op("act", lambda e: e.activation(out=Sb2[:, cs_], in_=St2[:, cs_], func=AF.Copy),
                             reads=[("St", hg)], writes=[("Sb", hg)])
                        yield

                for _ in itertools.zip_longest(grp(0), grp(1)):
                    pass
                if not full:
                    continue
                S.op("act", lambda e: e.activation(out=self.junk[:], in_=osb2, func=AF.Square),
                     reads=[("osb", c_, hb_) for c_ in range(2) for hb_ in range(2)] + [("dec", 0), ("dec", 1)], writes=["junk"])
                S.op("dve", lambda e: e.tensor_reduce(out=col(C_OSS), in_=self.junk[:].rearrange("p (h d) -> p h d", h=8),
                                                      axis=AX.X, op=ALU.add), reads=["junk"], writes=["oss"])
                self.rstd(col(C_OSS), dsc[:, 120:128], col(C_OSS), 1.0 / 128, RMS_EPS, "explog", ["oss"], "orstd")
                for h in range(8):
                    S.op("dve", lambda e, h=h, zs=zs: e.scalar_tensor_tensor(
                        out=gatedb[:, h * 128:(h + 1) * 128], in0=osb[:, h, :], scalar=dsc[:, C_OSS + h:C_OSS + h + 1],
                        in1=zs[:, h * 128:(h + 1) * 128], op0=ALU.mult, op1=ALU.mult),
                        reads=["orstd", ("zs", blk, h // 4), ("dec", h // 4)] + [("osb", c_, h // 4) for c_ in range(2)],
                        writes=[("gatedb", h), ("vnew", h // 4)])
                pT4 = PS[:, 4, :].bitcast(BF16)

                def tr_g(e):
                    ins = None
                    for c_ in range(8):
                        ins = e.transpose(out=pT4[:, c_ * 128:(c_ + 1) * 128], in_=gatedb[:, c_ * 128:(c_ + 1) * 128], identity=ident[:])
                    return ins
                S.op("pe", tr_g, reads=[("gatedb", h) for h in range(8)] + ["ident", ("vnew", 0), ("vnew", 1)], writes=[("ps", 4)])
                S.op("act", lambda e: e.activation(out=gatedT[:], in_=pT4.rearrange("p (c t) -> p c t", c=8), func=AF.Copy),
                     writes=[("ps", 4), "gatedT"])
                yb = (6, 7)
                for hh in range(2):
                    def mm_out(e, hh=hh, b=yb[hh]):
                        ins = None
                        for c_ in range(8):
                            ins = e.matmul(PS[:, b, :], lhsT=gatedT[:, c_, :], rhs=Wout[:, c_, hh * 512:(hh + 1) * 512],
                                           start=(c_ == 0), stop=(c_ == 7))
                        return ins
                    S.op("pe", mm_out, reads=["gatedT"] + [("Wdo", c_) for c_ in range(8)], writes=[("ps", yb[hh])])
                self.post_residual(srcap, dstap, brow, yb, 1.0, "explog", dst_off)

    def dram(self, name):
        return {"x": self.x_in, "xs": self.xs, "y": self.y_out}[name]


_PROG_CACHE = {}


def get_program(npre, nown, layers_key):
    key = (npre, nown, layers_key)
    if key not in _PROG_CACHE:
        _PROG_CACHE[key] = Builder(npre, nown, list(layers_key)).build()
    return _PROG_CACHE[key]


NPRE = 4096
NOWN = 4096


def full_layers(npre, nown):
    nt = npre + nown
    return (
        ("ffn", (0, 0, "x", "xs", 0, nt, 0)),
        ("dn", (0, "xs", "xs", 0, nt, npre, 0)),
        ("ffn", (0, 1, "xs", "xs", npre, nt, 0)),
        ("ffn", (1, 0, "xs", "xs", npre, nt, 0)),
        ("sg", (1, "xs", "xs", npre, nt, 0)),
        ("ffn", (1, 1, "xs", "y", npre, nt, -npre)),
    )


def kernel(**inputs):
    x = np.ascontiguousarray(np.asarray(inputs["x"], dtype=np.float32))
    B, SEQ, _ = x.shape
    half = SEQ // 2
    nc = get_program(half, half, full_layers(half, half))
    wnames = ["norm_g", "ffn_w_gate", "ffn_w_up", "ffn_w_down", "dn_w_in", "dn_conv_w", "dn_a_log", "dn_dt_bias",
              "dn_norm_g", "dn_w_out", "sg_w_in", "sg_b_in", "sg_ln_g", "sg_ln_b", "sg_w_s", "sg_b_s", "sg_w_out"]
    shared = {k: np.ascontiguousarray(np.asarray(inputs[k], dtype=np.float32)) for k in wnames}
    in_maps = []
    for c in range(2 * B):
        b, hf = c // 2, c % 2
        own = x[b, hf * half:(hf + 1) * half]
        pre = x[b, 0:half] if hf == 1 else np.zeros_like(own)
        m = dict(shared)
        m["x"] = np.ascontiguousarray(np.concatenate([pre, own], axis=0))
        in_maps.append(m)
    res = run_bass_kernel_spmd(nc, in_maps, core_ids=list(range(2 * B)))
    out = np.empty_like(x)
    for c in range(2 * B):
        b, hf = c // 2, c % 2
        out[b, hf * half:(hf + 1) * half] = res.results[c]["y"]
    return out
```

```python
import contextlib
import os
DN_STOP = int(os.environ.get('DN_STOP', '99'))
import numpy as np
import concourse.bass as bass
import concourse.mybir as mybir
from concourse.bass_utils import run_bass_kernel_spmd

F32 = mybir.dt.float32
BF16 = mybir.dt.bfloat16
AF = mybir.ActivationFunctionType
ALU = mybir.AluOpType
AX = mybir.AxisListType

D = 1024
FF = 2816
NFC = FF // 128
T = 512
RMS_EPS = 1e-6
LN_EPS = 1e-5
ENGS = ("pe", "act", "dve", "pool", "sp")


class Op:
    __slots__ = ("eng", "fn", "reads", "writes", "dma", "deps", "sig", "sem", "val", "idx", "bar")

    def __init__(self, eng, fn, reads, writes, dma):
        self.eng = eng
        self.fn = fn
        self.reads = reads
        self.writes = writes
        self.dma = dma
        self.deps = []
        self.sig = False
        self.sem = None
        self.val = 0
        self.bar = 0


class Sched:
    def __init__(self, nc):
        self.nc = nc
        self.ops = []
        self.phase = 0
        self.nbar = 0

    def barrier(self):
        self.nbar += 1
        self.phase += 1

    def op(self, eng, fn, reads=(), writes=(), dma=None):
        if dma == "cst":
            writes = tuple(writes) + ("cstall",)
        o = Op(eng, fn, tuple(reads), tuple(writes), dma)
        o.idx = len(self.ops)
        o.bar = self.nbar
        o.sem = ("dma", dma) if dma is not None else ("eng", eng)
        self.ops.append(o)
        return o

    def emit(self):
        nc = self.nc
        ops = self.ops
        last_w = {}
        readers = {}
        last_eng = {}
        last_dma = {}
        seen_bar = {e: 0 for e in ENGS}
        bar_snap = None
        cur_bar = 0
        for o in ops:
            if o.bar != cur_bar:
                cur_bar = o.bar
                bar_snap = (dict(last_eng), dict(last_dma))
            deps = set()
            for r in o.reads:
                w = last_w.get(r)
                if w is not None:
                    deps.add(w)
            for wk in o.writes:
                w = last_w.get(wk)
                if w is not None:
                    deps.add(w)
                for rd in readers.get(wk, ()):
                    deps.add(rd)
            for r in o.reads:
                readers.setdefault(r, []).append(o.idx)
            for wk in o.writes:
                last_w[wk] = o.idx
                readers[wk] = []
            deps.discard(o.idx)
            real = []
            for d in deps:
                p = ops[d]
                if p.dma is None and o.dma is None and p.eng == o.eng:
                    if o.eng == "pe":
                        continue
                    if not any(r in p.writes for r in o.reads):
                        continue
                real.append(d)
            if seen_bar[o.eng] != o.bar:
                seen_bar[o.eng] = o.bar
                for e, i in bar_snap[0].items():
                    if e != o.eng:
                        real.append(i)
                for k, i in bar_snap[1].items():
                    real.append(i)
            for d in real:
                ops[d].sig = True
            o.deps = real
            if o.dma is None:
                last_eng[o.eng] = o.idx
            else:
                last_dma[o.dma] = o.idx
        counts = {}
        for o in ops:
            if o.dma is not None:
                counts[o.sem] = counts.get(o.sem, 0) + 16
                o.val = counts[o.sem]
            elif o.sig:
                counts[o.sem] = counts.get(o.sem, 0) + 1
                o.val = counts[o.sem]
        semkeys = sorted(counts.keys(), key=str)
        self.n_sems = len(semkeys)
        self.max_val = max(counts.values()) if counts else 0
        per_eng = {e: [] for e in ENGS}
        for o in ops:
            per_eng[o.eng].append(o)
        with contextlib.ExitStack() as st:
            sems = {k: st.enter_context(nc.semaphore("s%d" % i)) for i, k in enumerate(semkeys)}
            block = st.enter_context(nc.Block())

            def run(engname, eng):
                known = {}
                for o in per_eng[engname]:
                    need = {}
                    for d in o.deps:
                        p = ops[d]
                        if known.get(p.sem, 0) >= p.val:
                            continue
                        if need.get(p.sem, 0) < p.val:
                            need[p.sem] = p.val
                    for s, v in need.items():
                        eng.wait_ge(sems[s], v)
                        known[s] = v
                    ins = o.fn(eng)
                    if o.dma is not None:
                        ins.then_inc(sems[o.sem], 16)
                    elif o.sig:
                        ins.then_inc(sems[o.sem], 1)
                fin = {}
                for o in per_eng[engname]:
                    if o.dma is not None:
                        fin[o.sem] = max(fin.get(o.sem, 0), o.val)
                for s, v in fin.items():
                    if known.get(s, 0) < v:
                        eng.wait_ge(sems[s], v)

            if per_eng["sp"]:
                @block.sync
                def _(e):
                    run("sp", e)
            if per_eng["pe"]:
                @block.tensor
                def _(e):
                    run("pe", e)
            if per_eng["act"]:
                @block.scalar
                def _(e):
                    run("act", e)
            if per_eng["dve"]:
                @block.vector
                def _(e):
                    run("dve", e)
            if per_eng["pool"]:
                @block.gpsimd
                def _(e):
                    run("pool", e)


class Builder:
    def __init__(self, npre, nown, layers):
        self.npre = npre
        self.nown = nown
        self.ntot = npre + nown
        self.layers = layers
        self.nc = bass.Bass("TRN2", target_bir_lowering=False)
        self.S = Sched(self.nc)
        self.st = contextlib.ExitStack()
        self.uid = 0

    def sb(self, name, shape, dt):
        return self.st.enter_context(self.nc.sbuf_tensor(name, shape, dt))

    def dram_in(self, name, shape):
        return self.nc.dram_tensor(name, list(shape), F32, kind="ExternalInput").ap()

    def build(self):
        nc, S = self.nc, self.S
        with self.st:
            self.declare_io()
            self.alloc()
            self.consts()
            li = 0
            for (kind, arg) in self.layers:
                S.barrier()
                if kind == "ffn":
                    self.ffn_phase(*arg)
                elif kind == "sg":
                    self.sg_phase(*arg)
                elif kind == "dn":
                    self.dn_phase(*arg)
                li += 1
            S.emit()
        return nc

    def declare_io(self):
        nc = self.nc
        self.x_in = self.dram_in("x", (self.ntot, D))
        self.norm_g = self.dram_in("norm_g", (2, 6, D))
        self.w_gate = self.dram_in("ffn_w_gate", (2, 2, D, FF))
        self.w_up = self.dram_in("ffn_w_up", (2, 2, D, FF))
        self.w_down = self.dram_in("ffn_w_down", (2, 2, FF, D))
        self.dn_w_in = self.dram_in("dn_w_in", (1, D, 4112))
        self.dn_conv_w = self.dram_in("dn_conv_w", (1, 4, 3072))
        self.dn_a_log = self.dram_in("dn_a_log", (1, 8))
        self.dn_dt_bias = self.dram_in("dn_dt_bias", (1, 8))
        self.dn_norm_g = self.dram_in("dn_norm_g", (1, 128))
        self.dn_w_out = self.dram_in("dn_w_out", (1, D, D))
        self.sg_w_in = self.dram_in("sg_w_in", (1, D, 4096))
        self.sg_b_in = self.dram_in("sg_b_in", (1, 4096))
        self.sg_ln_g = self.dram_in("sg_ln_g", (1, 2048))
        self.sg_ln_b = self.dram_in("sg_ln_b", (1, 2048))
        self.sg_w_s = self.dram_in("sg_w_s", (1, 8, 128, 128))
        self.sg_b_s = self.dram_in("sg_b_s", (1, 8, 128))
        self.sg_w_out = self.dram_in("sg_w_out", (1, 2048, D))
        self.y_out = nc.dram_tensor("y", [self.nown, D], F32, kind="ExternalOutput").ap()
        self.xs = nc.dram_tensor("xs_scratch", [self.ntot, D], F32, kind="Internal").ap()

    def alloc(self):
        nc = self.nc
        self.ARENA_N = 79872
        self.ARENA = self.sb("ARENA", [128, self.ARENA_N], BF16)
        self.xring = [self.sb("xr%d" % i, [128, D], F32) for i in range(4)]
        self.xres = [self.sb("xq%d" % i, [128, D], F32) for i in range(1)]
        self.xnew = [self.sb("xw%d" % i, [128, D], F32) for i in range(1)]
        self.xn = [self.sb("xn%d" % i, [128, D], BF16) for i in range(2)]
        self.junk = self.sb("junk", [128, D], BF16)
        self.hT = self.sb("hT", [128, 8, T], BF16)
        self.gpre = self.sb("gpre", [128, D], F32)
        self.gpost = self.sb("gpost", [128, D], F32)
        self.sgt = [self.sb("sgt%d" % i, [128, T], F32) for i in range(2)]
        self.ident = self.sb("ident", [128, 128], BF16)
        self.identf = self.sb("identf", [128, 128], F32)
        self.stat = self.sb("stat", [128, 64], F32)
        self.onesb = self.sb("onesb", [128, 128], BF16)
        self.onesf = self.sb("onesf", [128, 128], F32)
        self.PS = self.st.enter_context(nc.psum_tensor("PS", [128, 8, 512], F32))
        self.cnt = {"xr": 0, "xq": 0, "xw": 0, "xn": 0, "st": 0}

    def consts(self):
        S = self.S
        ident, identf = self.ident, self.identf
        S.op("pool", lambda e: e.memset(identf[:], 0.0), writes=["identf"])
        S.op("pool", lambda e: e.affine_select(out=identf[:], in_=identf[:], pattern=[[-1, 128]],
                                               compare_op=ALU.not_equal, fill=1.0, base=0,
                                               channel_multiplier=1),
             reads=["identf"], writes=["identf"])
        S.op("pool", lambda e: e.tensor_copy(out=ident[:], in_=identf[:]), reads=["identf"], writes=["ident"])
        S.op("pool", lambda e: e.memset(self.onesb[:], 1.0), writes=["onesb"])
        S.op("pool", lambda e: e.memset(self.onesf[:], 1.0), writes=["onesf"])

    def carve(self, off, n, dt=BF16):
        if dt == BF16:
            assert off + n <= self.ARENA_N
            return self.ARENA[:, off:off + n], off + n
        assert off % 2 == 0 and off + 2 * n <= self.ARENA_N
        return self.ARENA[:, off:off + 2 * n].bitcast(F32), off + 2 * n

    def bank(self, b):
        return self.PS[:, b, :]

    def load_gvec(self, dst, src_row, key):
        self.S.op("sp", lambda e: e.dma_start(out=dst[:], in_=src_row.partition_broadcast(128)),
                  writes=[key], dma=key)

    def prenorm_tile(self, src, row0, rstd_mode, nsub=4):
        S = self.S
        stat, junk, hT, gpre = self.stat, self.junk, self.hT, self.gpre
        xs_ = []
        for s in range(nsub):
            i = self.cnt["xr"] % 4
            self.cnt["xr"] += 1
            xt = self.xring[i]
            r0 = row0 + s * 128
            S.op("sp", lambda e, xt=xt, r0=r0: e.dma_start(out=xt[:], in_=src[r0:r0 + 128, :]),
                 writes=[("xr", i)], dma=("xr", i))
            S.op("act", lambda e, xt=xt, s=s: e.activation(out=junk[:], in_=xt[:], func=AF.Square,
                                                           accum_out=stat[:, s:s + 1]),
                 reads=[("xr", i)], writes=["junk", ("stat", s)])
            xs_.append((xt, i))
        self.rstd(stat[:, 0:nsub], stat[:, 4:4 + nsub], stat[:, 8:8 + nsub], 1.0 / D, RMS_EPS, rstd_mode,
                  [("stat", s) for s in range(nsub)], "rs_pre")
        for s in range(nsub):
            xt, i = xs_[s]
            j = self.cnt["xn"] % 2
            self.cnt["xn"] += 1
            xn = self.xn[j]
            S.op("dve", lambda e, xt=xt, xn=xn, s=s: e.scalar_tensor_tensor(
                out=xn[:], in0=xt[:], scalar=stat[:, 8 + s:9 + s], in1=gpre[:], op0=ALU.mult, op1=ALU.mult),
                reads=[("xr", i), "rs_pre", "gpre"], writes=[("xn", j)])
            pb = s % 4
            pT = self.PS[:, pb, :].bitcast(BF16)

            def tr(e, xn=xn, pT=pT):
                ins = None
                for c in range(8):
                    ins = e.transpose(out=pT[:, c * 128:(c + 1) * 128], in_=xn[:, c * 128:(c + 1) * 128],
                                      identity=self.ident[:])
                return ins
            S.op("pe", tr, reads=[("xn", j), "ident"], writes=[("ps", pb)])
            S.op("act", lambda e, pT=pT, s=s: e.activation(
                out=hT[:, :, s * 128:(s + 1) * 128], in_=pT.rearrange("p (c t) -> p c t", c=8), func=AF.Copy),
                writes=[("ps", pb), ("hT", s)])

    def rstd(self, ss, tmp, out, scale, eps, mode, rkeys, wkey):
        S = self.S
        if mode == "sqrt":
            S.op("dve", lambda e: e.tensor_scalar(out=tmp, in0=ss, scalar1=scale, scalar2=eps,
                                                  op0=ALU.mult, op1=ALU.add),
                 reads=rkeys, writes=[(wkey, "t")])
            S.op("act", lambda e: e.activation(out=tmp, in_=tmp, func=AF.Sqrt),
                 reads=[(wkey, "t")], writes=[(wkey, "t")])
            S.op("dve", lambda e: e.reciprocal(out=out, in_=tmp), reads=[(wkey, "t")], writes=[wkey])
        else:
            S.op("dve", lambda e: e.tensor_scalar(out=tmp, in0=ss, scalar1=scale, scalar2=eps,
                                                  op0=ALU.mult, op1=ALU.add),
                 reads=rkeys, writes=[(wkey, "t")])
            S.op("act", lambda e: e.activation(out=tmp, in_=tmp, func=AF.Ln),
                 reads=[(wkey, "t")], writes=[(wkey, "t")])
            S.op("act", lambda e: e.activation(out=out, in_=tmp, func=AF.Exp, scale=-0.5),
                 reads=[(wkey, "t")], writes=[wkey])

    def post_residual(self, src, dst, row0, ybanks, coef, rstd_mode, dst_off=0):
        S = self.S
        stat, junk, gpost = self.stat, self.junk, self.gpost
        b0, b1 = ybanks
        q = self.cnt["xq"] % 1
        self.cnt["xq"] += 1
        xq = self.xres[q]
        S.op("sp", lambda e: e.dma_start(out=xq[:], in_=src[row0:row0 + 128, :]),
             writes=[("xq", q)], dma=("xq", q))
        k = self.cnt["st"] % 4
        self.cnt["st"] += 1
        c0 = 16 + k * 8
        S.op("act", lambda e: e.activation(out=junk[:, 0:512], in_=self.PS[:, b0, :], func=AF.Square,
                                           accum_out=stat[:, c0:c0 + 1]),
             writes=[("ps", b0), "junk", ("pst", k, 0)])
        S.op("act", lambda e: e.activation(out=junk[:, 512:1024], in_=self.PS[:, b1, :], func=AF.Square,
                                           accum_out=stat[:, c0 + 1:c0 + 2]),
             writes=[("ps", b1), "junk", ("pst", k, 1)])
        S.op("dve", lambda e: e.tensor_tensor(out=stat[:, c0 + 2:c0 + 3], in0=stat[:, c0:c0 + 1],
                                              in1=stat[:, c0 + 1:c0 + 2], op=ALU.add),
             reads=[("pst", k, 0), ("pst", k, 1)], writes=[("pst", k, 2)])
        self.rstd(stat[:, c0 + 2:c0 + 3], stat[:, c0 + 3:c0 + 4], stat[:, c0 + 4:c0 + 5], 1.0 / D, RMS_EPS,
                  rstd_mode, [("pst", k, 2)], ("pst", k, 4))
        S.op("dve", lambda e: e.tensor_scalar(out=stat[:, c0 + 5:c0 + 6], in0=stat[:, c0 + 4:c0 + 5],
                                              scalar1=float(coef), scalar2=None, op0=ALU.mult),
             reads=[("pst", k, 4)], writes=[("pst", k, 5)])
        w = self.cnt["xw"] % 1
        self.cnt["xw"] += 1
        xw = self.xnew[w]
        for hh, b in ((0, b0), (1, b1)):
            S.op("dve", lambda e, hh=hh, b=b: e.scalar_tensor_tensor(
                out=xw[:, hh * 512:(hh + 1) * 512], in0=self.PS[:, b, :], scalar=stat[:, c0 + 5:c0 + 6],
                in1=gpost[:, hh * 512:(hh + 1) * 512], op0=ALU.mult, op1=ALU.mult),
                reads=[("pst", k, 5), "gpost"], writes=[("ps", b), ("xw", w, hh)])
        S.op("pool", lambda e: e.tensor_tensor(out=xw[:], in0=xw[:], in1=xq[:], op=ALU.add),
             reads=[("xw", w, 0), ("xw", w, 1), ("xq", q)], writes=[("xw", w, 0), ("xw", w, 1)])
        S.op("sp", lambda e: e.dma_start(out=dst[row0 + dst_off:row0 + dst_off + 128, :], in_=xw[:]),
             reads=[("xw", w, 0), ("xw", w, 1)], dma=("xwst", w))

    def ffn_phase(self, layer, which, src, dst, row_lo, row_hi, dst_off=0):
        S = self.S
        srcap, dstap = self.dram(src), self.dram(dst)
        ph = S.phase
        a, off = self.carve(0, 8 * FF)
        Wg = a.rearrange("p (c f) -> p c f", c=8)
        a, off = self.carve(off, 8 * FF)
        Wu = a.rearrange("p (c f) -> p c f", c=8)
        a, off = self.carve(off, NFC * D)
        Wd = a.rearrange("p (c d) -> p c d", c=NFC)
        a, off = self.carve(off, NFC * T)
        actT = a.rearrange("p (c t) -> p c t", c=NFC)
        sg = self.sgt
        wg_d = self.w_gate[layer, which].rearrange("(c p) f -> p c f", p=128)
        wu_d = self.w_up[layer, which].rearrange("(c p) f -> p c f", p=128)
        wd_d = self.w_down[layer, which].rearrange("(c p) d -> p c d", p=128)
        ipre, ipost = (0, 1) if which == 0 else (4, 5)
        self.load_gvec(self.gpre, self.norm_g[layer, ipre:ipre + 1, :], "gpre")
        self.load_gvec(self.gpost, self.norm_g[layer, ipost:ipost + 1, :], "gpost")
        for c in range(8):
            S.op("pool", lambda e, c=c: e.dma_start(out=Wg[:, c, :], in_=wg_d[:, c, :]),
                 writes=[("Wg", c)], dma="wgu")
            S.op("pool", lambda e, c=c: e.dma_start(out=Wu[:, c, :], in_=wu_d[:, c, :]),
                 writes=[("Wu", c)], dma="wgu")
        for c in range(0, NFC, 2):
            S.op("pool", lambda e, c=c: e.dma_start(out=Wd[:, c:c + 2, :], in_=wd_d[:, c:c + 2, :]),
                 writes=[("Wd", c), ("Wd", c + 1)], dma="wd")
        hT = self.hT
        tiles = list(range(row_lo, row_hi, T))
        self.prenorm_tile(srcap, tiles[0], "sqrt")
        for ti, row0 in enumerate(tiles):
            for fc in range(NFC):
                par = fc % 2
                gb, ub = 4 + 2 * par, 5 + 2 * par

                def mm_gu(e, fc=fc, gb=gb, ub=ub):
                    ins = None
                    for c in range(8):
                        ins = e.matmul(self.PS[:, gb, :], lhsT=Wg[:, c, fc * 128:(fc + 1) * 128], rhs=hT[:, c, :],
                                       start=(c == 0), stop=(c == 7))
                    for c in range(8):
                        ins = e.matmul(self.PS[:, ub, :], lhsT=Wu[:, c, fc * 128:(fc + 1) * 128], rhs=hT[:, c, :],
                                       start=(c == 0), stop=(c == 7))
                    return ins
                S.op("pe", mm_gu, reads=[("hT", s) for s in range(4)] + [("Wg", c) for c in range(8)] +
                     [("Wu", c) for c in range(8)], writes=[("ps", gb), ("ps", ub)])
                sgt = sg[par]
                S.op("act", lambda e, gb=gb, sgt=sgt: e.activation(out=sgt[:], in_=self.PS[:, gb, :], func=AF.Silu),
                     writes=[("ps", gb), ("sg", par)])
                S.op("dve", lambda e, ub=ub, sgt=sgt, fc=fc: e.tensor_tensor(
                    out=actT[:, fc, :], in0=self.PS[:, ub, :], in1=sgt[:], op=ALU.mult),
                    reads=[("sg", par)], writes=[("ps", ub), ("actT", fc)])
            if ti + 1 < len(tiles):
                self.prenorm_tile(srcap, tiles[ti + 1], "sqrt")
            for s in range(4):
                yb = (0, 1) if s % 2 == 0 else (2, 3)
                for hh in range(2):
                    def mm_d(e, s=s, hh=hh, b=yb[hh]):
                        ins = None
                        for fc in range(NFC):
                            ins = e.matmul(self.PS[:, b, :], lhsT=actT[:, fc, s * 128:(s + 1) * 128],
                                           rhs=Wd[:, fc, hh * 512:(hh + 1) * 512],
                                           start=(fc == 0), stop=(fc == NFC - 1))
                        return ins
                    S.op("pe", mm_d, reads=[("actT", fc) for fc in range(NFC)] + [("Wd", c) for c in range(NFC)],
                         writes=[("ps", yb[hh])])
                self.post_residual(srcap, dstap, row0 + s * 128, yb, 0.5, "sqrt", dst_off)

    def sg_phase(self, layer, src, dst, row_lo, row_hi, dst_off=0):
        S = self.S
        srcap, dstap = self.dram(src), self.dram(dst)
        E = 2048
        a, off = self.carve(0, 8 * 4096)
        Win = a.rearrange("p (c f) -> p c f", c=8)
        a, off = self.carve(off, 16 * D)
        Wout = a.rearrange("p (c d) -> p c d", c=16)
        zz, off = self.carve(off, 4096, F32)
        lng, off = self.carve(off, E, F32)
        lnb, off = self.carve(off, E, F32)
        vn, off = self.carve(off, E)
        gated, off = self.carve(off, E)
        a, off = self.carve(off, E)
        gT = a.rearrange("p (c t) -> p c t", c=16)
        a, off = self.carve(off, 1024)
        wcT = a.rearrange("p (g t) -> p g t", g=8)
        a, off = self.carve(off, 1024)
        wcb = a.rearrange("p (g t) -> p g t", g=8)
        bhl, off = self.carve(off, 4096)
        bf32 = zz
        stat = self.stat
        hT = self.hT
        win_d = self.sg_w_in[0].rearrange("(c p) f -> p c f", p=128)
        wout_d = self.sg_w_out[0].rearrange("(c p) d -> p c d", p=128)
        self.load_gvec(self.gpre, self.norm_g[layer, 2:3, :], "gpre")
        self.load_gvec(self.gpost, self.norm_g[layer, 3:4, :], "gpost")
        for c in range(8):
            S.op("pool", lambda e, c=c: e.dma_start(out=Win[:, c, :], in_=win_d[:, c, :]),
                 writes=[("Wsi", c)], dma="wgu")
        for c in range(0, 16, 2):
            S.op("pool", lambda e, c=c: e.dma_start(out=Wout[:, c:c + 2, :], in_=wout_d[:, c:c + 2, :]),
                 writes=[("Wso", c), ("Wso", c + 1)], dma="wd")
        S.op("sp", lambda e: e.dma_start(out=lng[:], in_=self.sg_ln_g[0:1, :].partition_broadcast(128)),
             writes=["lng"], dma="cst")
        S.op("sp", lambda e: e.dma_start(out=lnb[:], in_=self.sg_ln_b[0:1, :].partition_broadcast(128)),
             writes=["lnb"], dma="cst")
        S.op("sp", lambda e: e.dma_start(out=bf32[0:1, :], in_=self.sg_b_in[0:1, :]), writes=[("zz", b_) for b_ in range(8)], dma="cst")
        S.op("sp", lambda e: e.dma_start(out=stat[:, 48:56], in_=self.sg_b_s[0].rearrange("g t -> t g"),
                                         allow_slow_non_contiguous=True), writes=["bstok"], dma="cst")
        S.op("dve", lambda e: e.tensor_copy(out=bhl[0:1, :], in_=bf32[0:1, :]), reads=[("zz", b_) for b_ in range(8)], writes=["bhl0"])
        S.op("dve", lambda e: e.tensor_tensor(out=bf32[0:1, :], in0=bf32[0:1, :], in1=bhl[0:1, :], op=ALU.subtract),
             reads=[("zz", b_) for b_ in range(8)] + ["bhl0"], writes=[("zz", b_) for b_ in range(8)])
        S.op("dve", lambda e: e.tensor_copy(out=vn[0:1, :], in_=bf32[0:1, 0:2048]), reads=[("zz", b_) for b_ in range(8)], writes=["vn"])
        S.op("dve", lambda e: e.tensor_copy(out=gated[0:1, :], in_=bf32[0:1, 2048:4096]), reads=[("zz", b_) for b_ in range(8)],
             writes=[("gated", g) for g in range(8)])
        S.op("sp", lambda e: e.dma_start(out=bhl[1:2, 0:2048], in_=vn[0:1, :]), reads=["vn"], writes=["bhl1"], dma="cst")
        S.op("sp", lambda e: e.dma_start(out=bhl[1:2, 2048:4096], in_=gated[0:1, :]),
             reads=[("gated", g) for g in range(8)], writes=["bhl1b"], dma="cst")
        ones2 = self.onesb[0:2, 0:128]
        wtmp = self.xnew[0]
        wtv = wtmp[:].rearrange("p (g s) -> p g s", g=8)
        S.op("sp", lambda e: e.dma_start(out=wtv, in_=self.sg_w_s[0].rearrange("g t s -> t g s")),
             writes=[("xw", 0, 0), ("xw", 0, 1)], dma="cst")
        S.op("pool", lambda e: e.affine_select(out=wtv, in_=wtv, pattern=[[0, 8], [-1, 128]], compare_op=ALU.is_ge,
                                               fill=0.0, base=0, channel_multiplier=1),
             reads=[("xw", 0, 0), ("xw", 0, 1)], writes=[("xw", 0, 0), ("xw", 0, 1)])
        S.op("pool", lambda e: e.tensor_copy(out=wcb[:], in_=wtv), reads=[("xw", 0, 0), ("xw", 0, 1)], writes=["wcb"])
        pTw = self.PS[:, 7, :].bitcast(BF16)

        def trw(e):
            ins = None
            for g in range(8):
                ins = e.transpose(out=pTw[:, g * 128:(g + 1) * 128], in_=wcb[:, g, :], identity=self.ident[:])
            return ins
        S.op("pe", trw, reads=["wcb", "ident"], writes=[("ps", 7)])
        S.op("act", lambda e: e.activation(out=wcT[:], in_=pTw.rearrange("p (g t) -> p g t", g=8), func=AF.Copy),
             writes=[("ps", 7), "wcT"])
        def stage_in(row0, s):
            tok = slice(s * 128, (s + 1) * 128)
            for blk in range(8):
                b = 4 + blk % 2

                def mm_in(e, blk=blk, b=b, tok=tok):
                    ins = None
                    for c in range(8):
                        ins = e.matmul(self.PS[:, b, :], lhsT=hT[:, c, tok], rhs=Win[:, c, blk * 512:(blk + 1) * 512],
                                       start=(c == 0), stop=False)
                    ins = e.matmul(self.PS[:, b, :], lhsT=ones2, rhs=bhl[0:2, blk * 512:(blk + 1) * 512],
                                   start=False, stop=True)
                    return ins
                S.op("pe", mm_in, reads=[("hT", s), "bhl0", "bhl1", "bhl1b", "onesb"] + [("Wsi", c) for c in range(8)],
                     writes=[("ps", b)])
                if blk < 4:
                    S.op("act", lambda e, blk=blk, b=b: e.activation(
                        out=zz[:, blk * 512:(blk + 1) * 512], in_=self.PS[:, b, :], func=AF.Gelu),
                        writes=[("ps", b), ("zz", blk)])
                else:
                    S.op("act", lambda e, blk=blk, b=b: e.activation(
                        out=zz[:, blk * 512:(blk + 1) * 512], in_=self.PS[:, b, :], func=AF.Gelu,
                        accum_out=stat[:, 56 + blk - 4:57 + blk - 4]),
                        writes=[("ps", b), ("zz", blk), ("vs", blk)])

        def stage_mix(row0, s):
            zv = zz[:, E:2 * E]
            S.op("dve", lambda e: e.tensor_reduce(out=stat[:, 60:61], in_=stat[:, 56:60], axis=AX.X, op=ALU.add),
                 reads=[("vs", b_) for b_ in range(4, 8)], writes=["vsum"])
            S.op("dve", lambda e: e.tensor_scalar(out=stat[:, 61:62], in0=stat[:, 60:61], scalar1=-1.0 / E,
                                                  scalar2=None, op0=ALU.mult), reads=["vsum"], writes=["vnm"])
            S.op("act", lambda e: e.activation(out=self.junk[:, :], in_=zv[:, 0:1024], func=AF.Square,
                                               bias=stat[:, 61:62], accum_out=stat[:, 62:63]),
                 reads=["vnm", ("zz", 4), ("zz", 5)], writes=["junk", "vq0"])
            S.op("act", lambda e: e.activation(out=self.junk[:, :], in_=zv[:, 1024:2048], func=AF.Square,
                                               bias=stat[:, 61:62], accum_out=stat[:, 63:64]),
                 reads=["vnm", ("zz", 6), ("zz", 7)], writes=["junk", "vq1"])
            S.op("dve", lambda e: e.tensor_tensor(out=stat[:, 12:13], in0=stat[:, 62:63], in1=stat[:, 63:64], op=ALU.add),
                 reads=["vq0", "vq1"], writes=["vq"])
            self.rstd(stat[:, 12:13], stat[:, 13:14], stat[:, 14:15], 1.0 / E, LN_EPS, "sqrt", ["vq"], "vrs")
            S.op("dve", lambda e: e.tensor_scalar(out=zv, in0=zv, scalar1=stat[:, 61:62], scalar2=stat[:, 14:15],
                                                  op0=ALU.add, op1=ALU.mult),
                 reads=["vnm", "vrs", "vq0", "vq1"] + [("zz", b_) for b_ in range(4, 8)],
                 writes=[("zz", b_) for b_ in range(4, 8)])
            S.op("pool", lambda e: e.tensor_tensor(out=zv, in0=zv, in1=lng[:], op=ALU.mult),
                 reads=["lng"] + [("zz", b_) for b_ in range(4, 8)], writes=[("zz", b_) for b_ in range(4, 8)])
            S.op("pool", lambda e: e.tensor_tensor(out=vn[:], in0=zv, in1=lnb[:], op=ALU.add),
                 reads=["lnb"] + [("zz", b_) for b_ in range(4, 8)], writes=["vn"])
            for g in range(8):
                b = g // 2
                S.op("pe", lambda e, g=g, b=b: e.matmul(
                    self.PS[:, b, (g % 2) * 256:(g % 2) * 256 + 256], lhsT=wcT[:, g, :],
                    rhs=vn[:, g * 256:(g + 1) * 256], start=True, stop=True),
                    reads=["vn", "wcT"], writes=[("ps", b)])
                S.op("dve", lambda e, g=g, b=b: e.scalar_tensor_tensor(
                    out=gated[:, g * 256:(g + 1) * 256], in0=self.PS[:, b, (g % 2) * 256:(g % 2) * 256 + 256],
                    scalar=stat[:, 48 + g:49 + g], in1=zz[:, g * 256:(g + 1) * 256], op0=ALU.add, op1=ALU.mult),
                    reads=["bstok", ("zz", g // 2)], writes=[("ps", b), ("gated", g)])

        def stage_out(row0, s):
            for half in range(2):
                pb = 6 + half
                pT = self.PS[:, pb, :].bitcast(BF16)

                def trg(e, half=half, pT=pT):
                    ins = None
                    for c in range(8):
                        ec = half * 8 + c
                        ins = e.transpose(out=pT[:, c * 128:(c + 1) * 128], in_=gated[:, ec * 128:(ec + 1) * 128],
                                          identity=self.ident[:])
                    return ins
                S.op("pe", trg, reads=[("gated", g) for g in range(8)] + ["ident"], writes=[("ps", pb)])
                S.op("act", lambda e, half=half, pT=pT: e.activation(
                    out=gT[:, half * 8:(half + 1) * 8, :], in_=pT.rearrange("p (c t) -> p c t", c=8), func=AF.Copy),
                    writes=[("ps", pb), ("gT", half)])
            yb = (0, 1) if s % 2 == 0 else (2, 3)
            for hh in range(2):
                def mm_o(e, hh=hh, b=yb[hh]):
                    ins = None
                    for ec in range(16):
                        ins = e.matmul(self.PS[:, b, :], lhsT=gT[:, ec, :], rhs=Wout[:, ec, hh * 512:(hh + 1) * 512],
                                       start=(ec == 0), stop=(ec == 15))
                    return ins
                S.op("pe", mm_o, reads=[("gT", 0), ("gT", 1)] + [("Wso", c) for c in range(16)], writes=[("ps", yb[hh])])
            self.post_residual(srcap, dstap, row0 + s * 128, yb, 1.0, "sqrt", dst_off)

        tiles = list(range(row_lo, row_hi, T))
        subs = [(r, s_) for r in tiles for s_ in range(4)]
        self.prenorm_tile(srcap, tiles[0], "sqrt")
        for i in range(len(subs) + 1):
            if i < len(subs):
                stage_in(*subs[i])
                if subs[i][1] == 3 and subs[i][0] + T < row_hi:
                    self.prenorm_tile(srcap, subs[i][0] + T, "sqrt")
            if i > 0:
                stage_out(*subs[i - 1])
            if i < len(subs):
                stage_mix(*subs[i])

    def dn_phase(self, layer, src, dst, row_lo, row_hi, full_from, dst_off=0):
        import itertools
        S = self.S
        srcap, dstap = self.dram(src), self.dram(dst)
        TD = 256
        PS = self.PS
        hT = self.hT
        a, off = self.carve(0, 8 * 4112)
        Win = a.rearrange("p (c f) -> p c f", c=8)
        a, off = self.carve(off, 8 * D)
        Wout = a.rearrange("p (c d) -> p c d", c=8)
        a, off = self.carve(off, 8 * TD); qT = a.rearrange("p (h t) -> p h t", h=8)
        a, off = self.carve(off, 8 * TD); kT = a.rearrange("p (h t) -> p h t", h=8)
        a, off = self.carve(off, 8 * TD); vT = a.rearrange("p (h t) -> p h t", h=8)
        pc = []
        cv = []
        for i in range(2):
            a, off = self.carve(off, 260, F32); pc.append(a)
        for i in range(2):
            a, off = self.carve(off, TD, F32); cv.append(a)
        sq, off = self.carve(off, TD)
        rinv, off = self.carve(off, TD, F32)
        a, off = self.carve(off, 72, F32); carry = a.rearrange("p (c j) -> p c j", c=24)
        a, off = self.carve(off, 96, F32); cw = a.rearrange("p (c j) -> p c j", c=24)
        blk_off = off
        zs2 = []
        for i in range(2):
            a, off = self.carve(off, 1024, F32); zs2.append(a)
        a, off = self.carve(off, 1024); vb = a.rearrange("p (h d) -> p h d", h=8)
        a, off = self.carve(off, 1024); kbg = a.rearrange("p (h d) -> p h d", h=8)
        a, off = self.carve(off, 1024); kd = a.rearrange("p (h d) -> p h d", h=8)
        dec2, off = self.carve(off, 1024, F32); dec = dec2.rearrange("p (h j) -> p h j", h=8)
        attn2, off = self.carve(off, 1024); attn = attn2.rearrange("p (h j) -> p h j", h=8)
        attnT2, off = self.carve(off, 1024); attnT = attnT2.rearrange("p (h j) -> p h j", h=8)
        A2 = []; B2 = []
        for i in range(2):
            a, off = self.carve(off, 1024); A2.append(a)
        for i in range(2):
            a, off = self.carve(off, 1024); B2.append(a)
        A = [x.rearrange("p (h j) -> p h j", h=8) for x in A2]
        B = [x.rearrange("p (h j) -> p h j", h=8) for x in B2]
        TT2, off = self.carve(off, 1024); TT = TT2.rearrange("p (h j) -> p h j", h=8)
        ubuf2, off = self.carve(off, 1024, F32); ubuf = ubuf2.rearrange("p (h d) -> p h d", h=8)
        wT2, off = self.carve(off, 1024); wT = wT2.rearrange("p (h t) -> p h t", h=8)
        qdT2, off = self.carve(off, 1024); qdT = qdT2.rearrange("p (h t) -> p h t", h=8)
        osb2 = dec2; osb = dec
        St2, off = self.carve(off, 1024, F32); St = St2.rearrange("p (h d) -> p h d", h=8)
        Sb2, off = self.carve(off, 1024); Sb = Sb2.rearrange("p (h d) -> p h d", h=8)
        vnew2, off = self.carve(off, 1024); vnew = vnew2.rearrange("p (h d) -> p h d", h=8)
        gatedb = vnew2
        a, off = self.carve(off, 1024); gatedT = a.rearrange("p (c t) -> p c t", c=8)
        dng, off = self.carve(off, 1024, F32)
        cwj, _ = self.carve(blk_off, 3072, F32)
        dsc = self.sgt[0]
        cst = self.sgt[1]
        Tri = cst[:, 0:128]
        Mc = cst[:, 128:256]
        Bones = cst[:, 256:384]
        Bsel0 = cst[:, 384:512]
        Bsel1 = dsc[:, 128:256]
        onesf, onesb, ident = self.onesf, self.onesb, self.ident
        C_T1, C_XA, C_BETA, C_NB, C_SP, C_G, C_GC, C_EGC, C_EDK, C_GL, C_BEGC, C_DTB, C_NEA, C_OSS = \
            0, 8, 16, 24, 32, 40, 48, 56, 64, 72, 88, 96, 104, 112

        def col(c, n=8, so=0):
            return dsc[:, so + c:so + c + n]
        win_d = self.dn_w_in[0].rearrange("(c p) f -> p c f", p=128)
        wout_d = self.dn_w_out[0].rearrange("(c p) d -> p c d", p=128)
        self.load_gvec(self.gpre, self.norm_g[layer, 2:3, :], "gpre")
        self.load_gvec(self.gpost, self.norm_g[layer, 3:4, :], "gpost")
        for c in range(8):
            S.op("pool", lambda e, c=c: e.dma_start(out=Win[:, c, :], in_=win_d[:, c, :]),
                 writes=[("Wdi", c)], dma="wgu")
        for c in range(0, 8, 2):
            S.op("pool", lambda e, c=c: e.dma_start(out=Wout[:, c:c + 2, :], in_=wout_d[:, c:c + 2, :]),
                 writes=[("Wdo", c), ("Wdo", c + 1)], dma="wd")
        for h in range(8):
            S.op("sp", lambda e, h=h: e.dma_start(out=dng[:, h * 128:(h + 1) * 128],
                                                  in_=self.dn_norm_g[0:1, :].partition_broadcast(128)),
                 writes=[("dng", h)], dma="cst")
        S.op("sp", lambda e: e.dma_start(out=col(C_DTB), in_=self.dn_dt_bias[0:1, :].partition_broadcast(128)),
             writes=["dtb"], dma="cst")
        S.op("sp", lambda e: e.dma_start(out=col(C_NEA), in_=self.dn_a_log[0:1, :].partition_broadcast(128)),
             writes=["nea"], dma="cst")
        S.op("act", lambda e: e.activation(out=col(C_NEA), in_=col(C_NEA), func=AF.Exp), reads=["nea"], writes=["nea"])
        S.op("dve", lambda e: e.tensor_scalar(out=col(C_NEA), in0=col(C_NEA), scalar1=-1.0, scalar2=None, op0=ALU.mult),
             reads=["nea"], writes=["nea"])
        S.op("sp", lambda e: e.dma_start(out=cwj[0:4, :], in_=self.dn_conv_w[0]), writes=["cwj"], dma="cst")
        for cc in range(24):
            S.op("pe", lambda e, cc=cc: e.transpose(out=PS[:, 7, cc * 4:(cc + 1) * 4], in_=cwj[0:4, cc * 128:(cc + 1) * 128],
                                                    identity=self.identf[0:4, 0:4]),
                 reads=["cwj", "identf"], writes=[("ps", 7)])
        S.op("act", lambda e: e.activation(out=cw[:], in_=PS[:, 7, 0:96].rearrange("p (c j) -> p c j", c=24), func=AF.Copy),
             writes=[("ps", 7), "cw"])
        S.op("pool", lambda e: e.memset(Tri, 1.0), writes=["Tri"])
        S.op("pool", lambda e: e.affine_select(out=Tri, in_=Tri, pattern=[[1, 128]], compare_op=ALU.is_ge, fill=0.0,
                                               base=0, channel_multiplier=-1), reads=["Tri"], writes=["Tri"])
        S.op("pool", lambda e: e.memset(cst[0:64, 64:128], 0.0), reads=["Tri"], writes=["Tri"])
        S.op("pool", lambda e: e.memset(Bones, 0.0), writes=["Bones"])
        S.op("pool", lambda e: e.memset(cst[0:64, 256:320], 1.0), reads=["Bones"], writes=["Bones"])
        S.op("pool", lambda e: e.memset(cst[64:128, 320:384], 1.0), reads=["Bones"], writes=["Bones"])
        S.op("pool", lambda e: e.memset(Mc, 3.0e4), writes=["Mc"])
        S.op("pool", lambda e: e.affine_select(out=Mc, in_=Mc, pattern=[[1, 128]], compare_op=ALU.is_ge, fill=0.0,
                                               base=-1, channel_multiplier=-1), reads=["Mc"], writes=["Mc"])
        S.op("pool", lambda e: e.memset(cst[64:128, 128:192], 3.0e4), reads=["Mc"], writes=["Mc"])
        S.op("pool", lambda e: e.memset(Bsel0, 0.0), writes=["Bsel"])
        S.op("pool", lambda e: e.memset(cst[0:64, 384:512], 1.0), reads=["Bsel"], writes=["Bsel"])
        S.op("pool", lambda e: e.memset(Bsel1, 0.0), reads=["Bsel"], writes=["Bsel"])
        S.op("pool", lambda e: e.memset(dsc[64:128, 128:256], 1.0), reads=["Bsel"], writes=["Bsel"])
        S.op("pool", lambda e: e.memset(carry[:], 0.0), writes=["carry"])
        S.op("pool", lambda e: e.memset(St2, 0.0), writes=[("St", 0), ("St", 1)])
        S.op("pool", lambda e: e.memset(Sb2, 0.0), writes=[("Sb", 0), ("Sb", 1)])
        S.barrier()
        QSCALE = 128.0 ** -0.5

        def pb_of(h, base):
            return PS[:, base + h // 4, (h % 4) * 128:(h % 4) * 128 + 128]

        for row0 in range(row_lo, row_hi, TD):
            full = row0 >= full_from
            need_q = full or (row0 + TD) >= full_from
            self.prenorm_tile(srcap, row0, "explog", nsub=2)
            hkeys = [("hT", 0), ("hT", 1)]
            for cc in range(24):
                if cc < 8 and not need_q:
                    continue
                par = cc % 2
                pb = 2 + par

                def mm_qkv(e, cc=cc, pb=pb):
                    ins = None
                    for c in range(8):
                        ins = e.matmul(PS[:, pb, 0:TD], lhsT=Win[:, c, cc * 128:(cc + 1) * 128], rhs=hT[:, c, 0:TD],
                                       start=(c == 0), stop=(c == 7))
                    return ins
                S.op("pe", mm_qkv, reads=hkeys + [("Wdi", c) for c in range(8)], writes=[("ps", pb)])
                pcb, cvb = pc[par], cv[par]
                S.op("pool", lambda e, cc=cc, pcb=pcb: e.tensor_copy(out=pcb[:, 0:3], in_=carry[:, cc, :]),
                     reads=["carry"], writes=[("pc", par, 0)])
                S.op("act", lambda e, pb=pb, pcb=pcb: e.activation(out=pcb[:, 3:3 + TD], in_=PS[:, pb, 0:TD], func=AF.Copy),
                     writes=[("ps", pb), ("pc", par, 1)])
                S.op("pool", lambda e, cc=cc, pcb=pcb: e.tensor_copy(out=carry[:, cc, :], in_=pcb[:, TD:TD + 3]),
                     reads=[("pc", par, 1), ("pc", par, 0)], writes=["carry"])
                pk = [("pc", par, 0), ("pc", par, 1), "cw"]
                S.op("dve", lambda e, cc=cc, pcb=pcb, cvb=cvb: e.tensor_scalar(
                    out=cvb, in0=pcb[:, 0:TD], scalar1=cw[:, cc, 0:1], scalar2=None, op0=ALU.mult),
                    reads=pk, writes=[("cv", par)])
                for j in range(1, 4):
                    S.op("dve", lambda e, cc=cc, pcb=pcb, cvb=cvb, j=j: e.scalar_tensor_tensor(
                        out=cvb, in0=pcb[:, j:j + TD], scalar=cw[:, cc, j:j + 1], in1=cvb, op0=ALU.mult, op1=ALU.add),
                        reads=pk + [("cv", par)], writes=[("cv", par)])
                dstT = qT if cc < 8 else (kT if cc < 16 else vT)
                nm = "qT" if cc < 8 else ("kT" if cc < 16 else "vT")
                S.op("act", lambda e, cvb=cvb, dstT=dstT, cc=cc: e.activation(out=dstT[:, cc % 8, :], in_=cvb, func=AF.Silu),
                     reads=[("cv", par)], writes=[(nm, cc % 8)])
            if full:
                for blk in range(2):
                    for hb in range(2):
                        def mm_z(e, hb=hb, blk=blk):
                            ins = None
                            for c in range(8):
                                ins = e.matmul(PS[:, 4 + hb, :], lhsT=hT[:, c, blk * 128:(blk + 1) * 128],
                                               rhs=Win[:, c, 3072 + hb * 512:3072 + (hb + 1) * 512], start=(c == 0), stop=(c == 7))
                            return ins
                        S.op("pe", mm_z, reads=[("hT", blk)] + [("Wdi", c) for c in range(8)], writes=[("ps", 4 + hb)])
                        zsl = zs2[blk][:, hb * 512:(hb + 1) * 512]
                        S.op("act", lambda e, hb=hb, zsl=zsl: e.activation(out=zsl, in_=PS[:, 4 + hb, :], func=AF.Silu),
                             writes=[("ps", 4 + hb), ("zs", blk, hb)])
                        S.op("pool", lambda e, hb=hb, zsl=zsl: e.tensor_tensor(out=zsl, in0=zsl, in1=dng[:, hb * 512:(hb + 1) * 512],
                                                                               op=ALU.mult),
                             reads=[("zs", blk, hb)] + [("dng", h) for h in range(8)], writes=[("zs", blk, hb)])
            for cc in range(16):
                if cc < 8 and not need_q:
                    continue
                dstT = qT if cc < 8 else kT
                nm = "qT" if cc < 8 else "kT"
                hh_ = cc % 8
                lb = 2 + cc % 2
                S.op("act", lambda e, dstT=dstT, hh_=hh_: e.activation(out=sq, in_=dstT[:, hh_, :], func=AF.Square),
                     reads=[(nm, hh_)], writes=["sq"])
                S.op("pe", lambda e, lb=lb: e.matmul(PS[:, lb, 0:TD], lhsT=onesb[:], rhs=sq, start=True, stop=True),
                     reads=["sq", "onesb"], writes=[("ps", lb)])
                S.op("dve", lambda e, lb=lb: e.tensor_scalar(out=rinv, in0=PS[:, lb, 0:TD], scalar1=1e-6, scalar2=None, op0=ALU.add),
                     reads=["rinv"], writes=[("ps", lb), "rinv"])
                S.op("act", lambda e: e.activation(out=rinv, in_=rinv, func=AF.Ln), reads=["rinv"], writes=["rinv"])
                S.op("act", lambda e: e.activation(out=rinv, in_=rinv, func=AF.Exp, scale=-0.5), reads=["rinv"], writes=["rinv"])
                sc = QSCALE if cc < 8 else 1.0
                S.op("dve", lambda e, dstT=dstT, hh_=hh_, sc=sc: e.scalar_tensor_tensor(
                    out=dstT[:, hh_, :], in0=dstT[:, hh_, :], scalar=sc, in1=rinv, op0=ALU.mult, op1=ALU.mult),
                    reads=[(nm, hh_), "rinv"], writes=[(nm, hh_)])
            if True:
                def pro(tb, so, hk):
                    def mm_ba(e):
                        ins = None
                        for c in range(8):
                            ins = e.matmul(PS[:, 6, 0:16], lhsT=hT[:, c, tb], rhs=Win[:, c, 4096:4112], start=(c == 0), stop=(c == 7))
                        return ins
                    S.op("pe", mm_ba, reads=hk + [("Wdi", c) for c in range(8)], writes=[("ps", 6)])
                    S.op("act", lambda e: e.activation(out=col(C_T1, so=so), in_=PS[:, 6, 0:8], func=AF.Exp, scale=-1.0),
                         writes=[("ps", 6), ("t1", so)])
                    S.op("dve", lambda e: e.tensor_tensor(out=col(C_XA, so=so), in0=PS[:, 6, 8:16], in1=col(C_DTB), op=ALU.add),
                         reads=["dtb"], writes=[("ps", 6), ("xa", so)])
                    S.op("dve", lambda e: e.tensor_scalar(out=col(C_T1, so=so), in0=col(C_T1, so=so), scalar1=1.0, scalar2=None, op0=ALU.add),
                         reads=[("t1", so)], writes=[("t1", so)])
                    S.op("dve", lambda e: e.reciprocal(out=col(C_BETA, so=so), in_=col(C_T1, so=so)), reads=[("t1", so)], writes=[("beta", so)])
                    S.op("dve", lambda e: e.tensor_scalar(out=col(C_NB, so=so), in0=col(C_BETA, so=so), scalar1=-1.0, scalar2=None, op0=ALU.mult),
                         reads=[("beta", so)], writes=[("nb", so)])
                    yield
                    S.op("act", lambda e: e.activation(out=col(C_XA, so=so), in_=col(C_XA, so=so), func=AF.Exp), reads=[("xa", so)], writes=[("xa", so)])
                    S.op("dve", lambda e: e.tensor_scalar(out=col(C_XA, so=so), in0=col(C_XA, so=so), scalar1=1.0, scalar2=None, op0=ALU.add),
                         reads=[("xa", so)], writes=[("xa", so)])
                    S.op("act", lambda e: e.activation(out=col(C_SP, so=so), in_=col(C_XA, so=so), func=AF.Ln), reads=[("xa", so)], writes=[("sp", so)])
                    yield
                    S.op("dve", lambda e: e.tensor_tensor(out=col(C_G, so=so), in0=col(C_SP, so=so), in1=col(C_NEA), op=ALU.mult),
                         reads=[("sp", so), "nea"], writes=[("g", so)])

                    def mm_gc(e):
                        e.matmul(PS[:, 6, 16:24], lhsT=Tri, rhs=col(C_G, so=so), start=True, stop=True)
                        e.matmul(PS[:, 6, 24:32], lhsT=Bones, rhs=col(C_G, so=so), start=True, stop=True)
                        e.matmul(PS[:, 6, 32:40], lhsT=Bsel0, rhs=col(C_G, so=so), start=True, stop=True)
                        return e.matmul(PS[:, 6, 40:48], lhsT=Bsel1, rhs=col(C_G, so=so), start=True, stop=True)
                    S.op("pe", mm_gc, reads=[("g", so), "Tri", "Bones", "Bsel"], writes=[("ps", 6)])
                    S.op("act", lambda e: e.activation(out=col(C_GC, so=so), in_=PS[:, 6, 16:24], func=AF.Copy), writes=[("ps", 6), ("gc", so)])
                    S.op("act", lambda e: e.activation(out=col(C_EGC, so=so), in_=PS[:, 6, 16:24], func=AF.Exp), writes=[("ps", 6), ("egc", so)])
                    S.op("dve", lambda e: e.tensor_tensor(out=col(C_EDK, so=so), in0=PS[:, 6, 24:32], in1=col(C_GC, so=so), op=ALU.subtract),
                         reads=[("gc", so)], writes=[("ps", 6), ("edk", so)])
                    S.op("act", lambda e: e.activation(out=col(C_EDK, so=so), in_=col(C_EDK, so=so), func=AF.Exp), reads=[("edk", so)], writes=[("edk", so)])
                    S.op("act", lambda e: e.activation(out=col(C_GL, 16, so=so), in_=PS[:, 6, 32:48], func=AF.Exp), writes=[("ps", 6), ("gl", so)])
                    S.op("dve", lambda e: e.tensor_tensor(out=col(C_BEGC, so=so), in0=col(C_BETA, so=so), in1=col(C_EGC, so=so), op=ALU.mult),
                         reads=[("beta", so), ("egc", so)], writes=[("begc", so)])

                    yield

                def grp(hg, tb, full, so):
                    H = range(hg * 4, hg * 4 + 4)
                    cs_ = slice(hg * 512, (hg + 1) * 512)
                    b0, b1, b2, bt = hg, 2 + hg, 4 + hg, 6 + hg
                    pT = PS[:, bt, :].bitcast(BF16)[:, 0:512]
                    KH = lambda nm: [(nm, h) for h in H]
                    for h in H:
                        S.op("pool", lambda e, h=h: e.tensor_scalar(out=ubuf[:, h, :], in0=Tri, scalar1=dsc[:, so + C_G + h:so + C_G + h + 1],
                                                                    scalar2=None, op0=ALU.mult),
                             reads=[("g", so), "Tri"], writes=[("ubuf", h)])
                    S.op("pe", lambda e: e.matmul(PS[:, b1, :], lhsT=onesf[:], rhs=ubuf2[:, cs_], start=True, stop=True),
                         reads=KH("ubuf") + ["onesf"], writes=[("ps", b1)])
                    yield
                    if full:
                        S.op("act", lambda e: e.activation(out=dec2[:, cs_], in_=PS[:, b1, :], func=AF.Exp),
                             writes=[("ps", b1), ("dec", hg)])
                        S.op("dve", lambda e: e.tensor_tensor(out=qdT[:, hg * 4:hg * 4 + 4, :], in0=qT[:, hg * 4:hg * 4 + 4, tb],
                                                              in1=dec[:, hg * 4:hg * 4 + 4, :], op=ALU.mult),
                             reads=KH("qT") + [("dec", hg)], writes=[("qdT", hg)])
                        yield
                    for h in H:
                        S.op("dve", lambda e, h=h: e.scalar_tensor_tensor(
                            out=dec[:, h, :], in0=pb_of(h, 2), scalar=dsc[:, so + C_GC + h:so + C_GC + h + 1], in1=Mc,
                            op0=ALU.subtract, op1=ALU.max),
                            reads=[("gc", so), "Mc"], writes=[("ps", b1), ("dec", hg)])
                    S.op("act", lambda e: e.activation(out=dec2[:, cs_], in_=dec2[:, cs_], func=AF.Exp, scale=-1.0),
                         reads=[("dec", hg)], writes=[("dec", hg)])
                    yield
                    def tr_k(e):
                        ins = None
                        for i, h in enumerate(H):
                            ins = e.transpose(out=pT[:, i * 128:(i + 1) * 128], in_=kT[:, h, tb], identity=ident[:])
                        return ins
                    S.op("pe", tr_k, reads=KH("kT") + ["ident"], writes=[("ps", bt)])
                    for i, h in enumerate(H):
                        S.op("act", lambda e, h=h, i=i: e.activation(out=kbg[:, h, :], in_=pT[:, i * 128:(i + 1) * 128], func=AF.Copy,
                                                                     scale=dsc[:, so + C_BEGC + h:so + C_BEGC + h + 1]),
                             reads=[("begc", so)], writes=[("ps", bt), ("kbg", h)])
                        S.op("act", lambda e, h=h, i=i: e.activation(out=kd[:, h, :], in_=pT[:, i * 128:(i + 1) * 128], func=AF.Copy,
                                                                     scale=dsc[:, so + C_EDK + h:so + C_EDK + h + 1]),
                             reads=[("edk", so)], writes=[("ps", bt), ("kd", h)])
                    yield

                    def tr_v(e):
                        ins = None
                        for i, h in enumerate(H):
                            ins = e.transpose(out=pT[:, i * 128:(i + 1) * 128], in_=vT[:, h, tb], identity=ident[:])
                        return ins
                    S.op("pe", tr_v, reads=KH("vT") + ["ident"], writes=[("ps", bt)])
                    for i, h in enumerate(H):
                        S.op("act", lambda e, h=h, i=i: e.activation(out=vb[:, h, :], in_=pT[:, i * 128:(i + 1) * 128], func=AF.Copy,
                                                                     scale=dsc[:, so + C_BETA + h:so + C_BETA + h + 1]),
                             reads=[("beta", so)], writes=[("ps", bt), ("vb", h)])
                    yield
                    def mm_kk(e):
                        ins = None
                        for h in H:
                            ins = e.matmul(pb_of(h, 4), lhsT=kT[:, h, tb], rhs=kT[:, h, tb], start=True, stop=True)
                        return ins
                    S.op("pe", mm_kk, reads=KH("kT"), writes=[("ps", b2)])
                    for h in H:
                        S.op("dve", lambda e, h=h: e.scalar_tensor_tensor(
                            out=A[0][:, h, :], in0=pb_of(h, 4), scalar=dsc[:, so + C_NB + h:so + C_NB + h + 1], in1=dec[:, h, :],
                            op0=ALU.mult, op1=ALU.mult),
                            reads=[("nb", so), ("dec", hg)], writes=[("ps", b2), ("A", 0, hg)])
                    S.op("pool", lambda e: e.affine_select(out=A[0][:, hg * 4:hg * 4 + 4, :], in_=A[0][:, hg * 4:hg * 4 + 4, :],
                                                           pattern=[[0, 4], [-1, 128]], compare_op=ALU.not_equal, fill=0.0,
                                                           base=0, channel_multiplier=1),
                         reads=[("A", 0, hg)], writes=[("A", 0, hg)])
                    yield
                    if full:
                        def mm_qk(e):
                            ins = None
                            for h in H:
                                ins = e.matmul(pb_of(h, 0), lhsT=qT[:, h, tb], rhs=kT[:, h, tb], start=True, stop=True)
                            return ins
                        S.op("pe", mm_qk, reads=KH("kT") + KH("qT"), writes=[("ps", b0)])
                        S.op("dve", lambda e: e.tensor_tensor(out=attn2[:, cs_], in0=PS[:, b0, :], in1=dec2[:, cs_], op=ALU.mult),
                             reads=[("dec", hg)], writes=[("ps", b0), ("attn", hg)])
                        yield

                        def tr_at(e):
                            ins = None
                            for i, h in enumerate(H):
                                ins = e.transpose(out=pT[:, i * 128:(i + 1) * 128], in_=attn[:, h, :], identity=ident[:])
                            return ins
                        S.op("pe", tr_at, reads=[("attn", hg), "ident"], writes=[("ps", bt)])
                        S.op("act", lambda e: e.activation(out=attnT2[:, cs_], in_=pT, func=AF.Copy), writes=[("ps", bt), ("attnT", hg)])
                        yield

                    def tr_a0(e):
                        ins = None
                        for i, h in enumerate(H):
                            ins = e.transpose(out=pT[:, i * 128:(i + 1) * 128], in_=A[0][:, h, :], identity=ident[:])
                        return ins
                    S.op("pe", tr_a0, reads=[("A", 0, hg), "ident"], writes=[("ps", bt)])
                    S.op("act", lambda e: e.activation(out=B2[0][:, cs_], in_=pT, func=AF.Copy), writes=[("ps", bt), ("B", 0, hg)])
                    yield
                    for h in H:
                        S.op("pool", lambda e, h=h: e.tensor_tensor(out=TT[:, h, :], in0=B[0][:, h, :], in1=ident[:], op=ALU.add),
                             reads=[("B", 0, hg), "ident"], writes=[("TT", h)])
                    for k in range(5):
                        cur, nxt = k % 2, 1 - k % 2

                        def mm_a(e, cur=cur):
                            ins = None
                            for h in H:
                                ins = e.matmul(pb_of(h, 0), lhsT=B[cur][:, h, :], rhs=A[cur][:, h, :], start=True, stop=True)
                            return ins
                        S.op("pe", mm_a, reads=[("A", cur, hg), ("B", cur, hg)], writes=[("ps", b0)])
                        S.op("act", lambda e, nxt=nxt: e.activation(out=A2[nxt][:, cs_], in_=PS[:, b0, :], func=AF.Copy),
                             writes=[("ps", b0), ("A", nxt, hg)])
                        if k <= 3:
                            def mm_b(e, cur=cur):
                                ins = None
                                for h in H:
                                    ins = e.matmul(pb_of(h, 2), lhsT=A[cur][:, h, :], rhs=B[cur][:, h, :], start=True, stop=True)
                                return ins
                            S.op("pe", mm_b, reads=[("A", cur, hg), ("B", cur, hg)], writes=[("ps", b1)])
                            S.op("dve", lambda e, nxt=nxt: e.tensor_copy(out=B2[nxt][:, cs_], in_=PS[:, b1, :]),
                                 writes=[("ps", b1), ("B", nxt, hg)])
                        yield

                        def mm_t(e, nxt=nxt):
                            ins = None
                            for h in H:
                                ins = e.matmul(pb_of(h, 4), lhsT=A[nxt][:, h, :], rhs=TT[:, h, :], start=True, stop=True)
                            return ins
                        S.op("pe", mm_t, reads=[("A", nxt, hg)] + KH("TT"), writes=[("ps", b2)])
                        S.op("dve", lambda e: e.tensor_tensor(out=TT2[:, cs_], in0=TT2[:, cs_], in1=PS[:, b2, :], op=ALU.add),
                             reads=KH("TT"), writes=[("ps", b2)] + KH("TT"))
                        yield
                    def mm_u(e):
                        ins = None
                        for h in H:
                            ins = e.matmul(pb_of(h, 0), lhsT=TT[:, h, :], rhs=vb[:, h, :], start=True, stop=True)
                        return ins
                    S.op("pe", mm_u, reads=KH("TT") + KH("vb"), writes=[("ps", b0)])
                    S.op("act", lambda e: e.activation(out=ubuf2[:, cs_], in_=PS[:, b0, :], func=AF.Copy),
                         writes=[("ps", b0)] + KH("ubuf"))

                    def mm_w(e):
                        ins = None
                        for h in H:
                            ins = e.matmul(pb_of(h, 2), lhsT=kbg[:, h, :], rhs=TT[:, h, :], start=True, stop=True)
                        return ins
                    S.op("pe", mm_w, reads=KH("TT") + KH("kbg"), writes=[("ps", b1)])
                    S.op("dve", lambda e: e.tensor_copy(out=wT2[:, cs_], in_=PS[:, b1, :]), writes=[("ps", b1), ("wT", hg)])
                    yield
                    for c in range(2):
                        rs = slice(c * 64, (c + 1) * 64)

                        def mm_ws(e):
                            ins = None
                            for h in H:
                                ins = e.matmul(pb_of(h, 4), lhsT=wT[:, h, :], rhs=Sb[:, h, :], start=True, stop=True)
                            return ins
                        S.op("pe", mm_ws, reads=[("wT", hg), ("Sb", hg)], writes=[("ps", b2)])
                        S.op("dve", lambda e, rs=rs: e.tensor_tensor(out=vnew2[rs, cs_], in0=ubuf2[rs, cs_], in1=PS[rs, b2, :],
                                                                     op=ALU.subtract),
                             reads=KH("ubuf"), writes=[("ps", b2), ("vnew", hg)])
                        yield
                        if full:
                            def mm_o(e, rs=rs):
                                ins = None
                                for h in H:
                                    e.matmul(pb_of(h, 0), lhsT=qdT[:, h, :], rhs=Sb[:, h, :], start=True, stop=False)
                                    ins = e.matmul(pb_of(h, 0), lhsT=attnT[rs, h, :], rhs=vnew[rs, h, :], start=False, stop=True)
                                return ins
                            S.op("pe", mm_o, reads=[("qdT", hg), ("Sb", hg), ("attnT", hg), ("vnew", hg)], writes=[("ps", b0)])
                            S.op("act", lambda e, rs=rs: e.activation(out=osb2[rs, cs_], in_=PS[rs, b0, :], func=AF.Copy),
                                 writes=[("ps", b0), ("osb", c, hg), ("dec", hg)])

                        def mm_s(e, rs=rs):
                            ins = None
                            for h in H:
                                ins = e.matmul(pb_of(h, 2), lhsT=kd[rs, h, :], rhs=vnew[rs, h, :], start=True, stop=True)
                            return ins
                        S.op("pe", mm_s, reads=KH("kd") + [("vnew", hg)], writes=[("ps", b1)])
                        for h in H:
                            S.op("dve", lambda e, h=h, c=c: e.scalar_tensor_tensor(
                                out=St[:, h, :], in0=St[:, h, :], scalar=dsc[:, so + C_GL + c * 8 + h:so + C_GL + c * 8 + h + 1], in1=pb_of(h, 2),
                                op0=ALU.mult, op1=ALU.add),
                                reads=[("gl", so), ("St", hg)], writes=[("ps", b1), ("St", hg)])
                        S.op("act", lambda e: e.activation(out=Sb2[:, cs_], in_=St2[:, cs_], func=AF.Copy),
                             reads=[("St", hg)], writes=[("Sb", hg)])
                        yield


                def epi(blk, brow, so, zs):
                    S.op("act", lambda e: e.activation(out=self.junk[:], in_=osb2, func=AF.Square),
                         reads=[("osb", c_, hb_) for c_ in range(2) for hb_ in range(2)] + [("dec", 0), ("dec", 1)], writes=["junk"])
                    S.op("dve", lambda e: e.tensor_reduce(out=col(C_OSS, so=so), in_=self.junk[:].rearrange("p (h d) -> p h d", h=8),
                                                          axis=AX.X, op=ALU.add), reads=["junk"], writes=[("oss", so)])
                    yield
                    self.rstd(col(C_OSS, so=so), dsc[:, so + 120:so + 128], col(C_OSS, so=so), 1.0 / 128, RMS_EPS, "explog", [("oss", so)], ("orstd", so))
                    for h in range(8):
                        S.op("dve", lambda e, h=h, zs=zs, so=so: e.scalar_tensor_tensor(
                            out=gatedb[:, h * 128:(h + 1) * 128], in0=osb[:, h, :], scalar=dsc[:, so + C_OSS + h:so + C_OSS + h + 1],
                            in1=zs[:, h * 128:(h + 1) * 128], op0=ALU.mult, op1=ALU.mult),
                            reads=[("orstd", so), ("zs", blk, h // 4), ("dec", h // 4)] + [("osb", c_, h // 4) for c_ in range(2)],
                            writes=[("gatedb", h), ("vnew", h // 4)])
                    yield
                    pT4 = PS[:, 4, :].bitcast(BF16)

                    def tr_g(e):
                        ins = None
                        for c_ in range(8):
                            ins = e.transpose(out=pT4[:, c_ * 128:(c_ + 1) * 128], in_=gatedb[:, c_ * 128:(c_ + 1) * 128], identity=ident[:])
                        return ins
                    S.op("pe", tr_g, reads=[("gatedb", h) for h in range(8)] + ["ident", ("vnew", 0), ("vnew", 1)], writes=[("ps", 4)])
                    S.op("act", lambda e: e.activation(out=gatedT[:], in_=pT4.rearrange("p (c t) -> p c t", c=8), func=AF.Copy),
                         writes=[("ps", 4), "gatedT"])
                    yield
                    yb = (6, 7)
                    for hh in range(2):
                        def mm_out(e, hh=hh, b=yb[hh]):
                            ins = None
                            for c_ in range(8):
                                ins = e.matmul(PS[:, b, :], lhsT=gatedT[:, c_, :], rhs=Wout[:, c_, hh * 512:(hh + 1) * 512],
                                               start=(c_ == 0), stop=(c_ == 7))
                            return ins
                        S.op("pe", mm_out, reads=["gatedT"] + [("Wdo", c_) for c_ in range(8)], writes=[("ps", yb[hh])])
                    self.post_residual(srcap, dstap, brow, yb, 1.0, "explog", dst_off)
                    yield


            tbs = [slice(0, 128), slice(128, 256)]
            sos = [0, 256]

            def drain(g):
                if g is not None:
                    for _ in g:
                        pass
            for _ in pro(tbs[0], sos[0], [("hT", 0)]):
                pass
            streams0 = [grp(0, tbs[0], full, sos[0]), grp(1, tbs[0], full, sos[0]), pro(tbs[1], sos[1], [("hT", 1)])]
            for _ in itertools.zip_longest(*streams0):
                pass
            streams1 = [grp(0, tbs[1], full, sos[1]), grp(1, tbs[1], full, sos[1])]
            if full:
                e0 = epi(0, row0, sos[0], zs2[0])
                next(e0)
                next(e0)
                streams1.append(e0)
            for _ in itertools.zip_longest(*streams1):
                pass
            if full:
                drain(epi(1, row0 + 128, sos[1], zs2[1]))

    def dram(self, name):
        return {"x": self.x_in, "xs": self.xs, "y": self.y_out}[name]


_PROG_CACHE = {}


def get_program(npre, nown, layers_key):
    key = (npre, nown, layers_key)
    if key not in _PROG_CACHE:
        _PROG_CACHE[key] = Builder(npre, nown, list(layers_key)).build()
    return _PROG_CACHE[key]


NPRE = 4096
NOWN = 4096


def full_layers(npre, nown):
    nt = npre + nown
    return (
        ("ffn", (0, 0, "x", "xs", 0, nt, 0)),
        ("dn", (0, "xs", "xs", 0, nt, npre, 0)),
        ("ffn", (0, 1, "xs", "xs", npre, nt, 0)),
        ("ffn", (1, 0, "xs", "xs", npre, nt, 0)),
        ("sg", (1, "xs", "xs", npre, nt, 0)),
        ("ffn", (1, 1, "xs", "y", npre, nt, -npre)),
    )


def kernel(**inputs):
    x = np.ascontiguousarray(np.asarray(inputs["x"], dtype=np.float32))
    B, SEQ, _ = x.shape
    half = SEQ // 2
    nc = get_program(half, half, full_layers(half, half))
    wnames = ["norm_g", "ffn_w_gate", "ffn_w_up", "ffn_w_down", "dn_w_in", "dn_conv_w", "dn_a_log", "dn_dt_bias",
              "dn_norm_g", "dn_w_out", "sg_w_in", "sg_b_in", "sg_ln_g", "sg_ln_b", "sg_w_s", "sg_b_s", "sg_w_out"]
    shared = {k: np.ascontiguousarray(np.asarray(inputs[k], dtype=np.float32)) for k in wnames}
    in_maps = []
    for c in range(2 * B):
        b, hf = c // 2, c % 2
        own = x[b, hf * half:(hf + 1) * half]
        pre = x[b, 0:half] if hf == 1 else np.zeros_like(own)
        m = dict(shared)
        m["x"] = np.ascontiguousarray(np.concatenate([pre, own], axis=0))
        in_maps.append(m)
    res = run_bass_kernel_spmd(nc, in_maps, core_ids=list(range(2 * B)))
    out = np.empty_like(x)
    for c in range(2 * B):
        b, hf = c // 2, c % 2
        out[b, hf * half:(hf + 1) * half] = res.results[c]["y"]
    return out
```

```python
import contextlib
import os
DN_STOP = int(os.environ.get('DN_STOP', '99'))
import numpy as np
import concourse.bass as bass
import concourse.mybir as mybir
from concourse.bass_utils import run_bass_kernel_spmd

F32 = mybir.dt.float32
BF16 = mybir.dt.bfloat16
AF = mybir.ActivationFunctionType
ALU = mybir.AluOpType
AX = mybir.AxisListType

D = 1024
FF = 2816
NFC = FF // 128
T = 512
RMS_EPS = 1e-6
LN_EPS = 1e-5
ENGS = ("pe", "act", "dve", "pool", "sp")


class Op:
    __slots__ = ("eng", "fn", "reads", "writes", "dma", "deps", "sig", "sem", "val", "idx", "bar")

    def __init__(self, eng, fn, reads, writes, dma):
        self.eng = eng
        self.fn = fn
        self.reads = reads
        self.writes = writes
        self.dma = dma
        self.deps = []
        self.sig = False
        self.sem = None
        self.val = 0
        self.bar = 0


class Sched:
    def __init__(self, nc):
        self.nc = nc
        self.ops = []
        self.phase = 0
        self.nbar = 0

    def barrier(self):
        self.nbar += 1
        self.phase += 1

    def op(self, eng, fn, reads=(), writes=(), dma=None):
        if dma == "cst":
            writes = tuple(writes) + ("cstall",)
        o = Op(eng, fn, tuple(reads), tuple(writes), dma)
        o.idx = len(self.ops)
        o.bar = self.nbar
        o.sem = ("dma", dma) if dma is not None else ("eng", eng)
        self.ops.append(o)
        return o

    def emit(self):
        nc = self.nc
        ops = self.ops
        last_w = {}
        readers = {}
        last_eng = {}
        last_dma = {}
        seen_bar = {e: 0 for e in ENGS}
        bar_snap = None
        cur_bar = 0
        for o in ops:
            if o.bar != cur_bar:
                cur_bar = o.bar
                bar_snap = (dict(last_eng), dict(last_dma))
            deps = set()
            for r in o.reads:
                w = last_w.get(r)
                if w is not None:
                    deps.add(w)
            for wk in o.writes:
                w = last_w.get(wk)
                if w is not None:
                    deps.add(w)
                for rd in readers.get(wk, ()):
                    deps.add(rd)
            for r in o.reads:
                readers.setdefault(r, []).append(o.idx)
            for wk in o.writes:
                last_w[wk] = o.idx
                readers[wk] = []
            deps.discard(o.idx)
            real = []
            for d in deps:
                p = ops[d]
                if p.dma is None and o.dma is None and p.eng == o.eng:
                    if o.eng == "pe":
                        continue
                    if not any(r in p.writes for r in o.reads):
                        continue
                real.append(d)
            if seen_bar[o.eng] != o.bar:
                seen_bar[o.eng] = o.bar
                for e, i in bar_snap[0].items():
                    if e != o.eng:
                        real.append(i)
                for k, i in bar_snap[1].items():
                    real.append(i)
            for d in real:
                ops[d].sig = True
            o.deps = real
            if o.dma is None:
                last_eng[o.eng] = o.idx
            else:
                last_dma[o.dma] = o.idx
        counts = {}
        for o in ops:
            if o.dma is not None:
                counts[o.sem] = counts.get(o.sem, 0) + 16
                o.val = counts[o.sem]
            elif o.sig:
                counts[o.sem] = counts.get(o.sem, 0) + 1
                o.val = counts[o.sem]
        semkeys = sorted(counts.keys(), key=str)
        self.n_sems = len(semkeys)
        self.max_val = max(counts.values()) if counts else 0
        per_eng = {e: [] for e in ENGS}
        for o in ops:
            per_eng[o.eng].append(o)
        with contextlib.ExitStack() as st:
            sems = {k: st.enter_context(nc.semaphore("s%d" % i)) for i, k in enumerate(semkeys)}
            block = st.enter_context(nc.Block())

            def run(engname, eng):
                known = {}
                for o in per_eng[engname]:
                    need = {}
                    for d in o.deps:
                        p = ops[d]
                        if known.get(p.sem, 0) >= p.val:
                            continue
                        if need.get(p.sem, 0) < p.val:
                            need[p.sem] = p.val
                    for s, v in need.items():
                        eng.wait_ge(sems[s], v)
                        known[s] = v
                    ins = o.fn(eng)
                    if o.dma is not None:
                        ins.then_inc(sems[o.sem], 16)
                    elif o.sig:
                        ins.then_inc(sems[o.sem], 1)
                fin = {}
                for o in per_eng[engname]:
                    if o.dma is not None:
                        fin[o.sem] = max(fin.get(o.sem, 0), o.val)
                for s, v in fin.items():
                    if known.get(s, 0) < v:
                        eng.wait_ge(sems[s], v)

            if per_eng["sp"]:
                @block.sync
                def _(e):
                    run("sp", e)
            if per_eng["pe"]:
                @block.tensor
                def _(e):
                    run("pe", e)
            if per_eng["act"]:
                @block.scalar
                def _(e):
                    run("act", e)
            if per_eng["dve"]:
                @block.vector
                def _(e):
                    run("dve", e)
            if per_eng["pool"]:
                @block.gpsimd
                def _(e):
                    run("pool", e)


class Builder:
    def __init__(self, npre, nown, layers):
        self.npre = npre
        self.nown = nown
        self.ntot = npre + nown
        self.layers = layers
        self.nc = bass.Bass("TRN2", target_bir_lowering=False)
        self.S = Sched(self.nc)
        self.st = contextlib.ExitStack()
        self.uid = 0

    def sb(self, name, shape, dt):
        return self.st.enter_context(self.nc.sbuf_tensor(name, shape, dt))

    def dram_in(self, name, shape):
        return self.nc.dram_tensor(name, list(shape), F32, kind="ExternalInput").ap()

    def build(self):
        nc, S = self.nc, self.S
        with self.st:
            self.declare_io()
            self.alloc()
            self.consts()
            li = 0
            for (kind, arg) in self.layers:
                S.barrier()
                if kind == "ffn":
                    self.ffn_phase(*arg)
                elif kind == "sg":
                    self.sg_phase(*arg)
                elif kind == "dn":
                    self.dn_phase(*arg)
                li += 1
            S.emit()
        return nc

    def declare_io(self):
        nc = self.nc
        self.x_in = self.dram_in("x", (self.ntot, D))
        self.norm_g = self.dram_in("norm_g", (2, 6, D))
        self.w_gate = self.dram_in("ffn_w_gate", (2, 2, D, FF))
        self.w_up = self.dram_in("ffn_w_up", (2, 2, D, FF))
        self.w_down = self.dram_in("ffn_w_down", (2, 2, FF, D))
        self.dn_w_in = self.dram_in("dn_w_in", (1, D, 4112))
        self.dn_conv_w = self.dram_in("dn_conv_w", (1, 4, 3072))
        self.dn_a_log = self.dram_in("dn_a_log", (1, 8))
        self.dn_dt_bias = self.dram_in("dn_dt_bias", (1, 8))
        self.dn_norm_g = self.dram_in("dn_norm_g", (1, 128))
        self.dn_w_out = self.dram_in("dn_w_out", (1, D, D))
        self.sg_w_in = self.dram_in("sg_w_in", (1, D, 4096))
        self.sg_b_in = self.dram_in("sg_b_in", (1, 4096))
        self.sg_ln_g = self.dram_in("sg_ln_g", (1, 2048))
        self.sg_ln_b = self.dram_in("sg_ln_b", (1, 2048))
        self.sg_w_s = self.dram_in("sg_w_s", (1, 8, 128, 128))
        self.sg_b_s = self.dram_in("sg_b_s", (1, 8, 128))
        self.sg_w_out = self.dram_in("sg_w_out", (1, 2048, D))
        self.y_out = nc.dram_tensor("y", [self.nown, D], F32, kind="ExternalOutput").ap()
        self.xs = nc.dram_tensor("xs_scratch", [self.ntot, D], F32, kind="Internal").ap()

    def alloc(self):
        nc = self.nc
        self.ARENA_N = 79872
        self.ARENA = self.sb("ARENA", [128, self.ARENA_N], BF16)
        self.xring = [self.sb("xr%d" % i, [128, D], F32) for i in range(4)]
        self.xres = [self.sb("xq%d" % i, [128, D], F32) for i in range(1)]
        self.xnew = [self.sb("xw%d" % i, [128, D], F32) for i in range(1)]
        self.xn = [self.sb("xn%d" % i, [128, D], BF16) for i in range(2)]
        self.junk = self.sb("junk", [128, D], BF16)
        self.hT = self.sb("hT", [128, 8, T], BF16)
        self.gpre = self.sb("gpre", [128, D], F32)
        self.gpost = self.sb("gpost", [128, D], F32)
        self.sgt = [self.sb("sgt%d" % i, [128, T], F32) for i in range(2)]
        self.ident = self.sb("ident", [128, 128], BF16)
        self.identf = self.sb("identf", [128, 128], F32)
        self.stat = self.sb("stat", [128, 64], F32)
        self.onesb = self.sb("onesb", [128, 128], BF16)
        self.onesf = self.sb("onesf", [128, 128], F32)
        self.PS = self.st.enter_context(nc.psum_tensor("PS", [128, 8, 512], F32))
        self.cnt = {"xr": 0, "xq": 0, "xw": 0, "xn": 0, "st": 0}

    def consts(self):
        S = self.S
        ident, identf = self.ident, self.identf
        S.op("pool", lambda e: e.memset(identf[:], 0.0), writes=["identf"])
        S.op("pool", lambda e: e.affine_select(out=identf[:], in_=identf[:], pattern=[[-1, 128]],
                                               compare_op=ALU.not_equal, fill=1.0, base=0,
                                               channel_multiplier=1),
             reads=["identf"], writes=["identf"])
        S.op("pool", lambda e: e.tensor_copy(out=ident[:], in_=identf[:]), reads=["identf"], writes=["ident"])
        S.op("pool", lambda e: e.memset(self.onesb[:], 1.0), writes=["onesb"])
        S.op("pool", lambda e: e.memset(self.onesf[:], 1.0), writes=["onesf"])

    def carve(self, off, n, dt=BF16):
        if dt == BF16:
            assert off + n <= self.ARENA_N
            return self.ARENA[:, off:off + n], off + n
        assert off % 2 == 0 and off + 2 * n <= self.ARENA_N
        return self.ARENA[:, off:off + 2 * n].bitcast(F32), off + 2 * n

    def bank(self, b):
        return self.PS[:, b, :]

    def load_gvec(self, dst, src_row, key):
        self.S.op("sp", lambda e: e.dma_start(out=dst[:], in_=src_row.partition_broadcast(128)),
                  writes=[key], dma=key)

    def prenorm_tile(self, src, row0, rstd_mode, nsub=4):
        S = self.S
        stat, junk, hT, gpre = self.stat, self.junk, self.hT, self.gpre
        xs_ = []
        for s in range(nsub):
            i = self.cnt["xr"] % 4
            self.cnt["xr"] += 1
            xt = self.xring[i]
            r0 = row0 + s * 128
            S.op("sp", lambda e, xt=xt, r0=r0: e.dma_start(out=xt[:], in_=src[r0:r0 + 128, :]),
                 writes=[("xr", i)], dma=("xr", i))
            S.op("act", lambda e, xt=xt, s=s: e.activation(out=junk[:], in_=xt[:], func=AF.Square,
                                                           accum_out=stat[:, s:s + 1]),
                 reads=[("xr", i)], writes=["junk", ("stat", s)])
            xs_.append((xt, i))
        self.rstd(stat[:, 0:nsub], stat[:, 4:4 + nsub], stat[:, 8:8 + nsub], 1.0 / D, RMS_EPS, rstd_mode,
                  [("stat", s) for s in range(nsub)], "rs_pre")
        for s in range(nsub):
            xt, i = xs_[s]
            j = self.cnt["xn"] % 2
            self.cnt["xn"] += 1
            xn = self.xn[j]
            S.op("dve", lambda e, xt=xt, xn=xn, s=s: e.scalar_tensor_tensor(
                out=xn[:], in0=xt[:], scalar=stat[:, 8 + s:9 + s], in1=gpre[:], op0=ALU.mult, op1=ALU.mult),
                reads=[("xr", i), "rs_pre", "gpre"], writes=[("xn", j)])
            pb = s % 4
            pT = self.PS[:, pb, :].bitcast(BF16)

            def tr(e, xn=xn, pT=pT):
                ins = None
                for c in range(8):
                    ins = e.transpose(out=pT[:, c * 128:(c + 1) * 128], in_=xn[:, c * 128:(c + 1) * 128],
                                      identity=self.ident[:])
                return ins
            S.op("pe", tr, reads=[("xn", j), "ident"], writes=[("ps", pb)])
            S.op("act", lambda e, pT=pT, s=s: e.activation(
                out=hT[:, :, s * 128:(s + 1) * 128], in_=pT.rearrange("p (c t) -> p c t", c=8), func=AF.Copy),
                writes=[("ps", pb), ("hT", s)])

    def rstd(self, ss, tmp, out, scale, eps, mode, rkeys, wkey):
        S = self.S
        if mode == "sqrt":
            S.op("dve", lambda e: e.tensor_scalar(out=tmp, in0=ss, scalar1=scale, scalar2=eps,
                                                  op0=ALU.mult, op1=ALU.add),
                 reads=rkeys, writes=[(wkey, "t")])
            S.op("act", lambda e: e.activation(out=tmp, in_=tmp, func=AF.Sqrt),
                 reads=[(wkey, "t")], writes=[(wkey, "t")])
            S.op("dve", lambda e: e.reciprocal(out=out, in_=tmp), reads=[(wkey, "t")], writes=[wkey])
        else:
            S.op("dve", lambda e: e.tensor_scalar(out=tmp, in0=ss, scalar1=scale, scalar2=eps,
                                                  op0=ALU.mult, op1=ALU.add),
                 reads=rkeys, writes=[(wkey, "t")])
            S.op("act", lambda e: e.activation(out=tmp, in_=tmp, func=AF.Ln),
                 reads=[(wkey, "t")], writes=[(wkey, "t")])
            S.op("act", lambda e: e.activation(out=out, in_=tmp, func=AF.Exp, scale=-0.5),
                 reads=[(wkey, "t")], writes=[wkey])

    def post_residual(self, src, dst, row0, ybanks, coef, rstd_mode, dst_off=0):
        S = self.S
        stat, junk, gpost = self.stat, self.junk, self.gpost
        b0, b1 = ybanks
        q = self.cnt["xq"] % 1
        self.cnt["xq"] += 1
        xq = self.xres[q]
        S.op("sp", lambda e: e.dma_start(out=xq[:], in_=src[row0:row0 + 128, :]),
             writes=[("xq", q)], dma=("xq", q))
        k = self.cnt["st"] % 4
        self.cnt["st"] += 1
        c0 = 16 + k * 8
        S.op("act", lambda e: e.activation(out=junk[:, 0:512], in_=self.PS[:, b0, :], func=AF.Square,
                                           accum_out=stat[:, c0:c0 + 1]),
             writes=[("ps", b0), "junk", ("pst", k, 0)])
        S.op("act", lambda e: e.activation(out=junk[:, 512:1024], in_=self.PS[:, b1, :], func=AF.Square,
                                           accum_out=stat[:, c0 + 1:c0 + 2]),
             writes=[("ps", b1), "junk", ("pst", k, 1)])
        S.op("dve", lambda e: e.tensor_tensor(out=stat[:, c0 + 2:c0 + 3], in0=stat[:, c0:c0 + 1],
                                              in1=stat[:, c0 + 1:c0 + 2], op=ALU.add),
             reads=[("pst", k, 0), ("pst", k, 1)], writes=[("pst", k, 2)])
        self.rstd(stat[:, c0 + 2:c0 + 3], stat[:, c0 + 3:c0 + 4], stat[:, c0 + 4:c0 + 5], 1.0 / D, RMS_EPS,
                  rstd_mode, [("pst", k, 2)], ("pst", k, 4))
        S.op("dve", lambda e: e.tensor_scalar(out=stat[:, c0 + 5:c0 + 6], in0=stat[:, c0 + 4:c0 + 5],
                                              scalar1=float(coef), scalar2=None, op0=ALU.mult),
             reads=[("pst", k, 4)], writes=[("pst", k, 5)])
        w = self.cnt["xw"] % 1
        self.cnt["xw"] += 1
        xw = self.xnew[w]
        for hh, b in ((0, b0), (1, b1)):
            S.op("dve", lambda e, hh=hh, b=b: e.scalar_tensor_tensor(
                out=xw[:, hh * 512:(hh + 1) * 512], in0=self.PS[:, b, :], scalar=stat[:, c0 + 5:c0 + 6],
                in1=gpost[:, hh * 512:(hh + 1) * 512], op0=ALU.mult, op1=ALU.mult),
                reads=[("pst", k, 5), "gpost"], writes=[("ps", b), ("xw", w, hh)])
        S.op("pool", lambda e: e.tensor_tensor(out=xw[:], in0=xw[:], in1=xq[:], op=ALU.add),
             reads=[("xw", w, 0), ("xw", w, 1), ("xq", q)], writes=[("xw", w, 0), ("xw", w, 1)])
        S.op("sp", lambda e: e.dma_start(out=dst[row0 + dst_off:row0 + dst_off + 128, :], in_=xw[:]),
             reads=[("xw", w, 0), ("xw", w, 1)], dma=("xwst", w))

    def ffn_phase(self, layer, which, src, dst, row_lo, row_hi, dst_off=0):
        S = self.S
        srcap, dstap = self.dram(src), self.dram(dst)
        ph = S.phase
        a, off = self.carve(0, 8 * FF)
        Wg = a.rearrange("p (c f) -> p c f", c=8)
        a, off = self.carve(off, 8 * FF)
        Wu = a.rearrange("p (c f) -> p c f", c=8)
        a, off = self.carve(off, NFC * D)
        Wd = a.rearrange("p (c d) -> p c d", c=NFC)
        a, off = self.carve(off, NFC * T)
        actT = a.rearrange("p (c t) -> p c t", c=NFC)
        sg = self.sgt
        wg_d = self.w_gate[layer, which].rearrange("(c p) f -> p c f", p=128)
        wu_d = self.w_up[layer, which].rearrange("(c p) f -> p c f", p=128)
        wd_d = self.w_down[layer, which].rearrange("(c p) d -> p c d", p=128)
        ipre, ipost = (0, 1) if which == 0 else (4, 5)
        self.load_gvec(self.gpre, self.norm_g[layer, ipre:ipre + 1, :], "gpre")
        self.load_gvec(self.gpost, self.norm_g[layer, ipost:ipost + 1, :], "gpost")
        for c in range(8):
            S.op("pool", lambda e, c=c: e.dma_start(out=Wg[:, c, :], in_=wg_d[:, c, :]),
                 writes=[("Wg", c)], dma="wgu")
            S.op("pool", lambda e, c=c: e.dma_start(out=Wu[:, c, :], in_=wu_d[:, c, :]),
                 writes=[("Wu", c)], dma="wgu")
        for c in range(0, NFC, 2):
            S.op("pool", lambda e, c=c: e.dma_start(out=Wd[:, c:c + 2, :], in_=wd_d[:, c:c + 2, :]),
                 writes=[("Wd", c), ("Wd", c + 1)], dma="wd")
        hT = self.hT
        tiles = list(range(row_lo, row_hi, T))
        self.prenorm_tile(srcap, tiles[0], "sqrt")
        for ti, row0 in enumerate(tiles):
            for fc in range(NFC):
                par = fc % 2
                gb, ub = 4 + 2 * par, 5 + 2 * par

                def mm_gu(e, fc=fc, gb=gb, ub=ub):
                    ins = None
                    for c in range(8):
                        ins = e.matmul(self.PS[:, gb, :], lhsT=Wg[:, c, fc * 128:(fc + 1) * 128], rhs=hT[:, c, :],
                                       start=(c == 0), stop=(c == 7))
                    for c in range(8):
                        ins = e.matmul(self.PS[:, ub, :], lhsT=Wu[:, c, fc * 128:(fc + 1) * 128], rhs=hT[:, c, :],
                                       start=(c == 0), stop=(c == 7))
                    return ins
                S.op("pe", mm_gu, reads=[("hT", s) for s in range(4)] + [("Wg", c) for c in range(8)] +
                     [("Wu", c) for c in range(8)], writes=[("ps", gb), ("ps", ub)])
                sgt = sg[par]
                S.op("act", lambda e, gb=gb, sgt=sgt: e.activation(out=sgt[:], in_=self.PS[:, gb, :], func=AF.Silu),
                     writes=[("ps", gb), ("sg", par)])
                S.op("dve", lambda e, ub=ub, sgt=sgt, fc=fc: e.tensor_tensor(
                    out=actT[:, fc, :], in0=self.PS[:, ub, :], in1=sgt[:], op=ALU.mult),
                    reads=[("sg", par)], writes=[("ps", ub), ("actT", fc)])
            if ti + 1 < len(tiles):
                self.prenorm_tile(srcap, tiles[ti + 1], "sqrt")
            for s in range(4):
                yb = (0, 1) if s % 2 == 0 else (2, 3)
                for hh in range(2):
                    def mm_d(e, s=s, hh=hh, b=yb[hh]):
                        ins = None
                        for fc in range(NFC):
                            ins = e.matmul(self.PS[:, b, :], lhsT=actT[:, fc, s * 128:(s + 1) * 128],
                                           rhs=Wd[:, fc, hh * 512:(hh + 1) * 512],
                                           start=(fc == 0), stop=(fc == NFC - 1))
                        return ins
                    S.op("pe", mm_d, reads=[("actT", fc) for fc in range(NFC)] + [("Wd", c) for c in range(NFC)],
                         writes=[("ps", yb[hh])])
                self.post_residual(srcap, dstap, row0 + s * 128, yb, 0.5, "sqrt", dst_off)

    def sg_phase(self, layer, src, dst, row_lo, row_hi, dst_off=0):
        S = self.S
        srcap, dstap = self.dram(src), self.dram(dst)
        E = 2048
        a, off = self.carve(0, 8 * 4096)
        Win = a.rearrange("p (c f) -> p c f", c=8)
        a, off = self.carve(off, 16 * D)
        Wout = a.rearrange("p (c d) -> p c d", c=16)
        zz, off = self.carve(off, 4096, F32)
        lng, off = self.carve(off, E, F32)
        lnb, off = self.carve(off, E, F32)
        vn, off = self.carve(off, E)
        gated, off = self.carve(off, E)
        a, off = self.carve(off, E)
        gT = a.rearrange("p (c t) -> p c t", c=16)
        a, off = self.carve(off, 1024)
        wcT = a.rearrange("p (g t) -> p g t", g=8)
        a, off = self.carve(off, 1024)
        wcb = a.rearrange("p (g t) -> p g t", g=8)
        bhl, off = self.carve(off, 4096)
        bf32 = zz
        stat = self.stat
        hT = self.hT
        win_d = self.sg_w_in[0].rearrange("(c p) f -> p c f", p=128)
        wout_d = self.sg_w_out[0].rearrange("(c p) d -> p c d", p=128)
        self.load_gvec(self.gpre, self.norm_g[layer, 2:3, :], "gpre")
        self.load_gvec(self.gpost, self.norm_g[layer, 3:4, :], "gpost")
        for c in range(8):
            S.op("pool", lambda e, c=c: e.dma_start(out=Win[:, c, :], in_=win_d[:, c, :]),
                 writes=[("Wsi", c)], dma="wgu")
        for c in range(0, 16, 2):
            S.op("pool", lambda e, c=c: e.dma_start(out=Wout[:, c:c + 2, :], in_=wout_d[:, c:c + 2, :]),
                 writes=[("Wso", c), ("Wso", c + 1)], dma="wd")
        S.op("sp", lambda e: e.dma_start(out=lng[:], in_=self.sg_ln_g[0:1, :].partition_broadcast(128)),
             writes=["lng"], dma="cst")
        S.op("sp", lambda e: e.dma_start(out=lnb[:], in_=self.sg_ln_b[0:1, :].partition_broadcast(128)),
             writes=["lnb"], dma="cst")
        S.op("sp", lambda e: e.dma_start(out=bf32[0:1, :], in_=self.sg_b_in[0:1, :]), writes=[("zz", b_) for b_ in range(8)], dma="cst")
        S.op("sp", lambda e: e.dma_start(out=stat[:, 48:56], in_=self.sg_b_s[0].rearrange("g t -> t g"),
                                         allow_slow_non_contiguous=True), writes=["bstok"], dma="cst")
        S.op("dve", lambda e: e.tensor_copy(out=bhl[0:1, :], in_=bf32[0:1, :]), reads=[("zz", b_) for b_ in range(8)], writes=["bhl0"])
        S.op("dve", lambda e: e.tensor_tensor(out=bf32[0:1, :], in0=bf32[0:1, :], in1=bhl[0:1, :], op=ALU.subtract),
             reads=[("zz", b_) for b_ in range(8)] + ["bhl0"], writes=[("zz", b_) for b_ in range(8)])
        S.op("dve", lambda e: e.tensor_copy(out=vn[0:1, :], in_=bf32[0:1, 0:2048]), reads=[("zz", b_) for b_ in range(8)], writes=["vn"])
        S.op("dve", lambda e: e.tensor_copy(out=gated[0:1, :], in_=bf32[0:1, 2048:4096]), reads=[("zz", b_) for b_ in range(8)],
             writes=[("gated", g) for g in range(8)])
        S.op("sp", lambda e: e.dma_start(out=bhl[1:2, 0:2048], in_=vn[0:1, :]), reads=["vn"], writes=["bhl1"], dma="cst")
        S.op("sp", lambda e: e.dma_start(out=bhl[1:2, 2048:4096], in_=gated[0:1, :]),
             reads=[("gated", g) for g in range(8)], writes=["bhl1b"], dma="cst")
        ones2 = self.onesb[0:2, 0:128]
        wtmp = self.xnew[0]
        wtv = wtmp[:].rearrange("p (g s) -> p g s", g=8)
        S.op("sp", lambda e: e.dma_start(out=wtv, in_=self.sg_w_s[0].rearrange("g t s -> t g s")),
             writes=[("xw", 0, 0), ("xw", 0, 1)], dma="cst")
        S.op("pool", lambda e: e.affine_select(out=wtv, in_=wtv, pattern=[[0, 8], [-1, 128]], compare_op=ALU.is_ge,
                                               fill=0.0, base=0, channel_multiplier=1),
             reads=[("xw", 0, 0), ("xw", 0, 1)], writes=[("xw", 0, 0), ("xw", 0, 1)])
        S.op("pool", lambda e: e.tensor_copy(out=wcb[:], in_=wtv), reads=[("xw", 0, 0), ("xw", 0, 1)], writes=["wcb"])
        pTw = self.PS[:, 7, :].bitcast(BF16)

        def trw(e):
            ins = None
            for g in range(8):
                ins = e.transpose(out=pTw[:, g * 128:(g + 1) * 128], in_=wcb[:, g, :], identity=self.ident[:])
            return ins
        S.op("pe", trw, reads=["wcb", "ident"], writes=[("ps", 7)])
        S.op("act", lambda e: e.activation(out=wcT[:], in_=pTw.rearrange("p (g t) -> p g t", g=8), func=AF.Copy),
             writes=[("ps", 7), "wcT"])
        def stage_in(row0, s):
            tok = slice(s * 128, (s + 1) * 128)
            for blk in range(8):
                b = 4 + blk % 2

                def mm_in(e, blk=blk, b=b, tok=tok):
                    ins = None
                    for c in range(8):
                        ins = e.matmul(self.PS[:, b, :], lhsT=hT[:, c, tok], rhs=Win[:, c, blk * 512:(blk + 1) * 512],
                                       start=(c == 0), stop=False)
                    ins = e.matmul(self.PS[:, b, :], lhsT=ones2, rhs=bhl[0:2, blk * 512:(blk + 1) * 512],
                                   start=False, stop=True)
                    return ins
                S.op("pe", mm_in, reads=[("hT", s), "bhl0", "bhl1", "bhl1b", "onesb"] + [("Wsi", c) for c in range(8)],
                     writes=[("ps", b)])
                if blk < 4:
                    S.op("act", lambda e, blk=blk, b=b: e.activation(
                        out=zz[:, blk * 512:(blk + 1) * 512], in_=self.PS[:, b, :], func=AF.Gelu),
                        writes=[("ps", b), ("zz", blk)])
                else:
                    S.op("act", lambda e, blk=blk, b=b: e.activation(
                        out=zz[:, blk * 512:(blk + 1) * 512], in_=self.PS[:, b, :], func=AF.Gelu,
                        accum_out=stat[:, 56 + blk - 4:57 + blk - 4]),
                        writes=[("ps", b), ("zz", blk), ("vs", blk)])

        def stage_mix(row0, s):
            zv = zz[:, E:2 * E]
            S.op("dve", lambda e: e.tensor_reduce(out=stat[:, 60:61], in_=stat[:, 56:60], axis=AX.X, op=ALU.add),
                 reads=[("vs", b_) for b_ in range(4, 8)], writes=["vsum"])
            S.op("dve", lambda e: e.tensor_scalar(out=stat[:, 61:62], in0=stat[:, 60:61], scalar1=-1.0 / E,
                                                  scalar2=None, op0=ALU.mult), reads=["vsum"], writes=["vnm"])
            S.op("act", lambda e: e.activation(out=self.junk[:, :], in_=zv[:, 0:1024], func=AF.Square,
                                               bias=stat[:, 61:62], accum_out=stat[:, 62:63]),
                 reads=["vnm", ("zz", 4), ("zz", 5)], writes=["junk", "vq0"])
            S.op("act", lambda e: e.activation(out=self.junk[:, :], in_=zv[:, 1024:2048], func=AF.Square,
                                               bias=stat[:, 61:62], accum_out=stat[:, 63:64]),
                 reads=["vnm", ("zz", 6), ("zz", 7)], writes=["junk", "vq1"])
            S.op("dve", lambda e: e.tensor_tensor(out=stat[:, 12:13], in0=stat[:, 62:63], in1=stat[:, 63:64], op=ALU.add),
                 reads=["vq0", "vq1"], writes=["vq"])
            self.rstd(stat[:, 12:13], stat[:, 13:14], stat[:, 14:15], 1.0 / E, LN_EPS, "sqrt", ["vq"], "vrs")
            S.op("dve", lambda e: e.tensor_scalar(out=zv, in0=zv, scalar1=stat[:, 61:62], scalar2=stat[:, 14:15],
                                                  op0=ALU.add, op1=ALU.mult),
                 reads=["vnm", "vrs", "vq0", "vq1"] + [("zz", b_) for b_ in range(4, 8)],
                 writes=[("zz", b_) for b_ in range(4, 8)])
            S.op("pool", lambda e: e.tensor_tensor(out=zv, in0=zv, in1=lng[:], op=ALU.mult),
                 reads=["lng"] + [("zz", b_) for b_ in range(4, 8)], writes=[("zz", b_) for b_ in range(4, 8)])
            S.op("pool", lambda e: e.tensor_tensor(out=vn[:], in0=zv, in1=lnb[:], op=ALU.add),
                 reads=["lnb"] + [("zz", b_) for b_ in range(4, 8)], writes=["vn"])
            for g in range(8):
                b = g // 2
                S.op("pe", lambda e, g=g, b=b: e.matmul(
                    self.PS[:, b, (g % 2) * 256:(g % 2) * 256 + 256], lhsT=wcT[:, g, :],
                    rhs=vn[:, g * 256:(g + 1) * 256], start=True, stop=True),
                    reads=["vn", "wcT"], writes=[("ps", b)])
                S.op("dve", lambda e, g=g, b=b: e.scalar_tensor_tensor(
                    out=gated[:, g * 256:(g + 1) * 256], in0=self.PS[:, b, (g % 2) * 256:(g % 2) * 256 + 256],
                    scalar=stat[:, 48 + g:49 + g], in1=zz[:, g * 256:(g + 1) * 256], op0=ALU.add, op1=ALU.mult),
                    reads=["bstok", ("zz", g // 2)], writes=[("ps", b), ("gated", g)])

        def stage_out(row0, s):
            for half in range(2):
                pb = 6 + half
                pT = self.PS[:, pb, :].bitcast(BF16)

                def trg(e, half=half, pT=pT):
                    ins = None
                    for c in range(8):
                        ec = half * 8 + c
                        ins = e.transpose(out=pT[:, c * 128:(c + 1) * 128], in_=gated[:, ec * 128:(ec + 1) * 128],
                                          identity=self.ident[:])
                    return ins
                S.op("pe", trg, reads=[("gated", g) for g in range(8)] + ["ident"], writes=[("ps", pb)])
                S.op("act", lambda e, half=half, pT=pT: e.activation(
                    out=gT[:, half * 8:(half + 1) * 8, :], in_=pT.rearrange("p (c t) -> p c t", c=8), func=AF.Copy),
                    writes=[("ps", pb), ("gT", half)])
            yb = (0, 1) if s % 2 == 0 else (2, 3)
            for hh in range(2):
                def mm_o(e, hh=hh, b=yb[hh]):
                    ins = None
                    for ec in range(16):
                        ins = e.matmul(self.PS[:, b, :], lhsT=gT[:, ec, :], rhs=Wout[:, ec, hh * 512:(hh + 1) * 512],
                                       start=(ec == 0), stop=(ec == 15))
                    return ins
                S.op("pe", mm_o, reads=[("gT", 0), ("gT", 1)] + [("Wso", c) for c in range(16)], writes=[("ps", yb[hh])])
            self.post_residual(srcap, dstap, row0 + s * 128, yb, 1.0, "sqrt", dst_off)

        tiles = list(range(row_lo, row_hi, T))
        subs = [(r, s_) for r in tiles for s_ in range(4)]
        self.prenorm_tile(srcap, tiles[0], "sqrt")
        for i in range(len(subs) + 1):
            if i < len(subs):
                stage_in(*subs[i])
                if subs[i][1] == 3 and subs[i][0] + T < row_hi:
                    self.prenorm_tile(srcap, subs[i][0] + T, "sqrt")
            if i > 0:
                stage_out(*subs[i - 1])
            if i < len(subs):
                stage_mix(*subs[i])

    def dn_phase(self, layer, src, dst, row_lo, row_hi, full_from, dst_off=0):
        import itertools
        S = self.S
        srcap, dstap = self.dram(src), self.dram(dst)
        TD = 256
        PS = self.PS
        hT = self.hT
        a, off = self.carve(0, 8 * 4112)
        Win = a.rearrange("p (c f) -> p c f", c=8)
        a, off = self.carve(off, 8 * D)
        Wout = a.rearrange("p (c d) -> p c d", c=8)
        a, off = self.carve(off, 8 * TD); qT = a.rearrange("p (h t) -> p h t", h=8)
        a, off = self.carve(off, 8 * TD); kT = a.rearrange("p (h t) -> p h t", h=8)
        a, off = self.carve(off, 8 * TD); vT = a.rearrange("p (h t) -> p h t", h=8)
        pc = []
        cv = []
        for i in range(2):
            a, off = self.carve(off, 260, F32); pc.append(a)
        for i in range(2):
            a, off = self.carve(off, TD, F32); cv.append(a)
        sq, off = self.carve(off, TD)
        rinv, off = self.carve(off, TD, F32)
        a, off = self.carve(off, 72, F32); carry = a.rearrange("p (c j) -> p c j", c=24)
        a, off = self.carve(off, 96, F32); cw = a.rearrange("p (c j) -> p c j", c=24)
        blk_off = off
        zs2 = []
        for i in range(2):
            a, off = self.carve(off, 1024, F32); zs2.append(a)
        a, off = self.carve(off, 1024); vb = a.rearrange("p (h d) -> p h d", h=8)
        a, off = self.carve(off, 1024); kbg = a.rearrange("p (h d) -> p h d", h=8)
        a, off = self.carve(off, 1024); kd = a.rearrange("p (h d) -> p h d", h=8)
        dec2, off = self.carve(off, 1024, F32); dec = dec2.rearrange("p (h j) -> p h j", h=8)
        attn2, off = self.carve(off, 1024); attn = attn2.rearrange("p (h j) -> p h j", h=8)
        attnT2, off = self.carve(off, 1024); attnT = attnT2.rearrange("p (h j) -> p h j", h=8)
        A2 = []; B2 = []
        for i in range(2):
            a, off = self.carve(off, 1024); A2.append(a)
        for i in range(2):
            a, off = self.carve(off, 1024); B2.append(a)
        A = [x.rearrange("p (h j) -> p h j", h=8) for x in A2]
        B = [x.rearrange("p (h j) -> p h j", h=8) for x in B2]
        TT2, off = self.carve(off, 1024); TT = TT2.rearrange("p (h j) -> p h j", h=8)
        ubuf2, off = self.carve(off, 1024, F32); ubuf = ubuf2.rearrange("p (h d) -> p h d", h=8)
        wT2, off = self.carve(off, 1024); wT = wT2.rearrange("p (h t) -> p h t", h=8)
        qdT2, off = self.carve(off, 1024); qdT = qdT2.rearrange("p (h t) -> p h t", h=8)
        osb2 = dec2; osb = dec
        St2, off = self.carve(off, 1024, F32); St = St2.rearrange("p (h d) -> p h d", h=8)
        Sb2, off = self.carve(off, 1024); Sb = Sb2.rearrange("p (h d) -> p h d", h=8)
        vnew2, off = self.carve(off, 1024); vnew = vnew2.rearrange("p (h d) -> p h d", h=8)
        gatedb = vnew2
        a, off = self.carve(off, 1024); gatedT = a.rearrange("p (c t) -> p c t", c=8)
        dng, off = self.carve(off, 1024, F32)
        cwj, _ = self.carve(blk_off, 3072, F32)
        dsc = self.sgt[0]
        cst = self.sgt[1]
        Tri = cst[:, 0:128]
        Mc = cst[:, 128:256]
        Bones = cst[:, 256:384]
        Bsel0 = cst[:, 384:512]
        Bsel1 = dsc[:, 128:256]
        onesf, onesb, ident = self.onesf, self.onesb, self.ident
        C_T1, C_XA, C_BETA, C_NB, C_SP, C_G, C_GC, C_EGC, C_EDK, C_GL, C_BEGC, C_DTB, C_NEA, C_OSS = \
            0, 8, 16, 24, 32, 40, 48, 56, 64, 72, 88, 96, 104, 112

        def col(c, n=8, so=0):
            return dsc[:, so + c:so + c + n]
        win_d = self.dn_w_in[0].rearrange("(c p) f -> p c f", p=128)
        wout_d = self.dn_w_out[0].rearrange("(c p) d -> p c d", p=128)
        self.load_gvec(self.gpre, self.norm_g[layer, 2:3, :], "gpre")
        self.load_gvec(self.gpost, self.norm_g[layer, 3:4, :], "gpost")
        for c in range(8):
            S.op("pool", lambda e, c=c: e.dma_start(out=Win[:, c, :], in_=win_d[:, c, :]),
                 writes=[("Wdi", c)], dma="wgu")
        for c in range(0, 8, 2):
            S.op("pool", lambda e, c=c: e.dma_start(out=Wout[:, c:c + 2, :], in_=wout_d[:, c:c + 2, :]),
                 writes=[("Wdo", c), ("Wdo", c + 1)], dma="wd")
        for h in range(8):
            S.op("sp", lambda e, h=h: e.dma_start(out=dng[:, h * 128:(h + 1) * 128],
                                                  in_=self.dn_norm_g[0:1, :].partition_broadcast(128)),
                 writes=[("dng", h)], dma="cst")
        S.op("sp", lambda e: e.dma_start(out=col(C_DTB), in_=self.dn_dt_bias[0:1, :].partition_broadcast(128)),
             writes=["dtb"], dma="cst")
        S.op("sp", lambda e: e.dma_start(out=col(C_NEA), in_=self.dn_a_log[0:1, :].partition_broadcast(128)),
             writes=["nea"], dma="cst")
        S.op("act", lambda e: e.activation(out=col(C_NEA), in_=col(C_NEA), func=AF.Exp), reads=["nea"], writes=["nea"])
        S.op("dve", lambda e: e.tensor_scalar(out=col(C_NEA), in0=col(C_NEA), scalar1=-1.0, scalar2=None, op0=ALU.mult),
             reads=["nea"], writes=["nea"])
        S.op("sp", lambda e: e.dma_start(out=cwj[0:4, :], in_=self.dn_conv_w[0]), writes=["cwj"], dma="cst")
        for cc in range(24):
            S.op("pe", lambda e, cc=cc: e.transpose(out=PS[:, 7, cc * 4:(cc + 1) * 4], in_=cwj[0:4, cc * 128:(cc + 1) * 128],
                                                    identity=self.identf[0:4, 0:4]),
                 reads=["cwj", "identf"], writes=[("ps", 7)])
        S.op("act", lambda e: e.activation(out=cw[:], in_=PS[:, 7, 0:96].rearrange("p (c j) -> p c j", c=24), func=AF.Copy),
             writes=[("ps", 7), "cw"])
        S.op("pool", lambda e: e.memset(Tri, 1.0), writes=["Tri"])
        S.op("pool", lambda e: e.affine_select(out=Tri, in_=Tri, pattern=[[1, 128]], compare_op=ALU.is_ge, fill=0.0,
                                               base=0, channel_multiplier=-1), reads=["Tri"], writes=["Tri"])
        S.op("pool", lambda e: e.memset(cst[0:64, 64:128], 0.0), reads=["Tri"], writes=["Tri"])
        S.op("pool", lambda e: e.memset(Bones, 0.0), writes=["Bones"])
        S.op("pool", lambda e: e.memset(cst[0:64, 256:320], 1.0), reads=["Bones"], writes=["Bones"])
        S.op("pool", lambda e: e.memset(cst[64:128, 320:384], 1.0), reads=["Bones"], writes=["Bones"])
        S.op("pool", lambda e: e.memset(Mc, 3.0e4), writes=["Mc"])
        S.op("pool", lambda e: e.affine_select(out=Mc, in_=Mc, pattern=[[1, 128]], compare_op=ALU.is_ge, fill=0.0,
                                               base=-1, channel_multiplier=-1), reads=["Mc"], writes=["Mc"])
        S.op("pool", lambda e: e.memset(cst[64:128, 128:192], 3.0e4), reads=["Mc"], writes=["Mc"])
        S.op("pool", lambda e: e.memset(Bsel0, 0.0), writes=["Bsel"])
        S.op("pool", lambda e: e.memset(cst[0:64, 384:512], 1.0), reads=["Bsel"], writes=["Bsel"])
        S.op("pool", lambda e: e.memset(Bsel1, 0.0), reads=["Bsel"], writes=["Bsel"])
        S.op("pool", lambda e: e.memset(dsc[64:128, 128:256], 1.0), reads=["Bsel"], writes=["Bsel"])
        S.op("pool", lambda e: e.memset(carry[:], 0.0), writes=["carry"])
        S.op("pool", lambda e: e.memset(St2, 0.0), writes=[("St", 0), ("St", 1)])
        S.op("pool", lambda e: e.memset(Sb2, 0.0), writes=[("Sb", 0), ("Sb", 1)])
        S.barrier()
        QSCALE = 128.0 ** -0.5

        def pb_of(h, base):
            return PS[:, base + h // 4, (h % 4) * 128:(h % 4) * 128 + 128]

        for row0 in range(row_lo, row_hi, TD):
            full = row0 >= full_from
            need_q = full or (row0 + TD) >= full_from
            self.prenorm_tile(srcap, row0, "explog", nsub=2)
            hkeys = [("hT", 0), ("hT", 1)]
            for cc in range(24):
                if cc < 8 and not need_q:
                    continue
                par = cc % 2
                pb = 2 + par

                def mm_qkv(e, cc=cc, pb=pb):
                    ins = None
                    for c in range(8):
                        ins = e.matmul(PS[:, pb, 0:TD], lhsT=Win[:, c, cc * 128:(cc + 1) * 128], rhs=hT[:, c, 0:TD],
                                       start=(c == 0), stop=(c == 7))
                    return ins
                S.op("pe", mm_qkv, reads=hkeys + [("Wdi", c) for c in range(8)], writes=[("ps", pb)])
                pcb, cvb = pc[par], cv[par]
                S.op("pool", lambda e, cc=cc, pcb=pcb: e.tensor_copy(out=pcb[:, 0:3], in_=carry[:, cc, :]),
                     reads=["carry"], writes=[("pc", par, 0)])
                S.op("act", lambda e, pb=pb, pcb=pcb: e.activation(out=pcb[:, 3:3 + TD], in_=PS[:, pb, 0:TD], func=AF.Copy),
                     writes=[("ps", pb), ("pc", par, 1)])
                S.op("pool", lambda e, cc=cc, pcb=pcb: e.tensor_copy(out=carry[:, cc, :], in_=pcb[:, TD:TD + 3]),
                     reads=[("pc", par, 1), ("pc", par, 0)], writes=["carry"])
                pk = [("pc", par, 0), ("pc", par, 1), "cw"]
                S.op("dve", lambda e, cc=cc, pcb=pcb, cvb=cvb: e.tensor_scalar(
                    out=cvb, in0=pcb[:, 0:TD], scalar1=cw[:, cc, 0:1], scalar2=None, op0=ALU.mult),
                    reads=pk, writes=[("cv", par)])
                for j in range(1, 4):
                    S.op("dve", lambda e, cc=cc, pcb=pcb, cvb=cvb, j=j: e.scalar_tensor_tensor(
                        out=cvb, in0=pcb[:, j:j + TD], scalar=cw[:, cc, j:j + 1], in1=cvb, op0=ALU.mult, op1=ALU.add),
                        reads=pk + [("cv", par)], writes=[("cv", par)])
                dstT = qT if cc < 8 else (kT if cc < 16 else vT)
                nm = "qT" if cc < 8 else ("kT" if cc < 16 else "vT")
                S.op("act", lambda e, cvb=cvb, dstT=dstT, cc=cc: e.activation(out=dstT[:, cc % 8, :], in_=cvb, func=AF.Silu),
                     reads=[("cv", par)], writes=[(nm, cc % 8)])
            if full:
                for blk in range(2):
                    for hb in range(2):
                        def mm_z(e, hb=hb, blk=blk):
                            ins = None
                            for c in range(8):
                                ins = e.matmul(PS[:, 4 + hb, :], lhsT=hT[:, c, blk * 128:(blk + 1) * 128],
                                               rhs=Win[:, c, 3072 + hb * 512:3072 + (hb + 1) * 512], start=(c == 0), stop=(c == 7))
                            return ins
                        S.op("pe", mm_z, reads=[("hT", blk)] + [("Wdi", c) for c in range(8)], writes=[("ps", 4 + hb)])
                        zsl = zs2[blk][:, hb * 512:(hb + 1) * 512]
                        S.op("act", lambda e, hb=hb, zsl=zsl: e.activation(out=zsl, in_=PS[:, 4 + hb, :], func=AF.Silu),
                             writes=[("ps", 4 + hb), ("zs", blk, hb)])
                        S.op("pool", lambda e, hb=hb, zsl=zsl: e.tensor_tensor(out=zsl, in0=zsl, in1=dng[:, hb * 512:(hb + 1) * 512],
                                                                               op=ALU.mult),
                             reads=[("zs", blk, hb)] + [("dng", h) for h in range(8)], writes=[("zs", blk, hb)])
            for cc in range(16):
                if cc < 8 and not need_q:
                    continue
                dstT = qT if cc < 8 else kT
                nm = "qT" if cc < 8 else "kT"
                hh_ = cc % 8
                lb = 2 + cc % 2
                S.op("act", lambda e, dstT=dstT, hh_=hh_: e.activation(out=sq, in_=dstT[:, hh_, :], func=AF.Square),
                     reads=[(nm, hh_)], writes=["sq"])
                S.op("pe", lambda e, lb=lb: e.matmul(PS[:, lb, 0:TD], lhsT=onesb[:], rhs=sq, start=True, stop=True),
                     reads=["sq", "onesb"], writes=[("ps", lb)])
                S.op("dve", lambda e, lb=lb: e.tensor_scalar(out=rinv, in0=PS[:, lb, 0:TD], scalar1=1e-6, scalar2=None, op0=ALU.add),
                     reads=["rinv"], writes=[("ps", lb), "rinv"])
                S.op("act", lambda e: e.activation(out=rinv, in_=rinv, func=AF.Ln), reads=["rinv"], writes=["rinv"])
                S.op("act", lambda e: e.activation(out=rinv, in_=rinv, func=AF.Exp, scale=-0.5), reads=["rinv"], writes=["rinv"])
                sc = QSCALE if cc < 8 else 1.0
                S.op("dve", lambda e, dstT=dstT, hh_=hh_, sc=sc: e.scalar_tensor_tensor(
                    out=dstT[:, hh_, :], in0=dstT[:, hh_, :], scalar=sc, in1=rinv, op0=ALU.mult, op1=ALU.mult),
                    reads=[(nm, hh_), "rinv"], writes=[(nm, hh_)])
            if True:
                def pro(tb, so, hk):
                    def mm_ba(e):
                        ins = None
                        for c in range(8):
                            ins = e.matmul(PS[:, 6, 0:16], lhsT=hT[:, c, tb], rhs=Win[:, c, 4096:4112], start=(c == 0), stop=(c == 7))
                        return ins
                    S.op("pe", mm_ba, reads=hk + [("Wdi", c) for c in range(8)], writes=[("ps", 6)])
                    S.op("act", lambda e: e.activation(out=col(C_T1, so=so), in_=PS[:, 6, 0:8], func=AF.Exp, scale=-1.0),
                         writes=[("ps", 6), ("t1", so)])
                    S.op("dve", lambda e: e.tensor_tensor(out=col(C_XA, so=so), in0=PS[:, 6, 8:16], in1=col(C_DTB), op=ALU.add),
                         reads=["dtb"], writes=[("ps", 6), ("xa", so)])
                    S.op("dve", lambda e: e.tensor_scalar(out=col(C_T1, so=so), in0=col(C_T1, so=so), scalar1=1.0, scalar2=None, op0=ALU.add),
                         reads=[("t1", so)], writes=[("t1", so)])
                    S.op("dve", lambda e: e.reciprocal(out=col(C_BETA, so=so), in_=col(C_T1, so=so)), reads=[("t1", so)], writes=[("beta", so)])
                    S.op("dve", lambda e: e.tensor_scalar(out=col(C_NB, so=so), in0=col(C_BETA, so=so), scalar1=-1.0, scalar2=None, op0=ALU.mult),
                         reads=[("beta", so)], writes=[("nb", so)])
                    yield
                    S.op("act", lambda e: e.activation(out=col(C_XA, so=so), in_=col(C_XA, so=so), func=AF.Exp), reads=[("xa", so)], writes=[("xa", so)])
                    S.op("dve", lambda e: e.tensor_scalar(out=col(C_XA, so=so), in0=col(C_XA, so=so), scalar1=1.0, scalar2=None, op0=ALU.add),
                         reads=[("xa", so)], writes=[("xa", so)])
                    S.op("act", lambda e: e.activation(out=col(C_SP, so=so), in_=col(C_XA, so=so), func=AF.Ln), reads=[("xa", so)], writes=[("sp", so)])
                    yield
                    S.op("dve", lambda e: e.tensor_tensor(out=col(C_G, so=so), in0=col(C_SP, so=so), in1=col(C_NEA), op=ALU.mult),
                         reads=[("sp", so), "nea"], writes=[("g", so)])

                    def mm_gc(e):
                        e.matmul(PS[:, 6, 16:24], lhsT=Tri, rhs=col(C_G, so=so), start=True, stop=True)
                        e.matmul(PS[:, 6, 24:32], lhsT=Bones, rhs=col(C_G, so=so), start=True, stop=True)
                        e.matmul(PS[:, 6, 32:40], lhsT=Bsel0, rhs=col(C_G, so=so), start=True, stop=True)
                        return e.matmul(PS[:, 6, 40:48], lhsT=Bsel1, rhs=col(C_G, so=so), start=True, stop=True)
                    S.op("pe", mm_gc, reads=[("g", so), "Tri", "Bones", "Bsel"], writes=[("ps", 6)])
                    S.op("act", lambda e: e.activation(out=col(C_GC, so=so), in_=PS[:, 6, 16:24], func=AF.Copy), writes=[("ps", 6), ("gc", so)])
                    S.op("act", lambda e: e.activation(out=col(C_EGC, so=so), in_=PS[:, 6, 16:24], func=AF.Exp), writes=[("ps", 6), ("egc", so)])
                    S.op("dve", lambda e: e.tensor_tensor(out=col(C_EDK, so=so), in0=PS[:, 6, 24:32], in1=col(C_GC, so=so), op=ALU.subtract),
                         reads=[("gc", so)], writes=[("ps", 6), ("edk", so)])
                    S.op("act", lambda e: e.activation(out=col(C_EDK, so=so), in_=col(C_EDK, so=so), func=AF.Exp), reads=[("edk", so)], writes=[("edk", so)])
                    S.op("act", lambda e: e.activation(out=col(C_GL, 16, so=so), in_=PS[:, 6, 32:48], func=AF.Exp), writes=[("ps", 6), ("gl", so)])
                    S.op("dve", lambda e: e.tensor_tensor(out=col(C_BEGC, so=so), in0=col(C_BETA, so=so), in1=col(C_EGC, so=so), op=ALU.mult),
                         reads=[("beta", so), ("egc", so)], writes=[("begc", so)])

                    yield

                def grp(hg, tb, full, so):
                    H = range(hg * 4, hg * 4 + 4)
                    cs_ = slice(hg * 512, (hg + 1) * 512)
                    b0, b1, b2, bt = hg, 2 + hg, 4 + hg, 6 + hg
                    pT = PS[:, bt, :].bitcast(BF16)[:, 0:512]
                    KH = lambda nm: [(nm, h) for h in H]
                    for h in H:
                        S.op("pool", lambda e, h=h: e.tensor_scalar(out=ubuf[:, h, :], in0=Tri, scalar1=dsc[:, so + C_G + h:so + C_G + h + 1],
                                                                    scalar2=None, op0=ALU.mult),
                             reads=[("g", so), "Tri"], writes=[("ubuf", h)])
                    S.op("pe", lambda e: e.matmul(PS[:, b1, :], lhsT=onesf[:], rhs=ubuf2[:, cs_], start=True, stop=True),
                         reads=KH("ubuf") + ["onesf"], writes=[("ps", b1)])
                    yield
                    if full:
                        S.op("act", lambda e: e.activation(out=dec2[:, cs_], in_=PS[:, b1, :], func=AF.Exp),
                             writes=[("ps", b1), ("dec", hg)])
                        S.op("dve", lambda e: e.tensor_tensor(out=qdT[:, hg * 4:hg * 4 + 4, :], in0=qT[:, hg * 4:hg * 4 + 4, tb],
                                                              in1=dec[:, hg * 4:hg * 4 + 4, :], op=ALU.mult),
                             reads=KH("qT") + [("dec", hg)], writes=[("qdT", hg)])
                        yield
                    for h in H:
                        S.op("dve", lambda e, h=h: e.scalar_tensor_tensor(
                            out=dec[:, h, :], in0=pb_of(h, 2), scalar=dsc[:, so + C_GC + h:so + C_GC + h + 1], in1=Mc,
                            op0=ALU.subtract, op1=ALU.max),
                            reads=[("gc", so), "Mc"], writes=[("ps", b1), ("dec", hg)])
                    S.op("act", lambda e: e.activation(out=dec2[:, cs_], in_=dec2[:, cs_], func=AF.Exp, scale=-1.0),
                         reads=[("dec", hg)], writes=[("dec", hg)])
                    yield
                    def side():
                        def tr_k(e):
                            ins = None
                            for i, h in enumerate(H):
                                ins = e.transpose(out=pT[:, i * 128:(i + 1) * 128], in_=kT[:, h, tb], identity=ident[:])
                            return ins
                        S.op("pe", tr_k, reads=KH("kT") + ["ident"], writes=[("ps", bt)])
                        for i, h in enumerate(H):
                            S.op("act", lambda e, h=h, i=i: e.activation(out=kbg[:, h, :], in_=pT[:, i * 128:(i + 1) * 128], func=AF.Copy,
                                                                         scale=dsc[:, so + C_BEGC + h:so + C_BEGC + h + 1]),
                                 reads=[("begc", so)], writes=[("ps", bt), ("kbg", h)])
                            S.op("act", lambda e, h=h, i=i: e.activation(out=kd[:, h, :], in_=pT[:, i * 128:(i + 1) * 128], func=AF.Copy,
                                                                         scale=dsc[:, so + C_EDK + h:so + C_EDK + h + 1]),
                                 reads=[("edk", so)], writes=[("ps", bt), ("kd", h)])
                        yield

                        def tr_v(e):
                            ins = None
                            for i, h in enumerate(H):
                                ins = e.transpose(out=pT[:, i * 128:(i + 1) * 128], in_=vT[:, h, tb], identity=ident[:])
                            return ins
                        S.op("pe", tr_v, reads=KH("vT") + ["ident"], writes=[("ps", bt)])
                        for i, h in enumerate(H):
                            S.op("act", lambda e, h=h, i=i: e.activation(out=vb[:, h, :], in_=pT[:, i * 128:(i + 1) * 128], func=AF.Copy,
                                                                         scale=dsc[:, so + C_BETA + h:so + C_BETA + h + 1]),
                                 reads=[("beta", so)], writes=[("ps", bt), ("vb", h)])
                        yield
                        if full:
                            def mm_qk(e):
                                ins = None
                                for h in H:
                                    ins = e.matmul(pb_of(h, 0), lhsT=qT[:, h, tb], rhs=kT[:, h, tb], start=True, stop=True)
                                return ins
                            S.op("pe", mm_qk, reads=KH("kT") + KH("qT"), writes=[("ps", b0)])
                            S.op("dve", lambda e: e.tensor_tensor(out=attn2[:, cs_], in0=PS[:, b0, :], in1=dec2[:, cs_], op=ALU.mult),
                                 reads=[("dec", hg)], writes=[("ps", b0), ("attn", hg)])
                            yield

                            def tr_at(e):
                                ins = None
                                for i, h in enumerate(H):
                                    ins = e.transpose(out=pT[:, i * 128:(i + 1) * 128], in_=attn[:, h, :], identity=ident[:])
                                return ins
                            S.op("pe", tr_at, reads=[("attn", hg), "ident"], writes=[("ps", bt)])
                            S.op("act", lambda e: e.activation(out=attnT2[:, cs_], in_=pT, func=AF.Copy), writes=[("ps", bt), ("attnT", hg)])
                            yield

                        yield
                    sd = side()
                    def mm_kk(e):
                        ins = None
                        for h in H:
                            ins = e.matmul(pb_of(h, 4), lhsT=kT[:, h, tb], rhs=kT[:, h, tb], start=True, stop=True)
                        return ins
                    S.op("pe", mm_kk, reads=KH("kT"), writes=[("ps", b2)])
                    for h in H:
                        S.op("dve", lambda e, h=h: e.scalar_tensor_tensor(
                            out=A[0][:, h, :], in0=pb_of(h, 4), scalar=dsc[:, so + C_NB + h:so + C_NB + h + 1], in1=dec[:, h, :],
                            op0=ALU.mult, op1=ALU.mult),
                            reads=[("nb", so), ("dec", hg)], writes=[("ps", b2), ("A", 0, hg)])
                    S.op("pool", lambda e: e.affine_select(out=A[0][:, hg * 4:hg * 4 + 4, :], in_=A[0][:, hg * 4:hg * 4 + 4, :],
                                                           pattern=[[0, 4], [-1, 128]], compare_op=ALU.not_equal, fill=0.0,
                                                           base=0, channel_multiplier=1),
                         reads=[("A", 0, hg)], writes=[("A", 0, hg)])
                    yield
                    def tr_a0(e):
                        ins = None
                        for i, h in enumerate(H):
                            ins = e.transpose(out=pT[:, i * 128:(i + 1) * 128], in_=A[0][:, h, :], identity=ident[:])
                        return ins
                    S.op("pe", tr_a0, reads=[("A", 0, hg), "ident"], writes=[("ps", bt)])
                    S.op("act", lambda e: e.activation(out=B2[0][:, cs_], in_=pT, func=AF.Copy), writes=[("ps", bt), ("B", 0, hg)])
                    yield
                    for h in H:
                        S.op("pool", lambda e, h=h: e.tensor_tensor(out=TT[:, h, :], in0=B[0][:, h, :], in1=ident[:], op=ALU.add),
                             reads=[("B", 0, hg), "ident"], writes=[("TT", h)])
                    for k in range(5):
                        cur, nxt = k % 2, 1 - k % 2

                        def mm_a(e, cur=cur):
                            ins = None
                            for h in H:
                                ins = e.matmul(pb_of(h, 0), lhsT=B[cur][:, h, :], rhs=A[cur][:, h, :], start=True, stop=True)
                            return ins
                        S.op("pe", mm_a, reads=[("A", cur, hg), ("B", cur, hg)], writes=[("ps", b0)])
                        S.op("act", lambda e, nxt=nxt: e.activation(out=A2[nxt][:, cs_], in_=PS[:, b0, :], func=AF.Copy),
                             writes=[("ps", b0), ("A", nxt, hg)])
                        if k <= 3:
                            def mm_b(e, cur=cur):
                                ins = None
                                for h in H:
                                    ins = e.matmul(pb_of(h, 2), lhsT=A[cur][:, h, :], rhs=B[cur][:, h, :], start=True, stop=True)
                                return ins
                            S.op("pe", mm_b, reads=[("A", cur, hg), ("B", cur, hg)], writes=[("ps", b1)])
                            S.op("dve", lambda e, nxt=nxt: e.tensor_copy(out=B2[nxt][:, cs_], in_=PS[:, b1, :]),
                                 writes=[("ps", b1), ("B", nxt, hg)])
                        yield
                        next(sd, None)

                        def mm_t(e, nxt=nxt):
                            ins = None
                            for h in H:
                                ins = e.matmul(pb_of(h, 4), lhsT=A[nxt][:, h, :], rhs=TT[:, h, :], start=True, stop=True)
                            return ins
                        S.op("pe", mm_t, reads=[("A", nxt, hg)] + KH("TT"), writes=[("ps", b2)])
                        S.op("dve", lambda e: e.tensor_tensor(out=TT2[:, cs_], in0=TT2[:, cs_], in1=PS[:, b2, :], op=ALU.add),
                             reads=KH("TT"), writes=[("ps", b2)] + KH("TT"))
                        yield
                    for _ in sd:
                        pass
                    def mm_u(e):
                        ins = None
                        for h in H:
                            ins = e.matmul(pb_of(h, 0), lhsT=TT[:, h, :], rhs=vb[:, h, :], start=True, stop=True)
                        return ins
                    S.op("pe", mm_u, reads=KH("TT") + KH("vb"), writes=[("ps", b0)])
                    S.op("act", lambda e: e.activation(out=ubuf2[:, cs_], in_=PS[:, b0, :], func=AF.Copy),
                         writes=[("ps", b0)] + KH("ubuf"))

                    def mm_w(e):
                        ins = None
                        for h in H:
                            ins = e.matmul(pb_of(h, 2), lhsT=kbg[:, h, :], rhs=TT[:, h, :], start=True, stop=True)
                        return ins
                    S.op("pe", mm_w, reads=KH("TT") + KH("kbg"), writes=[("ps", b1)])
                    S.op("dve", lambda e: e.tensor_copy(out=wT2[:, cs_], in_=PS[:, b1, :]), writes=[("ps", b1), ("wT", hg)])
                    yield
                    for c in range(2):
                        rs = slice(c * 64, (c + 1) * 64)

                        def mm_ws(e):
                            ins = None
                            for h in H:
                                ins = e.matmul(pb_of(h, 4), lhsT=wT[:, h, :], rhs=Sb[:, h, :], start=True, stop=True)
                            return ins
                        S.op("pe", mm_ws, reads=[("wT", hg), ("Sb", hg)], writes=[("ps", b2)])
                        S.op("dve", lambda e, rs=rs: e.tensor_tensor(out=vnew2[rs, cs_], in0=ubuf2[rs, cs_], in1=PS[rs, b2, :],
                                                                     op=ALU.subtract),
                             reads=KH("ubuf"), writes=[("ps", b2), ("vnew", hg)])
                        yield
                        if full:
                            def mm_o(e, rs=rs):
                                ins = None
                                for h in H:
                                    e.matmul(pb_of(h, 0), lhsT=qdT[:, h, :], rhs=Sb[:, h, :], start=True, stop=False)
                                    ins = e.matmul(pb_of(h, 0), lhsT=attnT[rs, h, :], rhs=vnew[rs, h, :], start=False, stop=True)
                                return ins
                            S.op("pe", mm_o, reads=[("qdT", hg), ("Sb", hg), ("attnT", hg), ("vnew", hg)], writes=[("ps", b0)])
                            S.op("act", lambda e, rs=rs: e.activation(out=osb2[rs, cs_], in_=PS[rs, b0, :], func=AF.Copy),
                                 writes=[("ps", b0), ("osb", c, hg), ("dec", hg)])

                        def mm_s(e, rs=rs):
                            ins = None
                            for h in H:
                                ins = e.matmul(pb_of(h, 2), lhsT=kd[rs, h, :], rhs=vnew[rs, h, :], start=True, stop=True)
                            return ins
                        S.op("pe", mm_s, reads=KH("kd") + [("vnew", hg)], writes=[("ps", b1)])
                        for h in H:
                            S.op("dve", lambda e, h=h, c=c: e.scalar_tensor_tensor(
                                out=St[:, h, :], in0=St[:, h, :], scalar=dsc[:, so + C_GL + c * 8 + h:so + C_GL + c * 8 + h + 1], in1=pb_of(h, 2),
                                op0=ALU.mult, op1=ALU.add),
                                reads=[("gl", so), ("St", hg)], writes=[("ps", b1), ("St", hg)])
                        S.op("act", lambda e: e.activation(out=Sb2[:, cs_], in_=St2[:, cs_], func=AF.Copy),
                             reads=[("St", hg)], writes=[("Sb", hg)])
                        yield


                def epi(blk, brow, so, zs):
                    S.op("act", lambda e: e.activation(out=self.junk[:], in_=osb2, func=AF.Square),
                         reads=[("osb", c_, hb_) for c_ in range(2) for hb_ in range(2)] + [("dec", 0), ("dec", 1)], writes=["junk"])
                    S.op("dve", lambda e: e.tensor_reduce(out=col(C_OSS, so=so), in_=self.junk[:].rearrange("p (h d) -> p h d", h=8),
                                                          axis=AX.X, op=ALU.add), reads=["junk"], writes=[("oss", so)])
                    yield
                    self.rstd(col(C_OSS, so=so), dsc[:, so + 120:so + 128], col(C_OSS, so=so), 1.0 / 128, RMS_EPS, "explog", [("oss", so)], ("orstd", so))
                    for h in range(8):
                        S.op("dve", lambda e, h=h, zs=zs, so=so: e.scalar_tensor_tensor(
                            out=gatedb[:, h * 128:(h + 1) * 128], in0=osb[:, h, :], scalar=dsc[:, so + C_OSS + h:so + C_OSS + h + 1],
                            in1=zs[:, h * 128:(h + 1) * 128], op0=ALU.mult, op1=ALU.mult),
                            reads=[("orstd", so), ("zs", blk, h // 4), ("dec", h // 4)] + [("osb", c_, h // 4) for c_ in range(2)],
                            writes=[("gatedb", h), ("vnew", h // 4)])
                    yield
                    pT4 = PS[:, 4, :].bitcast(BF16)

                    def tr_g(e):
                        ins = None
                        for c_ in range(8):
                            ins = e.transpose(out=pT4[:, c_ * 128:(c_ + 1) * 128], in_=gatedb[:, c_ * 128:(c_ + 1) * 128], identity=ident[:])
                        return ins
                    S.op("pe", tr_g, reads=[("gatedb", h) for h in range(8)] + ["ident", ("vnew", 0), ("vnew", 1)], writes=[("ps", 4)])
                    S.op("act", lambda e: e.activation(out=gatedT[:], in_=pT4.rearrange("p (c t) -> p c t", c=8), func=AF.Copy),
                         writes=[("ps", 4), "gatedT"])
                    yield
                    yb = (6, 7)
                    for hh in range(2):
                        def mm_out(e, hh=hh, b=yb[hh]):
                            ins = None
                            for c_ in range(8):
                                ins = e.matmul(PS[:, b, :], lhsT=gatedT[:, c_, :], rhs=Wout[:, c_, hh * 512:(hh + 1) * 512],
                                               start=(c_ == 0), stop=(c_ == 7))
                            return ins
                        S.op("pe", mm_out, reads=["gatedT"] + [("Wdo", c_) for c_ in range(8)], writes=[("ps", yb[hh])])
                    self.post_residual(srcap, dstap, brow, yb, 1.0, "explog", dst_off)
                    yield


            tbs = [slice(0, 128), slice(128, 256)]
            sos = [0, 256]

            def drain(g):
                if g is not None:
                    for _ in g:
                        pass
            for _ in pro(tbs[0], sos[0], [("hT", 0)]):
                pass
            streams0 = [grp(0, tbs[0], full, sos[0]), grp(1, tbs[0], full, sos[0]), pro(tbs[1], sos[1], [("hT", 1)])]
            for _ in itertools.zip_longest(*streams0):
                pass
            streams1 = [grp(0, tbs[1], full, sos[1]), grp(1, tbs[1], full, sos[1])]
            if full:
                e0 = epi(0, row0, sos[0], zs2[0])
                next(e0)
                next(e0)
                streams1.append(e0)
            for _ in itertools.zip_longest(*streams1):
                pass
            if full:
                drain(epi(1, row0 + 128, sos[1], zs2[1]))

    def dram(self, name):
        return {"x": self.x_in, "xs": self.xs, "y": self.y_out}[name]


_PROG_CACHE = {}


def get_program(npre, nown, layers_key):
    key = (npre, nown, layers_key)
    if key not in _PROG_CACHE:
        _PROG_CACHE[key] = Builder(npre, nown, list(layers_key)).build()
    return _PROG_CACHE[key]


NPRE = 4096
NOWN = 4096


def full_layers(npre, nown):
    nt = npre + nown
    return (
        ("ffn", (0, 0, "x", "xs", 0, nt, 0)),
        ("dn", (0, "xs", "xs", 0, nt, npre, 0)),
        ("ffn", (0, 1, "xs", "xs", npre, nt, 0)),
        ("ffn", (1, 0, "xs", "xs", npre, nt, 0)),
        ("sg", (1, "xs", "xs", npre, nt, 0)),
        ("ffn", (1, 1, "xs", "y", npre, nt, -npre)),
    )


def kernel(**inputs):
    x = np.ascontiguousarray(np.asarray(inputs["x"], dtype=np.float32))
    B, SEQ, _ = x.shape
    half = SEQ // 2
    nc = get_program(half, half, full_layers(half, half))
    wnames = ["norm_g", "ffn_w_gate", "ffn_w_up", "ffn_w_down", "dn_w_in", "dn_conv_w", "dn_a_log", "dn_dt_bias",
              "dn_norm_g", "dn_w_out", "sg_w_in", "sg_b_in", "sg_ln_g", "sg_ln_b", "sg_w_s", "sg_b_s", "sg_w_out"]
    shared = {k: np.ascontiguousarray(np.asarray(inputs[k], dtype=np.float32)) for k in wnames}
    in_maps = []
    for c in range(2 * B):
        b, hf = c // 2, c % 2
        own = x[b, hf * half:(hf + 1) * half]
        pre = x[b, 0:half] if hf == 1 else np.zeros_like(own)
        m = dict(shared)
        m["x"] = np.ascontiguousarray(np.concatenate([pre, own], axis=0))
        in_maps.append(m)
    res = run_bass_kernel_spmd(nc, in_maps, core_ids=list(range(2 * B)))
    out = np.empty_like(x)
    for c in range(2 * B):
        b, hf = c // 2, c % 2
        out[b, hf * half:(hf + 1) * half] = res.results[c]["y"]
    return out
```

```python
import contextlib
import os
DN_STOP = int(os.environ.get('DN_STOP', '99'))
import numpy as np
import concourse.bass as bass
import concourse.mybir as mybir
from concourse.bass_utils import run_bass_kernel_spmd

F32 = mybir.dt.float32
BF16 = mybir.dt.bfloat16
AF = mybir.ActivationFunctionType
ALU = mybir.AluOpType
AX = mybir.AxisListType

D = 1024
FF = 2816
NFC = FF // 128
T = 512
RMS_EPS = 1e-6
LN_EPS = 1e-5
ENGS = ("pe", "act", "dve", "pool", "sp")


class Op:
    __slots__ = ("eng", "fn", "reads", "writes", "dma", "deps", "sig", "sem", "val", "idx", "bar")

    def __init__(self, eng, fn, reads, writes, dma):
        self.eng = eng
        self.fn = fn
        self.reads = reads
        self.writes = writes
        self.dma = dma
        self.deps = []
        self.sig = False
        self.sem = None
        self.val = 0
        self.bar = 0


class Sched:
    def __init__(self, nc):
        self.nc = nc
        self.ops = []
        self.phase = 0
        self.nbar = 0

    def barrier(self):
        self.nbar += 1
        self.phase += 1

    def op(self, eng, fn, reads=(), writes=(), dma=None):
        if dma == "cst":
            writes = tuple(writes) + ("cstall",)
        o = Op(eng, fn, tuple(reads), tuple(writes), dma)
        o.idx = len(self.ops)
        o.bar = self.nbar
        o.sem = ("dma", dma) if dma is not None else ("eng", eng)
        self.ops.append(o)
        return o

    def emit(self):
        nc = self.nc
        ops = self.ops
        last_w = {}
        readers = {}
        last_eng = {}
        last_dma = {}
        seen_bar = {e: 0 for e in ENGS}
        bar_snap = None
        cur_bar = 0
        for o in ops:
            if o.bar != cur_bar:
                cur_bar = o.bar
                bar_snap = (dict(last_eng), dict(last_dma))
            deps = set()
            for r in o.reads:
                w = last_w.get(r)
                if w is not None:
                    deps.add(w)
            for wk in o.writes:
                w = last_w.get(wk)
                if w is not None:
                    deps.add(w)
                for rd in readers.get(wk, ()):
                    deps.add(rd)
            for r in o.reads:
                readers.setdefault(r, []).append(o.idx)
            for wk in o.writes:
                last_w[wk] = o.idx
                readers[wk] = []
            deps.discard(o.idx)
            real = []
            for d in deps:
                p = ops[d]
                if p.dma is None and o.dma is None and p.eng == o.eng:
                    if o.eng == "pe":
                        continue
                    if not any(r in p.writes for r in o.reads):
                        continue
                real.append(d)
            if seen_bar[o.eng] != o.bar:
                seen_bar[o.eng] = o.bar
                for e, i in bar_snap[0].items():
                    if e != o.eng:
                        real.append(i)
                for k, i in bar_snap[1].items():
                    real.append(i)
            for d in real:
                ops[d].sig = True
            o.deps = real
            if o.dma is None:
                last_eng[o.eng] = o.idx
            else:
                last_dma[o.dma] = o.idx
        counts = {}
        for o in ops:
            if o.dma is not None:
                counts[o.sem] = counts.get(o.sem, 0) + 16
                o.val = counts[o.sem]
            elif o.sig:
                counts[o.sem] = counts.get(o.sem, 0) + 1
                o.val = counts[o.sem]
        semkeys = sorted(counts.keys(), key=str)
        self.n_sems = len(semkeys)
        self.max_val = max(counts.values()) if counts else 0
        per_eng = {e: [] for e in ENGS}
        for o in ops:
            per_eng[o.eng].append(o)
        with contextlib.ExitStack() as st:
            sems = {k: st.enter_context(nc.semaphore("s%d" % i)) for i, k in enumerate(semkeys)}
            block = st.enter_context(nc.Block())

            def run(engname, eng):
                known = {}
                for o in per_eng[engname]:
                    need = {}
                    for d in o.deps:
                        p = ops[d]
                        if known.get(p.sem, 0) >= p.val:
                            continue
                        if need.get(p.sem, 0) < p.val:
                            need[p.sem] = p.val
                    for s, v in need.items():
                        eng.wait_ge(sems[s], v)
                        known[s] = v
                    ins = o.fn(eng)
                    if o.dma is not None:
                        ins.then_inc(sems[o.sem], 16)
                    elif o.sig:
                        ins.then_inc(sems[o.sem], 1)
                fin = {}
                for o in per_eng[engname]:
                    if o.dma is not None:
                        fin[o.sem] = max(fin.get(o.sem, 0), o.val)
                for s, v in fin.items():
                    if known.get(s, 0) < v:
                        eng.wait_ge(sems[s], v)

            if per_eng["sp"]:
                @block.sync
                def _(e):
                    run("sp", e)
            if per_eng["pe"]:
                @block.tensor
                def _(e):
                    run("pe", e)
            if per_eng["act"]:
                @block.scalar
                def _(e):
                    run("act", e)
            if per_eng["dve"]:
                @block.vector
                def _(e):
                    run("dve", e)
            if per_eng["pool"]:
                @block.gpsimd
                def _(e):
                    run("pool", e)


class Builder:
    def __init__(self, npre, nown, layers):
        self.npre = npre
        self.nown = nown
        self.ntot = npre + nown
        self.layers = layers
        self.nc = bass.Bass("TRN2", target_bir_lowering=False)
        self.S = Sched(self.nc)
        self.st = contextlib.ExitStack()
        self.uid = 0

    def sb(self, name, shape, dt):
        return self.st.enter_context(self.nc.sbuf_tensor(name, shape, dt))

    def dram_in(self, name, shape):
        return self.nc.dram_tensor(name, list(shape), F32, kind="ExternalInput").ap()

    def build(self):
        nc, S = self.nc, self.S
        with self.st:
            self.declare_io()
            self.alloc()
            self.consts()
            li = 0
            for (kind, arg) in self.layers:
                S.barrier()
                if kind == "ffn":
                    self.ffn_phase(*arg)
                elif kind == "sg":
                    self.sg_phase(*arg)
                elif kind == "dn":
                    self.dn_phase(*arg)
                li += 1
            S.emit()
        return nc

    def declare_io(self):
        nc = self.nc
        self.x_in = self.dram_in("x", (self.ntot, D))
        self.norm_g = self.dram_in("norm_g", (2, 6, D))
        self.w_gate = self.dram_in("ffn_w_gate", (2, 2, D, FF))
        self.w_up = self.dram_in("ffn_w_up", (2, 2, D, FF))
        self.w_down = self.dram_in("ffn_w_down", (2, 2, FF, D))
        self.dn_w_in = self.dram_in("dn_w_in", (1, D, 4112))
        self.dn_conv_w = self.dram_in("dn_conv_w", (1, 4, 3072))
        self.dn_a_log = self.dram_in("dn_a_log", (1, 8))
        self.dn_dt_bias = self.dram_in("dn_dt_bias", (1, 8))
        self.dn_norm_g = self.dram_in("dn_norm_g", (1, 128))
        self.dn_w_out = self.dram_in("dn_w_out", (1, D, D))
        self.sg_w_in = self.dram_in("sg_w_in", (1, D, 4096))
        self.sg_b_in = self.dram_in("sg_b_in", (1, 4096))
        self.sg_ln_g = self.dram_in("sg_ln_g", (1, 2048))
        self.sg_ln_b = self.dram_in("sg_ln_b", (1, 2048))
        self.sg_w_s = self.dram_in("sg_w_s", (1, 8, 128, 128))
        self.sg_b_s = self.dram_in("sg_b_s", (1, 8, 128))
        self.sg_w_out = self.dram_in("sg_w_out", (1, 2048, D))
        self.y_out = nc.dram_tensor("y", [self.nown, D], F32, kind="ExternalOutput").ap()
        self.xs = nc.dram_tensor("xs_scratch", [self.ntot, D], F32, kind="Internal").ap()

    def alloc(self):
        nc = self.nc
        self.ARENA_N = 79872
        self.ARENA = self.sb("ARENA", [128, self.ARENA_N], BF16)
        self.xring = [self.sb("xr%d" % i, [128, D], F32) for i in range(4)]
        self.xres = [self.sb("xq%d" % i, [128, D], F32) for i in range(1)]
        self.xnew = [self.sb("xw%d" % i, [128, D], F32) for i in range(1)]
        self.xn = [self.sb("xn%d" % i, [128, D], BF16) for i in range(2)]
        self.junk = self.sb("junk", [128, D], BF16)
        self.hT = self.sb("hT", [128, 8, T], BF16)
        self.gpre = self.sb("gpre", [128, D], F32)
        self.gpost = self.sb("gpost", [128, D], F32)
        self.sgt = [self.sb("sgt%d" % i, [128, T], F32) for i in range(2)]
        self.ident = self.sb("ident", [128, 128], BF16)
        self.identf = self.sb("identf", [128, 128], F32)
        self.stat = self.sb("stat", [128, 64], F32)
        self.onesb = self.sb("onesb", [128, 128], BF16)
        self.onesf = self.sb("onesf", [128, 128], F32)
        self.PS = self.st.enter_context(nc.psum_tensor("PS", [128, 8, 512], F32))
        self.cnt = {"xr": 0, "xq": 0, "xw": 0, "xn": 0, "st": 0}

    def consts(self):
        S = self.S
        ident, identf = self.ident, self.identf
        S.op("pool", lambda e: e.memset(identf[:], 0.0), writes=["identf"])
        S.op("pool", lambda e: e.affine_select(out=identf[:], in_=identf[:], pattern=[[-1, 128]],
                                               compare_op=ALU.not_equal, fill=1.0, base=0,
                                               channel_multiplier=1),
             reads=["identf"], writes=["identf"])
        S.op("pool", lambda e: e.tensor_copy(out=ident[:], in_=identf[:]), reads=["identf"], writes=["ident"])
        S.op("pool", lambda e: e.memset(self.onesb[:], 1.0), writes=["onesb"])
        S.op("pool", lambda e: e.memset(self.onesf[:], 1.0), writes=["onesf"])

    def carve(self, off, n, dt=BF16):
        if dt == BF16:
            assert off + n <= self.ARENA_N
            return self.ARENA[:, off:off + n], off + n
        assert off % 2 == 0 and off + 2 * n <= self.ARENA_N
        return self.ARENA[:, off:off + 2 * n].bitcast(F32), off + 2 * n

    def bank(self, b):
        return self.PS[:, b, :]

    def load_gvec(self, dst, src_row, key):
        self.S.op("sp", lambda e: e.dma_start(out=dst[:], in_=src_row.partition_broadcast(128)),
                  writes=[key], dma=key)

    def prenorm_tile(self, src, row0, rstd_mode, nsub=4):
        S = self.S
        stat, junk, hT, gpre = self.stat, self.junk, self.hT, self.gpre
        xs_ = []
        for s in range(nsub):
            i = self.cnt["xr"] % 4
            self.cnt["xr"] += 1
            xt = self.xring[i]
            r0 = row0 + s * 128
            S.op("sp", lambda e, xt=xt, r0=r0: e.dma_start(out=xt[:], in_=src[r0:r0 + 128, :]),
                 writes=[("xr", i)], dma=("xr", i))
            S.op("act", lambda e, xt=xt, s=s: e.activation(out=junk[:], in_=xt[:], func=AF.Square,
                                                           accum_out=stat[:, s:s + 1]),
                 reads=[("xr", i)], writes=["junk", ("stat", s)])
            xs_.append((xt, i))
        self.rstd(stat[:, 0:nsub], stat[:, 4:4 + nsub], stat[:, 8:8 + nsub], 1.0 / D, RMS_EPS, rstd_mode,
                  [("stat", s) for s in range(nsub)], "rs_pre")
        for s in range(nsub):
            xt, i = xs_[s]
            j = self.cnt["xn"] % 2
            self.cnt["xn"] += 1
            xn = self.xn[j]
            S.op("dve", lambda e, xt=xt, xn=xn, s=s: e.scalar_tensor_tensor(
                out=xn[:], in0=xt[:], scalar=stat[:, 8 + s:9 + s], in1=gpre[:], op0=ALU.mult, op1=ALU.mult),
                reads=[("xr", i), "rs_pre", "gpre"], writes=[("xn", j)])
            pb = s % 4
            pT = self.PS[:, pb, :].bitcast(BF16)

            def tr(e, xn=xn, pT=pT):
                ins = None
                for c in range(8):
                    ins = e.transpose(out=pT[:, c * 128:(c + 1) * 128], in_=xn[:, c * 128:(c + 1) * 128],
                                      identity=self.ident[:])
                return ins
            S.op("pe", tr, reads=[("xn", j), "ident"], writes=[("ps", pb)])
            S.op("act", lambda e, pT=pT, s=s: e.activation(
                out=hT[:, :, s * 128:(s + 1) * 128], in_=pT.rearrange("p (c t) -> p c t", c=8), func=AF.Copy),
                writes=[("ps", pb), ("hT", s)])

    def rstd(self, ss, tmp, out, scale, eps, mode, rkeys, wkey):
        S = self.S
        if mode == "sqrt":
            S.op("dve", lambda e: e.tensor_scalar(out=tmp, in0=ss, scalar1=scale, scalar2=eps,
                                                  op0=ALU.mult, op1=ALU.add),
                 reads=rkeys, writes=[(wkey, "t")])
            S.op("act", lambda e: e.activation(out=tmp, in_=tmp, func=AF.Sqrt),
                 reads=[(wkey, "t")], writes=[(wkey, "t")])
            S.op("dve", lambda e: e.reciprocal(out=out, in_=tmp), reads=[(wkey, "t")], writes=[wkey])
        else:
            S.op("dve", lambda e: e.tensor_scalar(out=tmp, in0=ss, scalar1=scale, scalar2=eps,
                                                  op0=ALU.mult, op1=ALU.add),
                 reads=rkeys, writes=[(wkey, "t")])
            S.op("act", lambda e: e.activation(out=tmp, in_=tmp, func=AF.Ln),
                 reads=[(wkey, "t")], writes=[(wkey, "t")])
            S.op("act", lambda e: e.activation(out=out, in_=tmp, func=AF.Exp, scale=-0.5),
                 reads=[(wkey, "t")], writes=[wkey])

    def post_residual(self, src, dst, row0, ybanks, coef, rstd_mode, dst_off=0):
        S = self.S
        stat, junk, gpost = self.stat, self.junk, self.gpost
        b0, b1 = ybanks
        q = self.cnt["xq"] % 1
        self.cnt["xq"] += 1
        xq = self.xres[q]
        S.op("sp", lambda e: e.dma_start(out=xq[:], in_=src[row0:row0 + 128, :]),
             writes=[("xq", q)], dma=("xq", q))
        k = self.cnt["st"] % 4
        self.cnt["st"] += 1
        c0 = 16 + k * 8
        S.op("act", lambda e: e.activation(out=junk[:, 0:512], in_=self.PS[:, b0, :], func=AF.Square,
                                           accum_out=stat[:, c0:c0 + 1]),
             writes=[("ps", b0), "junk", ("pst", k, 0)])
        S.op("act", lambda e: e.activation(out=junk[:, 512:1024], in_=self.PS[:, b1, :], func=AF.Square,
                                           accum_out=stat[:, c0 + 1:c0 + 2]),
             writes=[("ps", b1), "junk", ("pst", k, 1)])
        S.op("dve", lambda e: e.tensor_tensor(out=stat[:, c0 + 2:c0 + 3], in0=stat[:, c0:c0 + 1],
                                              in1=stat[:, c0 + 1:c0 + 2], op=ALU.add),
             reads=[("pst", k, 0), ("pst", k, 1)], writes=[("pst", k, 2)])
        self.rstd(stat[:, c0 + 2:c0 + 3], stat[:, c0 + 3:c0 + 4], stat[:, c0 + 4:c0 + 5], 1.0 / D, RMS_EPS,
                  rstd_mode, [("pst", k, 2)], ("pst", k, 4))
        S.op("dve", lambda e: e.tensor_scalar(out=stat[:, c0 + 5:c0 + 6], in0=stat[:, c0 + 4:c0 + 5],
                                              scalar1=float(coef), scalar2=None, op0=ALU.mult),
             reads=[("pst", k, 4)], writes=[("pst", k, 5)])
        w = self.cnt["xw"] % 1
        self.cnt["xw"] += 1
        xw = self.xnew[w]
        for hh, b in ((0, b0), (1, b1)):
            S.op("dve", lambda e, hh=hh, b=b: e.scalar_tensor_tensor(
                out=xw[:, hh * 512:(hh + 1) * 512], in0=self.PS[:, b, :], scalar=stat[:, c0 + 5:c0 + 6],
                in1=gpost[:, hh * 512:(hh + 1) * 512], op0=ALU.mult, op1=ALU.mult),
                reads=[("pst", k, 5), "gpost"], writes=[("ps", b), ("xw", w, hh)])
        S.op("pool", lambda e: e.tensor_tensor(out=xw[:], in0=xw[:], in1=xq[:], op=ALU.add),
             reads=[("xw", w, 0), ("xw", w, 1), ("xq", q)], writes=[("xw", w, 0), ("xw", w, 1)])
        S.op("sp", lambda e: e.dma_start(out=dst[row0 + dst_off:row0 + dst_off + 128, :], in_=xw[:]),
             reads=[("xw", w, 0), ("xw", w, 1)], dma=("xwst", w))

    def ffn_phase(self, layer, which, src, dst, row_lo, row_hi, dst_off=0):
        S = self.S
        srcap, dstap = self.dram(src), self.dram(dst)
        ph = S.phase
        a, off = self.carve(0, 8 * FF)
        Wg = a.rearrange("p (c f) -> p c f", c=8)
        a, off = self.carve(off, 8 * FF)
        Wu = a.rearrange("p (c f) -> p c f", c=8)
        a, off = self.carve(off, NFC * D)
        Wd = a.rearrange("p (c d) -> p c d", c=NFC)
        a, off = self.carve(off, NFC * T)
        actT = a.rearrange("p (c t) -> p c t", c=NFC)
        sg = self.sgt
        wg_d = self.w_gate[layer, which].rearrange("(c p) f -> p c f", p=128)
        wu_d = self.w_up[layer, which].rearrange("(c p) f -> p c f", p=128)
        wd_d = self.w_down[layer, which].rearrange("(c p) d -> p c d", p=128)
        ipre, ipost = (0, 1) if which == 0 else (4, 5)
        self.load_gvec(self.gpre, self.norm_g[layer, ipre:ipre + 1, :], "gpre")
        self.load_gvec(self.gpost, self.norm_g[layer, ipost:ipost + 1, :], "gpost")
        for c in range(8):
            S.op("pool", lambda e, c=c: e.dma_start(out=Wg[:, c, :], in_=wg_d[:, c, :]),
                 writes=[("Wg", c)], dma="wgu")
            S.op("pool", lambda e, c=c: e.dma_start(out=Wu[:, c, :], in_=wu_d[:, c, :]),
                 writes=[("Wu", c)], dma="wgu")
        for c in range(0, NFC, 2):
            S.op("pool", lambda e, c=c: e.dma_start(out=Wd[:, c:c + 2, :], in_=wd_d[:, c:c + 2, :]),
                 writes=[("Wd", c), ("Wd", c + 1)], dma="wd")
        hT = self.hT
        tiles = list(range(row_lo, row_hi, T))
        self.prenorm_tile(srcap, tiles[0], "sqrt")
        for ti, row0 in enumerate(tiles):
            for fc in range(NFC):
                par = fc % 2
                gb, ub = 4 + 2 * par, 5 + 2 * par

                def mm_gu(e, fc=fc, gb=gb, ub=ub):
                    ins = None
                    for c in range(8):
                        ins = e.matmul(self.PS[:, gb, :], lhsT=Wg[:, c, fc * 128:(fc + 1) * 128], rhs=hT[:, c, :],
                                       start=(c == 0), stop=(c == 7))
                    for c in range(8):
                        ins = e.matmul(self.PS[:, ub, :], lhsT=Wu[:, c, fc * 128:(fc + 1) * 128], rhs=hT[:, c, :],
                                       start=(c == 0), stop=(c == 7))
                    return ins
                S.op("pe", mm_gu, reads=[("hT", s) for s in range(4)] + [("Wg", c) for c in range(8)] +
                     [("Wu", c) for c in range(8)], writes=[("ps", gb), ("ps", ub)])
                sgt = sg[par]
                S.op("act", lambda e, gb=gb, sgt=sgt: e.activation(out=sgt[:], in_=self.PS[:, gb, :], func=AF.Silu),
                     writes=[("ps", gb), ("sg", par)])
                S.op("dve", lambda e, ub=ub, sgt=sgt, fc=fc: e.tensor_tensor(
                    out=actT[:, fc, :], in0=self.PS[:, ub, :], in1=sgt[:], op=ALU.mult),
                    reads=[("sg", par)], writes=[("ps", ub), ("actT", fc)])
            if ti + 1 < len(tiles):
                self.prenorm_tile(srcap, tiles[ti + 1], "sqrt")
            for s in range(4):
                yb = (0, 1) if s % 2 == 0 else (2, 3)
                for hh in range(2):
                    def mm_d(e, s=s, hh=hh, b=yb[hh]):
                        ins = None
                        for fc in range(NFC):
                            ins = e.matmul(self.PS[:, b, :], lhsT=actT[:, fc, s * 128:(s + 1) * 128],
                                           rhs=Wd[:, fc, hh * 512:(hh + 1) * 512],
                                           start=(fc == 0), stop=(fc == NFC - 1))
                        return ins
                    S.op("pe", mm_d, reads=[("actT", fc) for fc in range(NFC)] + [("Wd", c) for c in range(NFC)],
                         writes=[("ps", yb[hh])])
                self.post_residual(srcap, dstap, row0 + s * 128, yb, 0.5, "sqrt", dst_off)

    def sg_phase(self, layer, src, dst, row_lo, row_hi, dst_off=0):
        S = self.S
        srcap, dstap = self.dram(src), self.dram(dst)
        E = 2048
        a, off = self.carve(0, 8 * 4096)
        Win = a.rearrange("p (c f) -> p c f", c=8)
        a, off = self.carve(off, 16 * D)
        Wout = a.rearrange("p (c d) -> p c d", c=16)
        zz, off = self.carve(off, 4096, F32)
        lng, off = self.carve(off, E, F32)
        lnb, off = self.carve(off, E, F32)
        vn, off = self.carve(off, E)
        gated, off = self.carve(off, E)
        a, off = self.carve(off, E)
        gT = a.rearrange("p (c t) -> p c t", c=16)
        a, off = self.carve(off, 1024)
        wcT = a.rearrange("p (g t) -> p g t", g=8)
        a, off = self.carve(off, 1024)
        wcb = a.rearrange("p (g t) -> p g t", g=8)
        bhl, off = self.carve(off, 4096)
        bf32 = zz
        stat = self.stat
        hT = self.hT
        win_d = self.sg_w_in[0].rearrange("(c p) f -> p c f", p=128)
        wout_d = self.sg_w_out[0].rearrange("(c p) d -> p c d", p=128)
        self.load_gvec(self.gpre, self.norm_g[layer, 2:3, :], "gpre")
        self.load_gvec(self.gpost, self.norm_g[layer, 3:4, :], "gpost")
        for c in range(8):
            S.op("pool", lambda e, c=c: e.dma_start(out=Win[:, c, :], in_=win_d[:, c, :]),
                 writes=[("Wsi", c)], dma="wgu")
        for c in range(0, 16, 2):
            S.op("pool", lambda e, c=c: e.dma_start(out=Wout[:, c:c + 2, :], in_=wout_d[:, c:c + 2, :]),
                 writes=[("Wso", c), ("Wso", c + 1)], dma="wd")
        S.op("sp", lambda e: e.dma_start(out=lng[:], in_=self.sg_ln_g[0:1, :].partition_broadcast(128)),
             writes=["lng"], dma="cst")
        S.op("sp", lambda e: e.dma_start(out=lnb[:], in_=self.sg_ln_b[0:1, :].partition_broadcast(128)),
             writes=["lnb"], dma="cst")
        S.op("sp", lambda e: e.dma_start(out=bf32[0:1, :], in_=self.sg_b_in[0:1, :]), writes=[("zz", b_) for b_ in range(8)], dma="cst")
        S.op("sp", lambda e: e.dma_start(out=stat[:, 48:56], in_=self.sg_b_s[0].rearrange("g t -> t g"),
                                         allow_slow_non_contiguous=True), writes=["bstok"], dma="cst")
        S.op("dve", lambda e: e.tensor_copy(out=bhl[0:1, :], in_=bf32[0:1, :]), reads=[("zz", b_) for b_ in range(8)], writes=["bhl0"])
        S.op("dve", lambda e: e.tensor_tensor(out=bf32[0:1, :], in0=bf32[0:1, :], in1=bhl[0:1, :], op=ALU.subtract),
             reads=[("zz", b_) for b_ in range(8)] + ["bhl0"], writes=[("zz", b_) for b_ in range(8)])
        S.op("dve", lambda e: e.tensor_copy(out=vn[0:1, :], in_=bf32[0:1, 0:2048]), reads=[("zz", b_) for b_ in range(8)], writes=["vn"])
        S.op("dve", lambda e: e.tensor_copy(out=gated[0:1, :], in_=bf32[0:1, 2048:4096]), reads=[("zz", b_) for b_ in range(8)],
             writes=[("gated", g) for g in range(8)])
        S.op("sp", lambda e: e.dma_start(out=bhl[1:2, 0:2048], in_=vn[0:1, :]), reads=["vn"], writes=["bhl1"], dma="cst")
        S.op("sp", lambda e: e.dma_start(out=bhl[1:2, 2048:4096], in_=gated[0:1, :]),
             reads=[("gated", g) for g in range(8)], writes=["bhl1b"], dma="cst")
        ones2 = self.onesb[0:2, 0:128]
        wtmp = self.xnew[0]
        wtv = wtmp[:].rearrange("p (g s) -> p g s", g=8)
        S.op("sp", lambda e: e.dma_start(out=wtv, in_=self.sg_w_s[0].rearrange("g t s -> t g s")),
             writes=[("xw", 0, 0), ("xw", 0, 1)], dma="cst")
        S.op("pool", lambda e: e.affine_select(out=wtv, in_=wtv, pattern=[[0, 8], [-1, 128]], compare_op=ALU.is_ge,
                                               fill=0.0, base=0, channel_multiplier=1),
             reads=[("xw", 0, 0), ("xw", 0, 1)], writes=[("xw", 0, 0), ("xw", 0, 1)])
        S.op("pool", lambda e: e.tensor_copy(out=wcb[:], in_=wtv), reads=[("xw", 0, 0), ("xw", 0, 1)], writes=["wcb"])
        pTw = self.PS[:, 7, :].bitcast(BF16)

        def trw(e):
            ins = None
            for g in range(8):
                ins = e.transpose(out=pTw[:, g * 128:(g + 1) * 128], in_=wcb[:, g, :], identity=self.ident[:])
            return ins
        S.op("pe", trw, reads=["wcb", "ident"], writes=[("ps", 7)])
        S.op("act", lambda e: e.activation(out=wcT[:], in_=pTw.rearrange("p (g t) -> p g t", g=8), func=AF.Copy),
             writes=[("ps", 7), "wcT"])
        def stage_in(row0, s):
            tok = slice(s * 128, (s + 1) * 128)
            for blk in range(8):
                b = 4 + blk % 2

                def mm_in(e, blk=blk, b=b, tok=tok):
                    ins = None
                    for c in range(8):
                        ins = e.matmul(self.PS[:, b, :], lhsT=hT[:, c, tok], rhs=Win[:, c, blk * 512:(blk + 1) * 512],
                                       start=(c == 0), stop=False)
                    ins = e.matmul(self.PS[:, b, :], lhsT=ones2, rhs=bhl[0:2, blk * 512:(blk + 1) * 512],
                                   start=False, stop=True)
                    return ins
                S.op("pe", mm_in, reads=[("hT", s), "bhl0", "bhl1", "bhl1b", "onesb"] + [("Wsi", c) for c in range(8)],
                     writes=[("ps", b)])
                if blk < 4:
                    S.op("act", lambda e, blk=blk, b=b: e.activation(
                        out=zz[:, blk * 512:(blk + 1) * 512], in_=self.PS[:, b, :], func=AF.Gelu),
                        writes=[("ps", b), ("zz", blk)])
                else:
                    S.op("act", lambda e, blk=blk, b=b: e.activation(
                        out=zz[:, blk * 512:(blk + 1) * 512], in_=self.PS[:, b, :], func=AF.Gelu,
                        accum_out=stat[:, 56 + blk - 4:57 + blk - 4]),
                        writes=[("ps", b), ("zz", blk), ("vs", blk)])

        def stage_mix(row0, s):
            zv = zz[:, E:2 * E]
            S.op("dve", lambda e: e.tensor_reduce(out=stat[:, 60:61], in_=stat[:, 56:60], axis=AX.X, op=ALU.add),
                 reads=[("vs", b_) for b_ in range(4, 8)], writes=["vsum"])
            S.op("dve", lambda e: e.tensor_scalar(out=stat[:, 61:62], in0=stat[:, 60:61], scalar1=-1.0 / E,
                                                  scalar2=None, op0=ALU.mult), reads=["vsum"], writes=["vnm"])
            S.op("act", lambda e: e.activation(out=self.junk[:, :], in_=zv[:, 0:1024], func=AF.Square,
                                               bias=stat[:, 61:62], accum_out=stat[:, 62:63]),
                 reads=["vnm", ("zz", 4), ("zz", 5)], writes=["junk", "vq0"])
            S.op("act", lambda e: e.activation(out=self.junk[:, :], in_=zv[:, 1024:2048], func=AF.Square,
                                               bias=stat[:, 61:62], accum_out=stat[:, 63:64]),
                 reads=["vnm", ("zz", 6), ("zz", 7)], writes=["junk", "vq1"])
            S.op("dve", lambda e: e.tensor_tensor(out=stat[:, 12:13], in0=stat[:, 62:63], in1=stat[:, 63:64], op=ALU.add),
                 reads=["vq0", "vq1"], writes=["vq"])
            self.rstd(stat[:, 12:13], stat[:, 13:14], stat[:, 14:15], 1.0 / E, LN_EPS, "sqrt", ["vq"], "vrs")
            S.op("dve", lambda e: e.tensor_scalar(out=zv, in0=zv, scalar1=stat[:, 61:62], scalar2=stat[:, 14:15],
                                                  op0=ALU.add, op1=ALU.mult),
                 reads=["vnm", "vrs", "vq0", "vq1"] + [("zz", b_) for b_ in range(4, 8)],
                 writes=[("zz", b_) for b_ in range(4, 8)])
            S.op("pool", lambda e: e.tensor_tensor(out=zv, in0=zv, in1=lng[:], op=ALU.mult),
                 reads=["lng"] + [("zz", b_) for b_ in range(4, 8)], writes=[("zz", b_) for b_ in range(4, 8)])
            S.op("pool", lambda e: e.tensor_tensor(out=vn[:], in0=zv, in1=lnb[:], op=ALU.add),
                 reads=["lnb"] + [("zz", b_) for b_ in range(4, 8)], writes=["vn"])
            for g in range(8):
                b = g // 2
                S.op("pe", lambda e, g=g, b=b: e.matmul(
                    self.PS[:, b, (g % 2) * 256:(g % 2) * 256 + 256], lhsT=wcT[:, g, :],
                    rhs=vn[:, g * 256:(g + 1) * 256], start=True, stop=True),
                    reads=["vn", "wcT"], writes=[("ps", b)])
                S.op("dve", lambda e, g=g, b=b: e.scalar_tensor_tensor(
                    out=gated[:, g * 256:(g + 1) * 256], in0=self.PS[:, b, (g % 2) * 256:(g % 2) * 256 + 256],
                    scalar=stat[:, 48 + g:49 + g], in1=zz[:, g * 256:(g + 1) * 256], op0=ALU.add, op1=ALU.mult),
                    reads=["bstok", ("zz", g // 2)], writes=[("ps", b), ("gated", g)])

        def stage_out(row0, s):
            for half in range(2):
                pb = 6 + half
                pT = self.PS[:, pb, :].bitcast(BF16)

                def trg(e, half=half, pT=pT):
                    ins = None
                    for c in range(8):
                        ec = half * 8 + c
                        ins = e.transpose(out=pT[:, c * 128:(c + 1) * 128], in_=gated[:, ec * 128:(ec + 1) * 128],
                                          identity=self.ident[:])
                    return ins
                S.op("pe", trg, reads=[("gated", g) for g in range(8)] + ["ident"], writes=[("ps", pb)])
                S.op("act", lambda e, half=half, pT=pT: e.activation(
                    out=gT[:, half * 8:(half + 1) * 8, :], in_=pT.rearrange("p (c t) -> p c t", c=8), func=AF.Copy),
                    writes=[("ps", pb), ("gT", half)])
            yb = (0, 1) if s % 2 == 0 else (2, 3)
            for hh in range(2):
                def mm_o(e, hh=hh, b=yb[hh]):
                    ins = None
                    for ec in range(16):
                        ins = e.matmul(self.PS[:, b, :], lhsT=gT[:, ec, :], rhs=Wout[:, ec, hh * 512:(hh + 1) * 512],
                                       start=(ec == 0), stop=(ec == 15))
                    return ins
                S.op("pe", mm_o, reads=[("gT", 0), ("gT", 1)] + [("Wso", c) for c in range(16)], writes=[("ps", yb[hh])])
            self.post_residual(srcap, dstap, row0 + s * 128, yb, 1.0, "sqrt", dst_off)

        tiles = list(range(row_lo, row_hi, T))
        subs = [(r, s_) for r in tiles for s_ in range(4)]
        self.prenorm_tile(srcap, tiles[0], "sqrt")
        for i in range(len(subs) + 1):
            if i < len(subs):
                stage_in(*subs[i])
                if subs[i][1] == 3 and subs[i][0] + T < row_hi:
                    self.prenorm_tile(srcap, subs[i][0] + T, "sqrt")
            if i > 0:
                stage_out(*subs[i - 1])
            if i < len(subs):
                stage_mix(*subs[i])

    def dn_phase(self, layer, src, dst, row_lo, row_hi, full_from, dst_off=0):
        import itertools
        S = self.S
        srcap, dstap = self.dram(src), self.dram(dst)
        TD = 256
        PS = self.PS
        hT = self.hT
        a, off = self.carve(0, 8 * 4112)
        Win = a.rearrange("p (c f) -> p c f", c=8)
        a, off = self.carve(off, 8 * D)
        Wout = a.rearrange("p (c d) -> p c d", c=8)
        a, off = self.carve(off, 8 * TD); qT = a.rearrange("p (h t) -> p h t", h=8)
        a, off = self.carve(off, 8 * TD); kT = a.rearrange("p (h t) -> p h t", h=8)
        a, off = self.carve(off, 8 * TD); vT = a.rearrange("p (h t) -> p h t", h=8)
        a, off = self.carve(off, 72, F32); carry = a.rearrange("p (c j) -> p c j", c=24)
        a, off = self.carve(off, 96, F32); cw = a.rearrange("p (c j) -> p c j", c=24)
        blk_off = off
        zs2 = []
        for i in range(2):
            a, off = self.carve(off, 1024, F32); zs2.append(a)
        r_off = off
        a, off = self.carve(off, 1024); vb = a.rearrange("p (h d) -> p h d", h=8)
        a, off = self.carve(off, 1024); kbg = a.rearrange("p (h d) -> p h d", h=8)
        a, off = self.carve(off, 1024); kd = a.rearrange("p (h d) -> p h d", h=8)
        dec2, off = self.carve(off, 1024, F32); dec = dec2.rearrange("p (h j) -> p h j", h=8)
        attn2, off = self.carve(off, 1024); attn = attn2.rearrange("p (h j) -> p h j", h=8)
        attnT2, off = self.carve(off, 1024); attnT = attnT2.rearrange("p (h j) -> p h j", h=8)
        A2 = []; B2 = []
        for i in range(2):
            a, off = self.carve(off, 1024); A2.append(a)
        for i in range(2):
            a, off = self.carve(off, 1024); B2.append(a)
        A = [x.rearrange("p (h j) -> p h j", h=8) for x in A2]
        B = [x.rearrange("p (h j) -> p h j", h=8) for x in B2]
        TT2, off = self.carve(off, 1024); TT = TT2.rearrange("p (h j) -> p h j", h=8)
        ubuf2, off = self.carve(off, 1024, F32); ubuf = ubuf2.rearrange("p (h d) -> p h d", h=8)
        wT2, off = self.carve(off, 1024); wT = wT2.rearrange("p (h t) -> p h t", h=8)
        qdT2, off = self.carve(off, 1024); qdT = qdT2.rearrange("p (h t) -> p h t", h=8)
        osb2 = dec2; osb = dec
        St2, off = self.carve(off, 1024, F32); St = St2.rearrange("p (h d) -> p h d", h=8)
        Sb2, off = self.carve(off, 1024); Sb = Sb2.rearrange("p (h d) -> p h d", h=8)
        vnew2, off = self.carve(off, 1024); vnew = vnew2.rearrange("p (h d) -> p h d", h=8)
        gatedb = vnew2
        a, off = self.carve(off, 1024); gatedT = a.rearrange("p (c t) -> p c t", c=8)
        dng, off = self.carve(off, 1024, F32)
        cwj, _ = self.carve(blk_off, 3072, F32)
        t_off = r_off
        pc = []
        cv = []
        for i in range(4):
            a, t_off = self.carve(t_off, 260, F32); pc.append(a)
        for i in range(4):
            a, t_off = self.carve(t_off, TD, F32); cv.append(a)
        sqall2, _ = self.carve(r_off, 16 * TD)
        sqall = sqall2.rearrange("p (c t) -> p c t", c=16)
        rinvall2, t_off = self.carve(t_off, 16 * TD, F32)
        rinvall = rinvall2.rearrange("p (c t) -> p c t", c=16)
        assert t_off <= r_off + 16384
        dsc = self.sgt[0]
        cst = self.sgt[1]
        Tri = cst[:, 0:128]
        Mc = cst[:, 128:256]
        Bones = cst[:, 256:384]
        Bsel0 = cst[:, 384:512]
        Bsel1 = dsc[:, 128:256]
        onesf, onesb, ident = self.onesf, self.onesb, self.ident
        C_T1, C_XA, C_BETA, C_NB, C_SP, C_G, C_GC, C_EGC, C_EDK, C_GL, C_BEGC, C_DTB, C_NEA, C_OSS = \
            0, 8, 16, 24, 32, 40, 48, 56, 64, 72, 88, 96, 104, 112

        def col(c, n=8, so=0):
            return dsc[:, so + c:so + c + n]
        win_d = self.dn_w_in[0].rearrange("(c p) f -> p c f", p=128)
        wout_d = self.dn_w_out[0].rearrange("(c p) d -> p c d", p=128)
        self.load_gvec(self.gpre, self.norm_g[layer, 2:3, :], "gpre")
        self.load_gvec(self.gpost, self.norm_g[layer, 3:4, :], "gpost")
        for c in range(8):
            S.op("pool", lambda e, c=c: e.dma_start(out=Win[:, c, :], in_=win_d[:, c, :]),
                 writes=[("Wdi", c)], dma="wgu")
        for c in range(0, 8, 2):
            S.op("pool", lambda e, c=c: e.dma_start(out=Wout[:, c:c + 2, :], in_=wout_d[:, c:c + 2, :]),
                 writes=[("Wdo", c), ("Wdo", c + 1)], dma="wd")
        for h in range(8):
            S.op("sp", lambda e, h=h: e.dma_start(out=dng[:, h * 128:(h + 1) * 128],
                                                  in_=self.dn_norm_g[0:1, :].partition_broadcast(128)),
                 writes=[("dng", h)], dma="cst")
        S.op("sp", lambda e: e.dma_start(out=col(C_DTB), in_=self.dn_dt_bias[0:1, :].partition_broadcast(128)),
             writes=["dtb"], dma="cst")
        S.op("sp", lambda e: e.dma_start(out=col(C_NEA), in_=self.dn_a_log[0:1, :].partition_broadcast(128)),
             writes=["nea"], dma="cst")
        S.op("act", lambda e: e.activation(out=col(C_NEA), in_=col(C_NEA), func=AF.Exp), reads=["nea"], writes=["nea"])
        S.op("dve", lambda e: e.tensor_scalar(out=col(C_NEA), in0=col(C_NEA), scalar1=-1.0, scalar2=None, op0=ALU.mult),
             reads=["nea"], writes=["nea"])
        S.op("sp", lambda e: e.dma_start(out=cwj[0:4, :], in_=self.dn_conv_w[0]), writes=["cwj"], dma="cst")
        for cc in range(24):
            S.op("pe", lambda e, cc=cc: e.transpose(out=PS[:, 7, cc * 4:(cc + 1) * 4], in_=cwj[0:4, cc * 128:(cc + 1) * 128],
                                                    identity=self.identf[0:4, 0:4]),
                 reads=["cwj", "identf"], writes=[("ps", 7)])
        S.op("act", lambda e: e.activation(out=cw[:], in_=PS[:, 7, 0:96].rearrange("p (c j) -> p c j", c=24), func=AF.Copy),
             writes=[("ps", 7), "cw"])
        S.op("pool", lambda e: e.memset(Tri, 1.0), writes=["Tri"])
        S.op("pool", lambda e: e.affine_select(out=Tri, in_=Tri, pattern=[[1, 128]], compare_op=ALU.is_ge, fill=0.0,
                                               base=0, channel_multiplier=-1), reads=["Tri"], writes=["Tri"])
        S.op("pool", lambda e: e.memset(cst[0:64, 64:128], 0.0), reads=["Tri"], writes=["Tri"])
        S.op("pool", lambda e: e.memset(Bones, 0.0), writes=["Bones"])
        S.op("pool", lambda e: e.memset(cst[0:64, 256:320], 1.0), reads=["Bones"], writes=["Bones"])
        S.op("pool", lambda e: e.memset(cst[64:128, 320:384], 1.0), reads=["Bones"], writes=["Bones"])
        S.op("pool", lambda e: e.memset(Mc, 3.0e4), writes=["Mc"])
        S.op("pool", lambda e: e.affine_select(out=Mc, in_=Mc, pattern=[[1, 128]], compare_op=ALU.is_ge, fill=0.0,
                                               base=-1, channel_multiplier=-1), reads=["Mc"], writes=["Mc"])
        S.op("pool", lambda e: e.memset(cst[64:128, 128:192], 3.0e4), reads=["Mc"], writes=["Mc"])
        S.op("pool", lambda e: e.memset(Bsel0, 0.0), writes=["Bsel"])
        S.op("pool", lambda e: e.memset(cst[0:64, 384:512], 1.0), reads=["Bsel"], writes=["Bsel"])
        S.op("pool", lambda e: e.memset(Bsel1, 0.0), reads=["Bsel"], writes=["Bsel"])
        S.op("pool", lambda e: e.memset(dsc[64:128, 128:256], 1.0), reads=["Bsel"], writes=["Bsel"])
        S.op("pool", lambda e: e.memset(carry[:], 0.0), writes=[("carry", cc_) for cc_ in range(24)])
        S.op("pool", lambda e: e.memset(St2, 0.0), writes=[("St", 0), ("St", 1)])
        S.op("pool", lambda e: e.memset(Sb2, 0.0), writes=[("Sb", 0), ("Sb", 1)])
        S.barrier()
        QSCALE = 128.0 ** -0.5

        def pb_of(h, base):
            return PS[:, base + h // 4, (h % 4) * 128:(h % 4) * 128 + 128]

        for row0 in range(row_lo, row_hi, TD):
            full = row0 >= full_from
            need_q = full or (row0 + TD) >= full_from
            S.barrier()
            self.prenorm_tile(srcap, row0, "explog", nsub=2)
            hkeys = [("hT", 0), ("hT", 1)]
            for cc in range(24):
                if cc < 8 and not need_q:
                    continue
                par = cc % 4
                pb = (2, 3, 6, 7)[par]

                def mm_qkv(e, cc=cc, pb=pb):
                    ins = None
                    for c in range(8):
                        ins = e.matmul(PS[:, pb, 0:TD], lhsT=Win[:, c, cc * 128:(cc + 1) * 128], rhs=hT[:, c, 0:TD],
                                       start=(c == 0), stop=(c == 7))
                    return ins
                S.op("pe", mm_qkv, reads=hkeys + [("Wdi", c) for c in range(8)], writes=[("ps", pb)])
                pcb, cvb = pc[par], cv[par]
                S.op("pool", lambda e, cc=cc, pcb=pcb: e.tensor_copy(out=pcb[:, 0:3], in_=carry[:, cc, :]),
                     reads=[("carry", cc)], writes=[("pc", par, 0)])
                S.op("act", lambda e, pb=pb, pcb=pcb: e.activation(out=pcb[:, 3:3 + TD], in_=PS[:, pb, 0:TD], func=AF.Copy),
                     writes=[("ps", pb), ("pc", par, 1)])
                S.op("pool", lambda e, cc=cc, pcb=pcb: e.tensor_copy(out=carry[:, cc, :], in_=pcb[:, TD:TD + 3]),
                     reads=[("pc", par, 1), ("pc", par, 0)], writes=[("carry", cc)])
                pk = [("pc", par, 0), ("pc", par, 1), "cw"]
                S.op("dve", lambda e, cc=cc, pcb=pcb, cvb=cvb: e.tensor_scalar(
                    out=cvb, in0=pcb[:, 0:TD], scalar1=cw[:, cc, 0:1], scalar2=None, op0=ALU.mult),
                    reads=pk, writes=[("cv", par)])
                for j in range(1, 4):
                    S.op("dve", lambda e, cc=cc, pcb=pcb, cvb=cvb, j=j: e.scalar_tensor_tensor(
                        out=cvb, in0=pcb[:, j:j + TD], scalar=cw[:, cc, j:j + 1], in1=cvb, op0=ALU.mult, op1=ALU.add),
                        reads=pk + [("cv", par)], writes=[("cv", par)])
                dstT = qT if cc < 8 else (kT if cc < 16 else vT)
                nm = "qT" if cc < 8 else ("kT" if cc < 16 else "vT")
                S.op("act", lambda e, cvb=cvb, dstT=dstT, cc=cc: e.activation(out=dstT[:, cc % 8, :], in_=cvb, func=AF.Silu),
                     reads=[("cv", par)], writes=[(nm, cc % 8)])
            if full:
                for blk in range(2):
                    for hb in range(2):
                        def mm_z(e, hb=hb, blk=blk):
                            ins = None
                            for c in range(8):
                                ins = e.matmul(PS[:, 4 + hb, :], lhsT=hT[:, c, blk * 128:(blk + 1) * 128],
                                               rhs=Win[:, c, 3072 + hb * 512:3072 + (hb + 1) * 512], start=(c == 0), stop=(c == 7))
                            return ins
                        S.op("pe", mm_z, reads=[("hT", blk)] + [("Wdi", c) for c in range(8)], writes=[("ps", 4 + hb)])
                        zsl = zs2[blk][:, hb * 512:(hb + 1) * 512]
                        S.op("act", lambda e, hb=hb, zsl=zsl: e.activation(out=zsl, in_=PS[:, 4 + hb, :], func=AF.Silu),
                             writes=[("ps", 4 + hb), ("zs", blk, hb)])
                        S.op("pool", lambda e, hb=hb, zsl=zsl: e.tensor_tensor(out=zsl, in0=zsl, in1=dng[:, hb * 512:(hb + 1) * 512],
                                                                               op=ALU.mult),
                             reads=[("zs", blk, hb)] + [("dng", h) for h in range(8)], writes=[("zs", blk, hb)])
            ccs = [cc for cc in range(16) if cc >= 8 or need_q]
            for cc in ccs:
                dstT = qT if cc < 8 else kT
                nm = "qT" if cc < 8 else "kT"
                S.op("act", lambda e, dstT=dstT, cc=cc: e.activation(out=sqall[:, cc, :], in_=dstT[:, cc % 8, :], func=AF.Square),
                     reads=[(nm, cc % 8)], writes=[("sqall", cc)])
            for cc in ccs:
                b = cc // 2
                S.op("pe", lambda e, cc=cc, b=b: e.matmul(PS[:, b, (cc % 2) * TD:(cc % 2) * TD + TD], lhsT=onesb[:], rhs=sqall[:, cc, :],
                                                          start=True, stop=True),
                     reads=[("sqall", cc), "onesb"], writes=[("ps", b)])
            for b in sorted(set(cc // 2 for cc in ccs)):
                S.op("dve", lambda e, b=b: e.tensor_scalar(out=rinvall2[:, b * 512:(b + 1) * 512], in0=PS[:, b, :], scalar1=1e-6,
                                                           scalar2=None, op0=ALU.add),
                     writes=[("ps", b), ("rinvall", b)])
            lo, hi = min(ccs) * TD, 16 * TD
            S.op("act", lambda e, lo=lo, hi=hi: e.activation(out=rinvall2[:, lo:hi], in_=rinvall2[:, lo:hi], func=AF.Ln),
                 reads=[("rinvall", b) for b in range(8)], writes=[("rinvall", b) for b in range(8)])
            S.op("act", lambda e, lo=lo, hi=hi: e.activation(out=rinvall2[:, lo:hi], in_=rinvall2[:, lo:hi], func=AF.Exp, scale=-0.5),
                 reads=[("rinvall", b) for b in range(8)], writes=[("rinvall", b) for b in range(8)])
            for cc in ccs:
                dstT = qT if cc < 8 else kT
                nm = "qT" if cc < 8 else "kT"
                sc = QSCALE if cc < 8 else 1.0
                S.op("dve", lambda e, dstT=dstT, cc=cc, sc=sc: e.scalar_tensor_tensor(
                    out=dstT[:, cc % 8, :], in0=dstT[:, cc % 8, :], scalar=sc, in1=rinvall[:, cc, :], op0=ALU.mult, op1=ALU.mult),
                    reads=[(nm, cc % 8), ("rinvall", cc // 2)], writes=[(nm, cc % 8)])
            S.barrier()
            if True:
                def pro(tb, so, hk):
                    def mm_ba(e):
                        ins = None
                        for c in range(8):
                            ins = e.matmul(PS[:, 6, 0:16], lhsT=hT[:, c, tb], rhs=Win[:, c, 4096:4112], start=(c == 0), stop=(c == 7))
                        return ins
                    S.op("pe", mm_ba, reads=hk + [("Wdi", c) for c in range(8)], writes=[("ps", 6)])
                    S.op("act", lambda e: e.activation(out=col(C_T1, so=so), in_=PS[:, 6, 0:8], func=AF.Exp, scale=-1.0),
                         writes=[("ps", 6), ("t1", so)])
                    S.op("dve", lambda e: e.tensor_tensor(out=col(C_XA, so=so), in0=PS[:, 6, 8:16], in1=col(C_DTB), op=ALU.add),
                         reads=["dtb"], writes=[("ps", 6), ("xa", so)])
                    S.op("dve", lambda e: e.tensor_scalar(out=col(C_T1, so=so), in0=col(C_T1, so=so), scalar1=1.0, scalar2=None, op0=ALU.add),
                         reads=[("t1", so)], writes=[("t1", so)])
                    S.op("dve", lambda e: e.reciprocal(out=col(C_BETA, so=so), in_=col(C_T1, so=so)), reads=[("t1", so)], writes=[("beta", so)])
                    S.op("dve", lambda e: e.tensor_scalar(out=col(C_NB, so=so), in0=col(C_BETA, so=so), scalar1=-1.0, scalar2=None, op0=ALU.mult),
                         reads=[("beta", so)], writes=[("nb", so)])
                    yield
                    S.op("act", lambda e: e.activation(out=col(C_XA, so=so), in_=col(C_XA, so=so), func=AF.Exp), reads=[("xa", so)], writes=[("xa", so)])
                    S.op("dve", lambda e: e.tensor_scalar(out=col(C_XA, so=so), in0=col(C_XA, so=so), scalar1=1.0, scalar2=None, op0=ALU.add),
                         reads=[("xa", so)], writes=[("xa", so)])
                    S.op("act", lambda e: e.activation(out=col(C_SP, so=so), in_=col(C_XA, so=so), func=AF.Ln), reads=[("xa", so)], writes=[("sp", so)])
                    yield
                    S.op("dve", lambda e: e.tensor_tensor(out=col(C_G, so=so), in0=col(C_SP, so=so), in1=col(C_NEA), op=ALU.mult),
                         reads=[("sp", so), "nea"], writes=[("g", so)])

                    def mm_gc(e):
                        e.matmul(PS[:, 6, 16:24], lhsT=Tri, rhs=col(C_G, so=so), start=True, stop=True)
                        e.matmul(PS[:, 6, 24:32], lhsT=Bones, rhs=col(C_G, so=so), start=True, stop=True)
                        e.matmul(PS[:, 6, 32:40], lhsT=Bsel0, rhs=col(C_G, so=so), start=True, stop=True)
                        return e.matmul(PS[:, 6, 40:48], lhsT=Bsel1, rhs=col(C_G, so=so), start=True, stop=True)
                    S.op("pe", mm_gc, reads=[("g", so), "Tri", "Bones", "Bsel"], writes=[("ps", 6)])
                    S.op("act", lambda e: e.activation(out=col(C_GC, so=so), in_=PS[:, 6, 16:24], func=AF.Copy), writes=[("ps", 6), ("gc", so)])
                    S.op("act", lambda e: e.activation(out=col(C_EGC, so=so), in_=PS[:, 6, 16:24], func=AF.Exp), writes=[("ps", 6), ("egc", so)])
                    S.op("dve", lambda e: e.tensor_tensor(out=col(C_EDK, so=so), in0=PS[:, 6, 24:32], in1=col(C_GC, so=so), op=ALU.subtract),
                         reads=[("gc", so)], writes=[("ps", 6), ("edk", so)])
                    S.op("act", lambda e: e.activation(out=col(C_EDK, so=so), in_=col(C_EDK, so=so), func=AF.Exp), reads=[("edk", so)], writes=[("edk", so)])
                    S.op("act", lambda e: e.activation(out=col(C_GL, 16, so=so), in_=PS[:, 6, 32:48], func=AF.Exp), writes=[("ps", 6), ("gl", so)])
                    S.op("dve", lambda e: e.tensor_tensor(out=col(C_BEGC, so=so), in0=col(C_BETA, so=so), in1=col(C_EGC, so=so), op=ALU.mult),
                         reads=[("beta", so), ("egc", so)], writes=[("begc", so)])

                    yield

                def grp(hg, tb, full, so):
                    H = range(hg * 4, hg * 4 + 4)
                    cs_ = slice(hg * 512, (hg + 1) * 512)
                    b0, b1, b2, bt = hg, 2 + hg, 4 + hg, 6 + hg
                    pT = PS[:, bt, :].bitcast(BF16)[:, 0:512]
                    KH = lambda nm: [(nm, h) for h in H]
                    for h in H:
                        S.op("pool", lambda e, h=h: e.tensor_scalar(out=ubuf[:, h, :], in0=Tri, scalar1=dsc[:, so + C_G + h:so + C_G + h + 1],
                                                                    scalar2=None, op0=ALU.mult),
                             reads=[("g", so), "Tri"], writes=[("ubuf", h)])
                    S.op("pe", lambda e: e.matmul(PS[:, b1, :], lhsT=onesf[:], rhs=ubuf2[:, cs_], start=True, stop=True),
                         reads=KH("ubuf") + ["onesf"], writes=[("ps", b1)])
                    yield
                    if full:
                        S.op("act", lambda e: e.activation(out=dec2[:, cs_], in_=PS[:, b1, :], func=AF.Exp),
                             writes=[("ps", b1), ("dec", hg)])
                        S.op("dve", lambda e: e.tensor_tensor(out=qdT[:, hg * 4:hg * 4 + 4, :], in0=qT[:, hg * 4:hg * 4 + 4, tb],
                                                              in1=dec[:, hg * 4:hg * 4 + 4, :], op=ALU.mult),
                             reads=KH("qT") + [("dec", hg)], writes=[("qdT", hg)])
                        yield
                    for h in H:
                        S.op("dve", lambda e, h=h: e.scalar_tensor_tensor(
                            out=dec[:, h, :], in0=pb_of(h, 2), scalar=dsc[:, so + C_GC + h:so + C_GC + h + 1], in1=Mc,
                            op0=ALU.subtract, op1=ALU.max),
                            reads=[("gc", so), "Mc"], writes=[("ps", b1), ("dec", hg)])
                    S.op("act", lambda e: e.activation(out=dec2[:, cs_], in_=dec2[:, cs_], func=AF.Exp, scale=-1.0),
                         reads=[("dec", hg)], writes=[("dec", hg)])
                    yield
                    def side():
                        def tr_k(e):
                            ins = None
                            for i, h in enumerate(H):
                                ins = e.transpose(out=pT[:, i * 128:(i + 1) * 128], in_=kT[:, h, tb], identity=ident[:])
                            return ins
                        S.op("pe", tr_k, reads=KH("kT") + ["ident"], writes=[("ps", bt)])
                        for i, h in enumerate(H):
                            S.op("act", lambda e, h=h, i=i: e.activation(out=kbg[:, h, :], in_=pT[:, i * 128:(i + 1) * 128], func=AF.Copy,
                                                                         scale=dsc[:, so + C_BEGC + h:so + C_BEGC + h + 1]),
                                 reads=[("begc", so)], writes=[("ps", bt), ("kbg", h)])
                            S.op("act", lambda e, h=h, i=i: e.activation(out=kd[:, h, :], in_=pT[:, i * 128:(i + 1) * 128], func=AF.Copy,
                                                                         scale=dsc[:, so + C_EDK + h:so + C_EDK + h + 1]),
                                 reads=[("edk", so)], writes=[("ps", bt), ("kd", h)])
                        yield

                        def tr_v(e):
                            ins = None
                            for i, h in enumerate(H):
                                ins = e.transpose(out=pT[:, i * 128:(i + 1) * 128], in_=vT[:, h, tb], identity=ident[:])
                            return ins
                        S.op("pe", tr_v, reads=KH("vT") + ["ident"], writes=[("ps", bt)])
                        for i, h in enumerate(H):
                            S.op("act", lambda e, h=h, i=i: e.activation(out=vb[:, h, :], in_=pT[:, i * 128:(i + 1) * 128], func=AF.Copy,
                                                                         scale=dsc[:, so + C_BETA + h:so + C_BETA + h + 1]),
                                 reads=[("beta", so)], writes=[("ps", bt), ("vb", h)])
                        yield
                        if full:
                            def mm_qk(e):
                                ins = None
                                for h in H:
                                    ins = e.matmul(pb_of(h, 0), lhsT=qT[:, h, tb], rhs=kT[:, h, tb], start=True, stop=True)
                                return ins
                            S.op("pe", mm_qk, reads=KH("kT") + KH("qT"), writes=[("ps", b0)])
                            S.op("dve", lambda e: e.tensor_tensor(out=attn2[:, cs_], in0=PS[:, b0, :], in1=dec2[:, cs_], op=ALU.mult),
                                 reads=[("dec", hg)], writes=[("ps", b0), ("attn", hg)])
                            yield

                            def tr_at(e):
                                ins = None
                                for i, h in enumerate(H):
                                    ins = e.transpose(out=pT[:, i * 128:(i + 1) * 128], in_=attn[:, h, :], identity=ident[:])
                                return ins
                            S.op("pe", tr_at, reads=[("attn", hg), "ident"], writes=[("ps", bt)])
                            S.op("act", lambda e: e.activation(out=attnT2[:, cs_], in_=pT, func=AF.Copy), writes=[("ps", bt), ("attnT", hg)])
                            yield

                        yield
                    sd = side()
                    def mm_kk(e):
                        ins = None
                        for h in H:
                            ins = e.matmul(pb_of(h, 4), lhsT=kT[:, h, tb], rhs=kT[:, h, tb], start=True, stop=True)
                        return ins
                    S.op("pe", mm_kk, reads=KH("kT"), writes=[("ps", b2)])
                    for h in H:
                        S.op("dve", lambda e, h=h: e.scalar_tensor_tensor(
                            out=A[0][:, h, :], in0=pb_of(h, 4), scalar=dsc[:, so + C_NB + h:so + C_NB + h + 1], in1=dec[:, h, :],
                            op0=ALU.mult, op1=ALU.mult),
                            reads=[("nb", so), ("dec", hg)], writes=[("ps", b2), ("A", 0, hg)])
                    S.op("pool", lambda e: e.affine_select(out=A[0][:, hg * 4:hg * 4 + 4, :], in_=A[0][:, hg * 4:hg * 4 + 4, :],
                                                           pattern=[[0, 4], [-1, 128]], compare_op=ALU.not_equal, fill=0.0,
                                                           base=0, channel_multiplier=1),
                         reads=[("A", 0, hg)], writes=[("A", 0, hg)])
                    yield
                    def tr_a0(e):
                        ins = None
                        for i, h in enumerate(H):
                            ins = e.transpose(out=pT[:, i * 128:(i + 1) * 128], in_=A[0][:, h, :], identity=ident[:])
                        return ins
                    S.op("pe", tr_a0, reads=[("A", 0, hg), "ident"], writes=[("ps", bt)])
                    S.op("act", lambda e: e.activation(out=B2[0][:, cs_], in_=pT, func=AF.Copy), writes=[("ps", bt), ("B", 0, hg)])
                    yield
                    for h in H:
                        S.op("pool", lambda e, h=h: e.tensor_tensor(out=TT[:, h, :], in0=B[0][:, h, :], in1=ident[:], op=ALU.add),
                             reads=[("B", 0, hg), "ident"], writes=[("TT", h)])
                    for k in range(5):
                        cur, nxt = k % 2, 1 - k % 2

                        def mm_a(e, cur=cur):
                            ins = None
                            for h in H:
                                ins = e.matmul(pb_of(h, 0), lhsT=B[cur][:, h, :], rhs=A[cur][:, h, :], start=True, stop=True)
                            return ins
                        S.op("pe", mm_a, reads=[("A", cur, hg), ("B", cur, hg)], writes=[("ps", b0)])
                        S.op("act", lambda e, nxt=nxt: e.activation(out=A2[nxt][:, cs_], in_=PS[:, b0, :], func=AF.Copy),
                             writes=[("ps", b0), ("A", nxt, hg)])
                        if k <= 3:
                            def mm_b(e, cur=cur):
                                ins = None
                                for h in H:
                                    ins = e.matmul(pb_of(h, 2), lhsT=A[cur][:, h, :], rhs=B[cur][:, h, :], start=True, stop=True)
                                return ins
                            S.op("pe", mm_b, reads=[("A", cur, hg), ("B", cur, hg)], writes=[("ps", b1)])
                            S.op("dve", lambda e, nxt=nxt: e.tensor_copy(out=B2[nxt][:, cs_], in_=PS[:, b1, :]),
                                 writes=[("ps", b1), ("B", nxt, hg)])
                        yield
                        next(sd, None)

                        def mm_t(e, nxt=nxt):
                            ins = None
                            for h in H:
                                ins = e.matmul(pb_of(h, 4), lhsT=A[nxt][:, h, :], rhs=TT[:, h, :], start=True, stop=True)
                            return ins
                        S.op("pe", mm_t, reads=[("A", nxt, hg)] + KH("TT"), writes=[("ps", b2)])
                        S.op("dve", lambda e: e.tensor_tensor(out=TT2[:, cs_], in0=TT2[:, cs_], in1=PS[:, b2, :], op=ALU.add),
                             reads=KH("TT"), writes=[("ps", b2)] + KH("TT"))
                        yield
                    for _ in sd:
                        pass
                    def mm_u(e):
                        ins = None
                        for h in H:
                            ins = e.matmul(pb_of(h, 0), lhsT=TT[:, h, :], rhs=vb[:, h, :], start=True, stop=True)
                        return ins
                    S.op("pe", mm_u, reads=KH("TT") + KH("vb"), writes=[("ps", b0)])
                    S.op("act", lambda e: e.activation(out=ubuf2[:, cs_], in_=PS[:, b0, :], func=AF.Copy),
                         writes=[("ps", b0)] + KH("ubuf"))

                    def mm_w(e):
                        ins = None
                        for h in H:
                            ins = e.matmul(pb_of(h, 2), lhsT=kbg[:, h, :], rhs=TT[:, h, :], start=True, stop=True)
                        return ins
                    S.op("pe", mm_w, reads=KH("TT") + KH("kbg"), writes=[("ps", b1)])
                    S.op("dve", lambda e: e.tensor_copy(out=wT2[:, cs_], in_=PS[:, b1, :]), writes=[("ps", b1), ("wT", hg)])
                    yield
                    for c in range(2):
                        rs = slice(c * 64, (c + 1) * 64)

                        def mm_ws(e):
                            ins = None
                            for h in H:
                                ins = e.matmul(pb_of(h, 4), lhsT=wT[:, h, :], rhs=Sb[:, h, :], start=True, stop=True)
                            return ins
                        S.op("pe", mm_ws, reads=[("wT", hg), ("Sb", hg)], writes=[("ps", b2)])
                        S.op("dve", lambda e, rs=rs: e.tensor_tensor(out=vnew2[rs, cs_], in0=ubuf2[rs, cs_], in1=PS[rs, b2, :],
                                                                     op=ALU.subtract),
                             reads=KH("ubuf"), writes=[("ps", b2), ("vnew", hg)])
                        yield
                        if full:
                            def mm_o(e, rs=rs):
                                ins = None
                                for h in H:
                                    e.matmul(pb_of(h, 0), lhsT=qdT[:, h, :], rhs=Sb[:, h, :], start=True, stop=False)
                                    ins = e.matmul(pb_of(h, 0), lhsT=attnT[rs, h, :], rhs=vnew[rs, h, :], start=False, stop=True)
                                return ins
                            S.op("pe", mm_o, reads=[("qdT", hg), ("Sb", hg), ("attnT", hg), ("vnew", hg)], writes=[("ps", b0)])
                            S.op("act", lambda e, rs=rs: e.activation(out=osb2[rs, cs_], in_=PS[rs, b0, :], func=AF.Copy),
                                 writes=[("ps", b0), ("osb", c, hg), ("dec", hg)])

                        def mm_s(e, rs=rs):
                            ins = None
                            for h in H:
                                ins = e.matmul(pb_of(h, 2), lhsT=kd[rs, h, :], rhs=vnew[rs, h, :], start=True, stop=True)
                            return ins
                        S.op("pe", mm_s, reads=KH("kd") + [("vnew", hg)], writes=[("ps", b1)])
                        for h in H:
                            S.op("dve", lambda e, h=h, c=c: e.scalar_tensor_tensor(
                                out=St[:, h, :], in0=St[:, h, :], scalar=dsc[:, so + C_GL + c * 8 + h:so + C_GL + c * 8 + h + 1], in1=pb_of(h, 2),
                                op0=ALU.mult, op1=ALU.add),
                                reads=[("gl", so), ("St", hg)], writes=[("ps", b1), ("St", hg)])
                        S.op("act", lambda e: e.activation(out=Sb2[:, cs_], in_=St2[:, cs_], func=AF.Copy),
                             reads=[("St", hg)], writes=[("Sb", hg)])
                        yield


                def epi(blk, brow, so, zs):
                    S.op("act", lambda e: e.activation(out=self.junk[:], in_=osb2, func=AF.Square),
                         reads=[("osb", c_, hb_) for c_ in range(2) for hb_ in range(2)] + [("dec", 0), ("dec", 1)], writes=["junk"])
                    S.op("dve", lambda e: e.tensor_reduce(out=col(C_OSS, so=so), in_=self.junk[:].rearrange("p (h d) -> p h d", h=8),
                                                          axis=AX.X, op=ALU.add), reads=["junk"], writes=[("oss", so)])
                    yield
                    self.rstd(col(C_OSS, so=so), dsc[:, so + 120:so + 128], col(C_OSS, so=so), 1.0 / 128, RMS_EPS, "explog", [("oss", so)], ("orstd", so))
                    for h in range(8):
                        S.op("dve", lambda e, h=h, zs=zs, so=so: e.scalar_tensor_tensor(
                            out=gatedb[:, h * 128:(h + 1) * 128], in0=osb[:, h, :], scalar=dsc[:, so + C_OSS + h:so + C_OSS + h + 1],
                            in1=zs[:, h * 128:(h + 1) * 128], op0=ALU.mult, op1=ALU.mult),
                            reads=[("orstd", so), ("zs", blk, h // 4), ("dec", h // 4)] + [("osb", c_, h // 4) for c_ in range(2)],
                            writes=[("gatedb", h), ("vnew", h // 4)])
                    yield
                    pT4 = PS[:, 4, :].bitcast(BF16)

                    def tr_g(e):
                        ins = None
                        for c_ in range(8):
                            ins = e.transpose(out=pT4[:, c_ * 128:(c_ + 1) * 128], in_=gatedb[:, c_ * 128:(c_ + 1) * 128], identity=ident[:])
                        return ins
                    S.op("pe", tr_g, reads=[("gatedb", h) for h in range(8)] + ["ident", ("vnew", 0), ("vnew", 1)], writes=[("ps", 4)])
                    S.op("act", lambda e: e.activation(out=gatedT[:], in_=pT4.rearrange("p (c t) -> p c t", c=8), func=AF.Copy),
                         writes=[("ps", 4), "gatedT"])
                    yield
                    yb = (6, 7)
                    for hh in range(2):
                        def mm_out(e, hh=hh, b=yb[hh]):
                            ins = None
                            for c_ in range(8):
                                ins = e.matmul(PS[:, b, :], lhsT=gatedT[:, c_, :], rhs=Wout[:, c_, hh * 512:(hh + 1) * 512],
                                               start=(c_ == 0), stop=(c_ == 7))
                            return ins
                        S.op("pe", mm_out, reads=["gatedT"] + [("Wdo", c_) for c_ in range(8)], writes=[("ps", yb[hh])])
                    self.post_residual(srcap, dstap, brow, yb, 1.0, "explog", dst_off)
                    yield


            tbs = [slice(0, 128), slice(128, 256)]
            sos = [0, 256]

            def drain(g):
                if g is not None:
                    for _ in g:
                        pass
            for _ in pro(tbs[0], sos[0], [("hT", 0)]):
                pass
            streams0 = [grp(0, tbs[0], full, sos[0]), grp(1, tbs[0], full, sos[0]), pro(tbs[1], sos[1], [("hT", 1)])]
            for _ in itertools.zip_longest(*streams0):
                pass
            streams1 = [grp(0, tbs[1], full, sos[1]), grp(1, tbs[1], full, sos[1])]
            if full:
                e0 = epi(0, row0, sos[0], zs2[0])
                next(e0)
                next(e0)
                streams1.append(e0)
            for _ in itertools.zip_longest(*streams1):
                pass
            if full:
                drain(epi(1, row0 + 128, sos[1], zs2[1]))

    def dram(self, name):
        return {"x": self.x_in, "xs": self.xs, "y": self.y_out}[name]


_PROG_CACHE = {}


def get_program(npre, nown, layers_key):
    key = (npre, nown, layers_key)
    if key not in _PROG_CACHE:
        _PROG_CACHE[key] = Builder(npre, nown, list(layers_key)).build()
    return _PROG_CACHE[key]


NPRE = 4096
NOWN = 4096


def full_layers(npre, nown):
    nt = npre + nown
    return (
        ("ffn", (0, 0, "x", "xs", 0, nt, 0)),
        ("dn", (0, "xs", "xs", 0, nt, npre, 0)),
        ("ffn", (0, 1, "xs", "xs", npre, nt, 0)),
        ("ffn", (1, 0, "xs", "xs", npre, nt, 0)),
        ("sg", (1, "xs", "xs", npre, nt, 0)),
        ("ffn", (1, 1, "xs", "y", npre, nt, -npre)),
    )


def kernel(**inputs):
    x = np.ascontiguousarray(np.asarray(inputs["x"], dtype=np.float32))
    B, SEQ, _ = x.shape
    half = SEQ // 2
    nc = get_program(half, half, full_layers(half, half))
    wnames = ["norm_g", "ffn_w_gate", "ffn_w_up", "ffn_w_down", "dn_w_in", "dn_conv_w", "dn_a_log", "dn_dt_bias",
              "dn_norm_g", "dn_w_out", "sg_w_in", "sg_b_in", "sg_ln_g", "sg_ln_b", "sg_w_s", "sg_b_s", "sg_w_out"]
    shared = {k: np.ascontiguousarray(np.asarray(inputs[k], dtype=np.float32)) for k in wnames}
    in_maps = []
    for c in range(2 * B):
        b, hf = c // 2, c % 2
        own = x[b, hf * half:(hf + 1) * half]
        pre = x[b, 0:half] if hf == 1 else np.zeros_like(own)
        m = dict(shared)
        m["x"] = np.ascontiguousarray(np.concatenate([pre, own], axis=0))
        in_maps.append(m)
    res = run_bass_kernel_spmd(nc, in_maps, core_ids=list(range(2 * B)))
    out = np.empty_like(x)
    for c in range(2 * B):
        b, hf = c // 2, c % 2
        out[b, hf * half:(hf + 1) * half] = res.results[c]["y"]
    return out
```
